# Optimizing a Trainium2 kernel written in Bass

```python
import math
import jax, jax.numpy as jnp
from jax import lax
import numpy as np

D_MODEL = 2048
BATCH = 8
SEQ = 2048
DEPTH = 1

N_META = 16
ATTN_HEADS = 8
HEAD_DIM = 64
V_HEAD_DIM = 2 * HEAD_DIM
QK_WIDTH = ATTN_HEADS * 2 * HEAD_DIM
ATTN_WIDTH = ATTN_HEADS * V_HEAD_DIM
ROT_DIM = HEAD_DIM // 4
ROPE_THETA = 500000.0
CONV_WIDTH = D_MODEL // 2
CONV_KERNEL = 31
D_FF = 4 * D_MODEL
Q_BLOCK = 128
EPS = 1e-6
SPLITS = (QK_WIDTH, QK_WIDTH, ATTN_WIDTH, CONV_WIDTH, CONV_WIDTH, D_MODEL, D_MODEL)
IN_COLS = sum(SPLITS)

kernel_name = "hybrid_diffattn_conformer_gated_block"


def lambda_init_fn(layer_idx):
    return 0.8 - 0.6 * math.exp(-0.3 * layer_idx)


def rms_norm(x, g):
    xf = x.astype(jnp.float32)
    y = xf * lax.rsqrt(jnp.mean(xf * xf, axis=-1, keepdims=True) + EPS)
    return (y * g.astype(jnp.float32)).astype(x.dtype)


def layer_norm(x, g, b):
    xf = x.astype(jnp.float32)
    mu = jnp.mean(xf, axis=-1, keepdims=True)
    var = jnp.mean(jnp.square(xf - mu), axis=-1, keepdims=True)
    y = (xf - mu) * lax.rsqrt(var + EPS)
    return (y * g.astype(jnp.float32) + b.astype(jnp.float32)).astype(x.dtype)


def rope_tables(length):
    pos = jnp.arange(length, dtype=jnp.float32)
    inv_freq = ROPE_THETA ** (-jnp.arange(0, ROT_DIM, 2, dtype=jnp.float32) / ROT_DIM)
    ang = pos[:, None] * inv_freq[None, :]
    cos = jnp.concatenate([jnp.cos(ang), jnp.cos(ang)], axis=-1)
    sin = jnp.concatenate([jnp.sin(ang), jnp.sin(ang)], axis=-1)
    return cos, sin


def rope_partial(t, cos, sin):
    cos = cos.astype(t.dtype)
    sin = sin.astype(t.dtype)
    rot, rest = t[..., :ROT_DIM], t[..., ROT_DIM:]
    r1, r2 = rot[..., :ROT_DIM // 2], rot[..., ROT_DIM // 2:]
    rotated = jnp.concatenate([-r2, r1], axis=-1)
    return jnp.concatenate([rot * cos + rotated * sin, rest], axis=-1)


def diff_attention_causal(q, k, v, lam):
    lp = q.shape[3]
    scale = HEAD_DIM ** -0.5
    outs = []
    for i in range(lp // Q_BLOCK):
        s0, end = i * Q_BLOCK, (i + 1) * Q_BLOCK
        qb = q[:, :, :, s0:end]
        kb = k[:, :, :, :end]
        vb = v[:, :, :end]
        s = jnp.einsum('bhcqd,bhckd->bhcqk', qb, kb).astype(jnp.float32) * scale
        qpos = jnp.arange(s0, end)[:, None]
        kpos = jnp.arange(end)[None, :]
        s = jnp.where(kpos <= qpos, s, -jnp.inf)
        p = jax.nn.softmax(s, axis=-1)
        a = p[:, :, 0] - lam * p[:, :, 1]
        outs.append(jnp.einsum('bhqk,bhkv->bhqv', a.astype(v.dtype), vb))
    return jnp.concatenate(outs, axis=2)


def setup_inputs(seed: int = 0) -> dict:
    key = jax.random.key(seed)
    ks = jax.random.split(key, 24)
    f32 = jnp.float32
    nrm = lambda k, shape, s: jax.random.normal(k, shape, f32) * s
    return {
        "x": nrm(ks[0], (BATCH, SEQ, D_MODEL), 1.0),
        "meta": nrm(ks[1], (N_META, D_MODEL), 1.0),
        "norm1_g": 1.0 + nrm(ks[2], (DEPTH, D_MODEL), 0.02),
        "w_in": nrm(ks[3], (DEPTH, D_MODEL, IN_COLS), D_MODEL ** -0.5),
        "q_norm_g": 1.0 + nrm(ks[4], (DEPTH, HEAD_DIM), 0.02),
        "k_norm_g": 1.0 + nrm(ks[5], (DEPTH, HEAD_DIM), 0.02),
        "lambda_q1": nrm(ks[6], (DEPTH, HEAD_DIM), 0.1),
        "lambda_k1": nrm(ks[7], (DEPTH, HEAD_DIM), 0.1),
        "lambda_q2": nrm(ks[8], (DEPTH, HEAD_DIM), 0.1),
        "lambda_k2": nrm(ks[9], (DEPTH, HEAD_DIM), 0.1),
        "subln_g": 1.0 + nrm(ks[10], (DEPTH, V_HEAD_DIM), 0.02),
        "w_attn_o": nrm(ks[11], (DEPTH, ATTN_WIDTH, D_MODEL), ATTN_WIDTH ** -0.5),
        "dw_kernel": nrm(ks[12], (DEPTH, CONV_KERNEL, CONV_WIDTH), CONV_KERNEL ** -0.5),
        "dw_bias": nrm(ks[13], (DEPTH, CONV_WIDTH), 0.02),
        "conv_ln_g": 1.0 + nrm(ks[14], (DEPTH, CONV_WIDTH), 0.02),
        "conv_ln_b": nrm(ks[15], (DEPTH, CONV_WIDTH), 0.02),
        "w_conv_o": nrm(ks[16], (DEPTH, CONV_WIDTH, D_MODEL), CONV_WIDTH ** -0.5),
        "w_out": nrm(ks[17], (DEPTH, D_MODEL, D_MODEL), D_MODEL ** -0.5),
        "norm2_g": 1.0 + nrm(ks[18], (DEPTH, D_MODEL), 0.02),
        "w_up": nrm(ks[19], (DEPTH, D_MODEL, D_FF), D_MODEL ** -0.5),
        "w_down": nrm(ks[20], (DEPTH, D_FF, D_MODEL), D_FF ** -0.5),
    }


def reference(x, meta, norm1_g, w_in, q_norm_g, k_norm_g, lambda_q1, lambda_k1,
              lambda_q2, lambda_k2, subln_g, w_attn_o, dw_kernel, dw_bias,
              conv_ln_g, conv_ln_b, w_conv_o, w_out, norm2_g, w_up, w_down):
    b, s, d = x.shape
    h = jnp.concatenate([jnp.broadcast_to(meta[None].astype(x.dtype), (b, N_META, d)), x], axis=1)
    L = s + N_META
    Lp = ((L + Q_BLOCK - 1) // Q_BLOCK) * Q_BLOCK
    cos, sin = rope_tables(L)
    split_idx = [int(v) for v in np.cumsum(SPLITS)[:-1]]

    for l in range(DEPTH):
        lam_init = lambda_init_fn(l)
        u = rms_norm(h, norm1_g[l])
        proj = jnp.einsum('bld,dc->blc', u, w_in[l])
        q_all, k_all, v_all, ga, gb, g_attn, g_conv = jnp.split(proj, split_idx, axis=-1)

        q = q_all.reshape(b, L, ATTN_HEADS, 2, HEAD_DIM).transpose(0, 2, 3, 1, 4)
        k = k_all.reshape(b, L, ATTN_HEADS, 2, HEAD_DIM).transpose(0, 2, 3, 1, 4)
        v = v_all.reshape(b, L, ATTN_HEADS, V_HEAD_DIM).transpose(0, 2, 1, 3)
        q = rope_partial(rms_norm(q, q_norm_g[l]), cos, sin)
        k = rope_partial(rms_norm(k, k_norm_g[l]), cos, sin)
        pad = Lp - L
        q = jnp.pad(q, ((0, 0), (0, 0), (0, 0), (0, pad), (0, 0)))
        k = jnp.pad(k, ((0, 0), (0, 0), (0, 0), (0, pad), (0, 0)))
        v = jnp.pad(v, ((0, 0), (0, 0), (0, pad), (0, 0)))
        lam = (jnp.exp(jnp.sum(lambda_q1[l].astype(jnp.float32) * lambda_k1[l].astype(jnp.float32)))
               - jnp.exp(jnp.sum(lambda_q2[l].astype(jnp.float32) * lambda_k2[l].astype(jnp.float32)))
               + lam_init)
        o = diff_attention_causal(q, k, v, lam)[:, :, :L]
        o = rms_norm(o, subln_g[l]) * (1.0 - lam_init)
        o = o.transpose(0, 2, 1, 3).reshape(b, L, ATTN_WIDTH)
        y_attn = jnp.einsum('blc,cd->bld', o, w_attn_o[l])

        a = ga * jax.nn.sigmoid(gb)
        c = lax.conv_general_dilated(
            a, dw_kernel[l][:, None, :].astype(a.dtype), window_strides=(1,),
            padding=[(CONV_KERNEL - 1, 0)], dimension_numbers=('NWC', 'WIO', 'NWC'),
            feature_group_count=CONV_WIDTH) + dw_bias[l]
        c = jax.nn.silu(layer_norm(c, conv_ln_g[l], conv_ln_b[l]))
        y_conv = jnp.einsum('blc,cd->bld', c, w_conv_o[l])

        m = jax.nn.sigmoid(g_attn) * y_attn + jax.nn.sigmoid(g_conv) * y_conv
        h = h + jnp.einsum('bld,de->ble', m, w_out[l])

        u2 = rms_norm(h, norm2_g[l])
        z = jnp.square(jax.nn.relu(jnp.einsum('bld,df->blf', u2, w_up[l])))
        h = h + jnp.einsum('blf,fd->bld', z, w_down[l])

    return h[:, N_META:]
```

```python
import math
import numpy as np
import ml_dtypes
import concourse.bass as bass
import concourse.mybir as mybir
from concourse.bass_utils import run_bass_kernel_spmd
from concourse.alu_op_type import AluOpType as ALU

F32 = mybir.dt.float32
BF16 = mybir.dt.bfloat16
AF = mybir.ActivationFunctionType

D = 2048
S = 2048
NMETA = 16
L = S + NMETA
NT = S // 128
H = 8
DFF = 4 * D
CW = 1024
KTAPS = 31
EPS = 1e-6
LAM_INIT = 0.8 - 0.6 * math.exp(0.0)
NEG = -30000.0
DBG_HEADS = None
DBG_GROUPS = None


class Sched:
    def __init__(self, nc, esems, dsems):
        self.nc = nc
        self.esem = esems
        self.dsem = dsems
        self.cnt = {e: 0 for e in esems}
        self.ops = {e: [] for e in esems}
        self.seen = {e: {} for e in esems}
        self.lastw = {}
        self.readers = {}
        self.dcnt = {q: [0] * len(dsems[q]) for q in dsems}
        self.dnext = {q: 0 for q in dsems}
        self.out_handles = []

    def _sem(self, k):
        return self.esem[k] if isinstance(k, str) else self.dsem[k[1]][k[2]]

    def _collect(self, eng, reads, writes, extra):
        deps = list(extra)
        for r in reads:
            h = self.lastw.get(r)
            if h is not None:
                deps.append(h)
        for w in writes:
            h = self.lastw.get(w)
            if h is not None:
                deps.append(h)
            deps.extend(self.readers.get(w, {}).items())
        waits = {}
        for (k, n) in deps:
            if k == eng and eng == 'pe':
                continue
            if self.seen[eng].get(k, 0) >= n:
                continue
            if waits.get(k, 0) < n:
                waits[k] = n
        for k, n in waits.items():
            self.seen[eng][k] = n
        return list(waits.items())

    def _record(self, h, reads, writes):
        for r in reads:
            d = self.readers.setdefault(r, {})
            if d.get(h[0], 0) < h[1]:
                d[h[0]] = h[1]
        for w in writes:
            self.lastw[w] = h
            self.readers[w] = {}

    def op(self, eng, fn, reads=(), writes=(), signal=True):
        waits = self._collect(eng, reads, writes, ())
        if eng != 'pe':
            signal = True
        h = (eng, self.cnt[eng] + 1)
        if signal:
            self.cnt[eng] += 1
        self.ops[eng].append((waits, fn, self.esem[eng] if signal else None, 1))
        self._record(h, reads, writes)
        return h

    def dma(self, q, fn, reads=(), writes=(), is_out=False):
        i = self.dnext[q]
        self.dnext[q] = (i + 1) % len(self.dsem[q])
        key = ('d', q, i)
        extra = []
        if self.dcnt[q][i] > 0:
            extra.append((key, self.dcnt[q][i]))
        waits = self._collect(q, reads, writes, extra)
        self.dcnt[q][i] += 16
        h = (key, self.dcnt[q][i])
        self.ops[q].append((waits, fn, self.dsem[q][i], 16))
        self._record(h, reads, writes)
        if is_out:
            self.out_handles.append(h)
        return h

    def finish(self, eng='sp'):
        waits = {}
        for (k, n) in self.out_handles:
            if waits.get(k, 0) < n:
                waits[k] = n
        self.ops[eng].append((list(waits.items()), None, None, 0))

    def replay(self, eng, e):
        for (waits, fn, sem, inc) in self.ops[eng]:
            for (k, n) in waits:
                e.wait_ge(self._sem(k), n)
            if fn is None:
                continue
            ins = fn(e)
            if sem is not None:
                ins.then_inc(sem, inc)


def build_program(dbg=None, stop_after=None):
    nc = bass.Bass("TRN2", target_bir_lowering=False)
    dt = nc.dram_tensor
    x = dt("x", [S, D], F32, kind="ExternalInput").ap()
    meta = dt("meta", [NMETA, D], F32, kind="ExternalInput").ap()
    w_in = dt("w_in", [D, 9216], F32, kind="ExternalInput").ap()
    w_attn_o = dt("w_attn_o", [1024, D], F32, kind="ExternalInput").ap()
    w_conv_o = dt("w_conv_o", [1024, D], F32, kind="ExternalInput").ap()
    w_out = dt("w_out", [D, D], F32, kind="ExternalInput").ap()
    w_up = dt("w_up", [D, DFF], F32, kind="ExternalInput").ap()
    w_down = dt("w_down", [DFF, D], F32, kind="ExternalInput").ap()
    g1bc_d = dt("g1bc", [128, D], F32, kind="ExternalInput").ap()
    g2bc_d = dt("g2bc", [128, D], F32, kind="ExternalInput").ap()
    qkg_d = dt("qkg", [128, 256], F32, kind="ExternalInput").ap()
    sublng_d = dt("sublng", [128, 128], F32, kind="ExternalInput").ap()
    lamv_d = dt("lamv", [128, 256], F32, kind="ExternalInput").ap()
    cs_d = dt("cstab", [128, 17 * 32], F32, kind="ExternalInput").ap()
    dwk_d = dt("dwk", [128, 8 * KTAPS], F32, kind="ExternalInput").ap()
    cvec_d = dt("cvec", [128, 24], F32, kind="ExternalInput").ap()
    ident_d = dt("ident", [128, 128], BF16, kind="ExternalInput").ap()
    maskneg_d = dt("maskneg", [128, 128], BF16, kind="ExternalInput").ap()
    y = dt("y", [S, D], F32, kind="ExternalOutput").ap()
    h1s = dt("h1s", [S, D], F32, kind="ExternalOutput").ap()
    dbg_out = None
    if dbg is not None:
        dbg_out = dt("dbg", list(dbg[1]), dbg[2], kind="ExternalOutput").ap()

    import contextlib
    es = contextlib.ExitStack()
    with es:
        def sb(name, shape, dtype):
            return es.enter_context(nc.sbuf_tensor("sb_" + name, shape, dtype))

        def sem(name):
            return es.enter_context(nc.semaphore(name))

        uT = sb("uT", [128, 16, L], BF16)
        AR = sb("AR", [128, 32768], BF16)
        T0 = sb("T0", [128, 8192], BF16)
        WS = sb("WS", [128, 10240], BF16)
        WA = [sb("WA%d" % i, [128, 2064], F32) for i in range(2)]
        WB = [sb("WB%d" % i, [128, 2048], BF16) for i in range(2)]
        ident = sb("ident", [128, 128], BF16)
        maskneg = sb("maskneg", [128, 128], BF16)
        ones_bf = sb("ones_bf", [128, 128], BF16)
        gbc = sb("gbc", [128, D], F32)
        qkg = sb("qkg", [128, 256], F32)
        sublng = sb("sublng", [128, 128], F32)
        lamv = sb("lamv", [128, 256], F32)
        cstab = sb("cstab", [128, 17, 32], F32)
        dwk = sb("dwk", [128, 8, KTAPS], F32)
        cvec = sb("cvec", [128, 24], F32)
        small = sb("small", [128, 64], F32)
        mhalf = sb("mhalf", [128, 512], BF16)
        ssb = sb("ssb", [128, 48], F32)

        oT = AR[:, 0:16384].rearrange("p (c t) -> p c t", c=8)
        cT = AR[:, 16384:32768].rearrange("p (c t) -> p c t", c=8)
        ZT = AR[:, :].rearrange("p (f t) -> p f t", f=64)

        psum = [es.enter_context(nc.psum_tensor("ps%d" % i, [128, 1024], F32)) for i in range(4)]

        def bank(i):
            return psum[i // 2][:, (i % 2) * 512:(i % 2) * 512 + 512]

        def bank_bf(i):
            return psum[i // 2][:].bitcast(BF16)[:, (i % 2) * 1024:(i % 2) * 1024 + 1024]

        def PB(i):
            return ('ps', i)

        esems = {e: sem("s_" + e) for e in ['pe', 'act', 'dve', 'pool', 'sp']}
        dsems = {'sp': [sem("d_sp%d" % i) for i in range(12)], 'pool': [sem("d_pl%d" % i) for i in range(12)]}
        sc = Sched(nc, esems, dsems)
        dummy = sb("dummy", [128, 8], F32)

        def barrier(names):
            sc.op('pool', lambda e: e.memset(dummy[:, 0:2], 0.0), writes=list(names))
        w_in_v = w_in.rearrange("(c p) n -> p c n", p=128)
        w_ao_v = w_attn_o.rearrange("(c p) n -> p c n", p=128)
        w_co_v = w_conv_o.rearrange("(c p) n -> p c n", p=128)
        w_out_v = w_out.rearrange("(c p) n -> p c n", p=128)
        w_up_v = w_up.rearrange("(c p) n -> p c n", p=128)
        w_dn_v = w_down.rearrange("(c p) n -> p c n", p=128)

        WAr = [[('WA', i, k) for k in range(4)] for i in range(2)]
        WBr = [[('WB', i)] for i in range(2)]

        def ld(dst, src, res):
            sc.dma('sp', lambda e, d=dst, s=src: e.dma_start(out=d, in_=s), writes=[res])
        ld(ident[:], ident_d, 'ident')
        ld(maskneg[:], maskneg_d, 'maskneg')
        ld(gbc[:], g1bc_d, 'gbc')
        ld(qkg[:], qkg_d, 'qkg')
        ld(sublng[:], sublng_d, 'sublng')
        ld(lamv[:], lamv_d, 'lamv')
        ld(cstab[:].rearrange("p a b -> p (a b)"), cs_d, 'cstab')
        ld(dwk[:].rearrange("p a b -> p (a b)"), dwk_d, 'dwk')
        ld(cvec[:], cvec_d, 'cvec')
        sc.op('pool', lambda e: e.memset(mhalf[:], -0.5), writes=['mhalf'])
        sc.op('pool', lambda e: e.memset(ones_bf[:], 1.0), writes=['ones'])
        sc.op('dve', lambda e: e.tensor_scalar(out=dwk[:], in0=dwk[:], scalar1=0.5, scalar2=None, op0=ALU.mult),
              reads=['dwk'], writes=['dwk'])
        sc.op('dve', lambda e: e.tensor_scalar(out=cvec[:, 8:24], in0=cvec[:, 8:24], scalar1=0.5, scalar2=None,
                                               op0=ALU.mult), reads=['cvec'], writes=['cvec'])
        sc.op('dve', lambda e: e.tensor_scalar(out=sublng[:], in0=sublng[:], scalar1=1.0 - LAM_INIT, scalar2=None,
                                               op0=ALU.mult), reads=['sublng'], writes=['sublng'])
        sc.op('dve', lambda e: e.scalar_tensor_tensor(out=WB[0][:, 0:64], in0=lamv[:, 0:64], scalar=1.0, in1=lamv[:, 64:128],
                                                      op0=ALU.mult, op1=ALU.mult, accum_out=small[:, 0:1]),
              reads=['lamv'], writes=WBr[0] + ['sm0'])
        sc.op('dve', lambda e: e.scalar_tensor_tensor(out=WB[0][:, 0:64], in0=lamv[:, 128:192], scalar=1.0, in1=lamv[:, 192:256],
                                                      op0=ALU.mult, op1=ALU.mult, accum_out=small[:, 1:2]),
              reads=['lamv'], writes=WBr[0] + ['sm1'])
        sc.op('act', lambda e: e.activation(out=small[:, 2:4], in_=small[:, 0:2], func=AF.Exp),
              reads=['sm0', 'sm1'], writes=['sm23'])
        sc.op('dve', lambda e: e.tensor_tensor(out=small[:, 5:6], in0=small[:, 3:4], in1=small[:, 2:3],
                                               op=ALU.subtract), reads=['sm23'], writes=['sm5'])
        sc.op('dve', lambda e: e.tensor_scalar(out=small[:, 4:5], in0=small[:, 5:6], scalar1=-LAM_INIT,
                                               scalar2=None, op0=ALU.add), reads=['sm5'], writes=['neglam'])
        neglam = small[:, 4:5]

        def rstd_pool(io_ap, n, inv_n, res):
            rows = io_ap.shape[0]
            sc.op('pool', lambda e: e.tensor_scalar(out=io_ap, in0=io_ap, scalar1=inv_n, scalar2=EPS,
                                                    op0=ALU.mult, op1=ALU.add), reads=[res], writes=[res])
            sc.op('pool', lambda e: e.tensor_tensor(out=io_ap, in0=io_ap, in1=mhalf[:rows, 0:n], op=ALU.pow),
                  reads=[res, 'mhalf'], writes=[res])

        def norm_to_featmajor(xs, xs_res, rows, slot, colofs, dst_res, ssidx, have_ss=False):
            ssr = ('ss', ssidx)
            ss_ap = ssb[:rows, ssidx:ssidx + 1]
            if not have_ss:
                sc.op('act', lambda e: e.activation(out=WB[slot][:rows, :], in_=xs, func=AF.Square, accum_out=ss_ap),
                      reads=xs_res, writes=WBr[slot] + [ssr])
            rstd_pool(ss_ap, 1, 1.0 / D, ssr)
            xn = WB[slot]
            sc.op('dve', lambda e: e.scalar_tensor_tensor(out=xn[:rows, :], in0=xs, scalar=ss_ap,
                                                          in1=gbc[:rows, :], op0=ALU.mult, op1=ALU.mult),
                  reads=xs_res + [ssr, 'gbc'], writes=WBr[slot])
            pi = 3 - slot
            pt = psum[pi][:].bitcast(BF16)
            pres = [PB(2 * pi), PB(2 * pi + 1)]
            for c in range(16):
                sc.op('pe', lambda e, c=c: e.transpose(out=pt[:, c * 128:c * 128 + rows],
                                                       in_=xn[:rows, c * 128:(c + 1) * 128],
                                                       identity=ident[:rows, :rows]),
                      reads=WBr[slot] + ['ident'], writes=pres, signal=(c == 15))
            sc.op('act', lambda e: e.activation(out=uT[:, :, colofs:colofs + rows],
                                                in_=pt.rearrange("p (c t) -> p c t", c=16)[:, :, 0:rows],
                                                func=AF.Copy),
                  reads=pres, writes=[dst_res])

        def ublk(b):
            return [('uT', 4 * b + k) for k in range(4)]

        for j in range(NT + 1):
            slot = j % 2
            rows = 128 if j < NT else NMETA
            src = x[j * 128:(j + 1) * 128, :] if j < NT else meta
            colofs = NMETA + j * 128 if j < NT else 0
            sc.dma('sp', lambda e, s=src, sl=slot, r=rows: e.dma_start(out=WA[sl][:r, 0:D], in_=s),
                   writes=WAr[slot])
            norm_to_featmajor(WA[slot][:rows, 0:D], WAr[slot], rows, slot, colofs, ('uT', j), j)

        if dbg is not None and dbg[0] == 'uT':
            sc.dma('sp', lambda e: e.dma_start(out=dbg_out, in_=uT[:].rearrange("p a b -> p (a b)")),
                   reads=[('uT', j) for j in range(NT + 1)], is_out=True)
        if stop_after is None or stop_after > 1:
            sc.dma('sp', lambda e: e.dma_start(out=gbc[:], in_=g2bc_d), writes=['gbc'])

        TB = [(0, NMETA)] + [(NMETA + 512 * b, 512) for b in range(4)]
        TBres = [[('uT', 16)]] + [ublk(b) for b in range(4)]

        if stop_after is None or stop_after >= 2:
            AR32 = AR[:, 0:16384].bitcast(F32)
            aTp = [AR32[:, 0:2094], AR32[:, 2096:2096 + 2094]]
            aTr = ['aT0', 'aT1']
            for k in range(2):
                sc.op('pool', lambda e, k=k: e.memset(aTp[k][:, 0:30], 0.0), writes=[aTr[k]])
            for cc in range(8):
                slot = cc % 2
                wg = T0[:, slot * 4096:(slot + 1) * 4096].rearrange("p (c w) -> p c w", c=16)
                wres = ('T0', slot)
                sc.dma('pool', lambda e, wg=wg, cc=cc: e.dma_start(out=wg[:, :, 0:128],
                                                                 in_=w_in_v[:, :, 3072 + cc * 128:3072 + (cc + 1) * 128]),
                       writes=[wres])
                sc.dma('pool', lambda e, wg=wg, cc=cc: e.dma_start(out=wg[:, :, 128:256],
                                                                 in_=w_in_v[:, :, 4096 + cc * 128:4096 + (cc + 1) * 128]),
                       writes=[wres])
                for bi, (c0, n) in enumerate(TB):
                    ba, bb = 2 * (bi % 2), 2 * (bi % 2) + 1
                    for (bk, wo) in ((ba, 0), (bb, 128)):
                        for c in range(16):
                            sc.op('pe', lambda e, c=c, bk=bk, wo=wo, wg=wg, c0=c0, n=n: e.matmul(
                                bank(bk)[:, 0:n], wg[:, c, wo:wo + 128], uT[:, c, c0:c0 + n],
                                start=(c == 0), stop=(c == 15)),
                                reads=[wres] + TBres[bi], writes=[PB(bk)], signal=(c == 15))
                    th = WB[bi % 2][:].bitcast(F32)[:, 0:512]
                    sc.op('act', lambda e, th=th, bb=bb, n=n: e.activation(out=th[:, 0:n], in_=bank(bb)[:, 0:n],
                                                                        func=AF.Tanh, scale=0.5),
                          reads=[PB(bb)], writes=WBr[bi % 2])
                    sc.op('dve', lambda e, th=th, ba=ba, n=n, c0=c0, slot=slot: e.scalar_tensor_tensor(
                        out=aTp[slot][:, 30 + c0:30 + c0 + n], in0=th[:, 0:n], scalar=1.0, in1=bank(ba)[:, 0:n],
                        op0=ALU.add, op1=ALU.mult),
                        reads=WBr[bi % 2] + [PB(ba)], writes=[aTr[slot]])
                accs = [WA[0][:, 0:S], WA[1][:, 0:S]]
                for j in range(KTAPS):
                    a = j % 2
                    src_ap = aTp[slot][:, NMETA + j:NMETA + j + S]
                    if j == 0:
                        sc.op('dve', lambda e, src_ap=src_ap, cc=cc: e.tensor_scalar(
                            out=accs[0], in0=src_ap, scalar1=dwk[:, cc, 0:1], scalar2=cvec[:, cc:cc + 1],
                            op0=ALU.mult, op1=ALU.add),
                            reads=[aTr[slot], 'dwk', 'cvec'], writes=WAr[0])
                    elif j == 1:
                        sc.op('dve', lambda e, src_ap=src_ap, cc=cc: e.tensor_scalar(
                            out=accs[1], in0=src_ap, scalar1=dwk[:, cc, 1:2], scalar2=None, op0=ALU.mult),
                            reads=[aTr[slot], 'dwk'], writes=WAr[1])
                    else:
                        sc.op('dve', lambda e, src_ap=src_ap, cc=cc, j=j, a=a: e.scalar_tensor_tensor(
                            out=accs[a], in0=src_ap, scalar=dwk[:, cc, j:j + 1], in1=accs[a],
                            op0=ALU.mult, op1=ALU.add),
                            reads=[aTr[slot], 'dwk'] + WAr[a], writes=WAr[a])
                sc.op('dve', lambda e, cc=cc: e.tensor_tensor(out=cT[:, cc, :], in0=accs[0], in1=accs[1], op=ALU.add),
                      reads=WAr[0] + WAr[1], writes=[('cT', cc)])
            sqb = T0[:, 0:4096].rearrange("p (c t) -> p c t", c=8)
            for b in range(4):
                blk = slice(512 * b, 512 * b + 512)
                sc.op('dve', lambda e, blk=blk: e.tensor_tensor(out=sqb, in0=cT[:, :, blk], in1=cT[:, :, blk],
                                                             op=ALU.mult),
                      reads=[('cT', c) for c in range(8)], writes=[('T0', 0)])
                for (bk, srcs) in ((4, 'c'), (5, 's')):
                    for c in range(8):
                        rhs = cT[:, c, blk] if srcs == 'c' else sqb[:, c, :]
                        sc.op('pe', lambda e, bk=bk, rhs=rhs, c=c: e.matmul(bank(bk), ones_bf[:, :], rhs,
                                                                         start=(c == 0), stop=(c == 7)),
                              reads=['ones', ('cT', c), ('T0', 0)], writes=[PB(bk)], signal=(c == 7))
                mean = WA[0][:, 0:512]
                msq = WA[0][:, 512:1024]
                rstd = WA[0][:, 1024:1536]
                sc.op('dve', lambda e: e.tensor_scalar(out=mean, in0=bank(4), scalar1=1.0 / CW, scalar2=None,
                                                       op0=ALU.mult), reads=[PB(4)], writes=[WAr[0][0]])
                sc.op('dve', lambda e: e.tensor_tensor(out=msq, in0=mean, in1=mean, op=ALU.mult),
                      reads=[WAr[0][0]], writes=[WAr[0][1]])
                sc.op('dve', lambda e: e.scalar_tensor_tensor(out=rstd, in0=bank(5), scalar=1.0 / CW, in1=msq,
                                                              op0=ALU.mult, op1=ALU.subtract),
                      reads=[PB(5), WAr[0][1]], writes=[WAr[0][2]])
                sc.op('pool', lambda e: e.tensor_scalar(out=rstd, in0=rstd, scalar1=EPS, scalar2=None, op0=ALU.add),
                      reads=[WAr[0][2]], writes=[WAr[0][2]])
                sc.op('pool', lambda e: e.tensor_tensor(out=rstd, in0=rstd, in1=mhalf[:, 0:512], op=ALU.pow),
                      reads=[WAr[0][2], 'mhalf'], writes=[WAr[0][2]])
                for cc in range(8):
                    k = cc % 2
                    t = WA[1][:, 1024 * k:1024 * k + 512]
                    th = WA[1][:, 1024 * k + 512:1024 * k + 1024]
                    tr = [WAr[1][2 * k]]
                    thr = [WAr[1][2 * k + 1]]
                    sc.op('dve', lambda e, t=t, cc=cc, blk=blk: e.tensor_tensor(out=t, in0=cT[:, cc, blk], in1=mean,
                                                                             op=ALU.subtract),
                          reads=[('cT', cc), WAr[0][0]], writes=tr)
                    sc.op('dve', lambda e, t=t: e.tensor_tensor(out=t, in0=t, in1=rstd, op=ALU.mult),
                          reads=tr + [WAr[0][2]], writes=tr)
                    sc.op('dve', lambda e, t=t, cc=cc: e.tensor_scalar(out=t, in0=t, scalar1=cvec[:, 8 + cc:9 + cc],
                                                                    scalar2=cvec[:, 16 + cc:17 + cc],
                                                                    op0=ALU.mult, op1=ALU.add),
                          reads=tr + ['cvec'], writes=tr)
                    sc.op('act', lambda e, t=t, th=th: e.activation(out=th, in_=t, func=AF.Tanh),
                          reads=tr, writes=thr)
                    sc.op('dve', lambda e, t=t, th=th, cc=cc, blk=blk: e.scalar_tensor_tensor(
                        out=cT[:, cc, blk], in0=th, scalar=1.0, in1=t, op0=ALU.add, op1=ALU.mult),
                        reads=tr + thr, writes=[('cT', cc)])
            if dbg is not None and dbg[0] == 'cT':
                sc.dma('sp', lambda e: e.dma_start(out=dbg_out, in_=AR[:, 16384:32768]),
                       reads=[('cT', c) for c in range(8)], is_out=True)

        if stop_after is None or stop_after >= 3:
            qTs = [T0[:, 0:2048], WS[:, 8192:10240]]
            kT = T0[:, 2048:2048 + L]
            VA0 = 2048 + L
            vaug = T0[:, VA0:VA0 + 17 * 130].rearrange("p (j v) -> p j v", j=17)
            PT0 = VA0 + 17 * 130
            PTs = [T0[:, PT0 + 512 * i:PT0 + 512 * (i + 1)] for i in range(3)]
            assert PT0 + 1536 <= 8192
            sc.op('pool', lambda e: e.memset(T0[:, VA0:VA0 + 17 * 130], 1.0),
                  writes=['vaug', 'qT', 'kT', ('T0', 0), ('T0', 1), 'aT0', 'aT1', ('WS', 2)] + [('PT', i) for i in range(3)]
                  + [('oT', c) for c in range(8)])
            sc.op('pool', lambda e: e.memset(qTs[0][64:128, :], 0.0), writes=['qT'])
            sc.op('pool', lambda e: e.memset(qTs[1][0:64, :], 0.0), writes=['qT'])
            for h in range(H if DBG_HEADS is None else DBG_HEADS):
                slot = h % 2
                wqk = WS[:, slot * 4096:(slot + 1) * 4096].rearrange("p (c w) -> p c w", c=16)
                wv = WB[slot][:, :].rearrange("p (c w) -> p c w", c=16)
                wqk_r = ('WS', slot)
                sc.dma('pool', lambda e, wqk=wqk, h=h: e.dma_start(out=wqk[:, :, 0:128],
                                                                 in_=w_in_v[:, :, h * 128:(h + 1) * 128]), writes=[wqk_r])
                sc.dma('pool', lambda e, wqk=wqk, h=h: e.dma_start(out=wqk[:, :, 128:256],
                                                                 in_=w_in_v[:, :, 1024 + h * 128:1024 + (h + 1) * 128]),
                       writes=[wqk_r])
                sc.dma('pool', lambda e, wv=wv, h=h: e.dma_start(out=wv, in_=w_in_v[:, :, 2048 + h * 128:2048 + (h + 1) * 128]),
                       writes=WBr[slot])
                for j in range(NT + 1):
                    rows = 128 if j < NT else NMETA
                    colofs = NMETA + j * 128 if j < NT else 0
                    pj = j % 2
                    pr = bank(pj)
                    for c in range(16):
                        sc.op('pe', lambda e, c=c, pr=pr, rows=rows, colofs=colofs, wqk=wqk: e.matmul(
                            pr[:rows, 0:256], uT[:, c, colofs:colofs + rows], wqk[:, c, :], start=(c == 0), stop=(c == 15)),
                            reads=[('uT', j), wqk_r], writes=[PB(pj)], signal=(c == 15))
                    for c in range(16):
                        sc.op('pe', lambda e, c=c, pr=pr, rows=rows, colofs=colofs, wv=wv: e.matmul(
                            pr[:rows, 256:384], uT[:, c, colofs:colofs + rows], wv[:, c, :], start=(c == 0), stop=(c == 15)),
                            reads=[('uT', j)] + WBr[slot], writes=[PB(pj)], signal=(c == 15))
                    wa = WA[pj]
                    sq = wa[:rows, 0:256]
                    qn = wa[:rows, 256:512]
                    ra = wa[:rows, 512:576].rearrange("p (g d) -> p g d", g=4)
                    rb = wa[:rows, 576:640].rearrange("p (g d) -> p g d", g=4)
                    s4 = ssb[:rows, 24 + 4 * pj:28 + 4 * pj]
                    s4r = ('s4', pj)
                    war = [WAr[pj][0], WAr[pj][1]]
                    sc.op('act', lambda e, sq=sq, pr=pr, rows=rows: e.activation(out=sq, in_=pr[:rows, 0:256], func=AF.Square),
                          reads=[PB(pj)], writes=war)
                    sc.op('dve', lambda e, sq=sq, s4=s4: e.tensor_reduce(out=s4, in_=sq.rearrange("p (g d) -> p g d", g=4),
                                                                      axis=mybir.AxisListType.X, op=ALU.add),
                          reads=war, writes=[s4r])
                    rstd_pool(s4, 4, 1.0 / 64, s4r)
                    sc.op('dve', lambda e, qn=qn, pr=pr, rows=rows, s4=s4: e.tensor_tensor(
                        out=qn.rearrange("p (g d) -> p g d", g=4),
                        in0=pr[:rows, 0:256].rearrange("p (g d) -> p g d", g=4),
                        in1=s4.unsqueeze(2).to_broadcast([rows, 4, 64]), op=ALU.mult),
                        reads=[PB(pj), s4r], writes=war)
                    sc.op('dve', lambda e, qn=qn, rows=rows: e.tensor_tensor(out=qn, in0=qn, in1=qkg[:rows, :], op=ALU.mult),
                          reads=war + ['qkg'], writes=war)
                    q3 = qn.rearrange("p (g d) -> p g d", g=4)
                    cs = cstab[:rows, j, :]
                    sc.op('dve', lambda e, q3=q3, ra=ra, cs=cs, rows=rows: e.tensor_tensor(
                        out=ra, in0=q3[:, :, 0:16], in1=cs[:, 0:16].unsqueeze(1).to_broadcast([rows, 4, 16]), op=ALU.mult),
                        reads=war + ['cstab'], writes=war)
                    sc.op('dve', lambda e, q3=q3, rb=rb, cs=cs, rows=rows: e.tensor_tensor(
                        out=rb[:, :, 0:8], in0=q3[:, :, 8:16], in1=cs[:, 16:24].unsqueeze(1).to_broadcast([rows, 4, 8]),
                        op=ALU.mult), reads=war + ['cstab'], writes=war)
                    sc.op('dve', lambda e, q3=q3, rb=rb, cs=cs, rows=rows: e.tensor_tensor(
                        out=rb[:, :, 8:16], in0=q3[:, :, 0:8], in1=cs[:, 24:32].unsqueeze(1).to_broadcast([rows, 4, 8]),
                        op=ALU.mult), reads=war + ['cstab'], writes=war)
                    sc.op('dve', lambda e, q3=q3, ra=ra, rb=rb: e.tensor_tensor(out=q3[:, :, 0:16], in0=ra, in1=rb, op=ALU.add),
                          reads=war, writes=war)
                    qkb = WA[pj][:, 1024:1152].bitcast(BF16)
                    qkr = [WAr[pj][2]]
                    sc.op('act', lambda e, qkb=qkb, qn=qn, rows=rows: e.activation(out=qkb[:rows, :], in_=qn, func=AF.Copy),
                          reads=war, writes=qkr)
                    sc.op('act', lambda e, pr=pr, rows=rows, j=j: e.activation(out=vaug[:rows, j, 0:128],
                                                                            in_=pr[:rows, 256:384], func=AF.Copy),
                          reads=[PB(pj)], writes=['vaug'])
                    tb = 2 + pj
                    ptb = bank_bf(tb)
                    sc.op('pe', lambda e, ptb=ptb, qkb=qkb, rows=rows: e.transpose(out=ptb[:, 0:rows], in_=qkb[:rows, 0:128],
                                                                                identity=ident[:rows, :rows]),
                          reads=qkr + ['ident'], writes=[PB(tb)], signal=False)
                    sc.op('pe', lambda e, ptb=ptb, qkb=qkb, rows=rows: e.transpose(out=ptb[:, 128:128 + rows],
                                                                                in_=qkb[:rows, 128:256],
                                                                                identity=ident[:rows, :rows]),
                          reads=qkr + ['ident'], writes=[PB(tb)], signal=True)
                    if j < NT:
                        for c in range(2):
                            sc.op('act', lambda e, ptb=ptb, j=j, c=c: e.activation(
                                out=qTs[c][64 * c:64 * c + 64, 128 * j:128 * j + 128], in_=ptb[64 * c:64 * c + 64, 0:128],
                                func=AF.Copy), reads=[PB(tb)], writes=['qT'])
                    sc.op('dve', lambda e, ptb=ptb, rows=rows, colofs=colofs: e.tensor_copy(
                        out=kT[:, colofs:colofs + rows], in_=ptb[:, 128:128 + rows]),
                        reads=[PB(tb)], writes=['kT'])
                step = 0
                for g2 in range(NT // 2 if DBG_GROUPS is None else DBG_GROUPS):
                    j0, j1 = 2 * g2, 2 * g2 + 1
                    kts = ['m'] + list(range(j1 + 1))
                    for kt in kts:
                        krows = NMETA if kt == 'm' else 128
                        kc0 = 0 if kt == 'm' else NMETA + 128 * kt
                        kidx = 16 if kt == 'm' else kt
                        both = (kt == 'm') or (kt <= j0)
                        qc0 = 128 * j0 if both else 128 * j1
                        N = 256 if both else 128
                        sbk = 4 + step % 3
                        sbank = bank(sbk)
                        diag = (kt != 'm') and (kt == j0 or kt == j1)
                        for c in range(2):
                            sc.op('pe', lambda e, c=c, sbank=sbank, krows=krows, kc0=kc0, qc0=qc0, N=N, diag=diag: e.matmul(
                                sbank[:krows, c * 256:c * 256 + N], kT[:, kc0:kc0 + krows],
                                qTs[c][:, qc0:qc0 + N], start=True, stop=(not diag)),
                                reads=['kT', 'qT'], writes=[PB(sbk)], signal=(c == 1 and not diag))
                            if diag:
                                sc.op('pe', lambda e, c=c, sbank=sbank: e.matmul(
                                    sbank[:, c * 256:c * 256 + 128], ident[:, :], maskneg[:, :], start=False, stop=True),
                                    reads=['ident', 'maskneg'], writes=[PB(sbk)], signal=(c == 1))
                        PT = PTs[step % 3]
                        ptr = ('PT', step % 3)
                        sc.op('act', lambda e, PT=PT, sbank=sbank, krows=krows, N=N: e.activation(
                            out=PT[:krows, :].rearrange("p (c n) -> p c n", c=2)[:, :, 0:N],
                            in_=sbank[:krows, :].rearrange("p (c n) -> p c n", c=2)[:, :, 0:N],
                            func=AF.Exp, scale=0.125),
                            reads=[PB(sbk)], writes=[ptr])
                        for c in range(2):
                            qts = (j0, j1) if both else (j1,)
                            for qt in qts:
                                ab = 2 * (qt - j0) + c
                                po = c * 256 + ((qt - j0) * 128 if both else 0)
                                sc.op('pe', lambda e, ab=ab, PT=PT, po=po, krows=krows, kidx=kidx, kt=kt, qt=qt: e.matmul(
                                    bank(ab)[:, 0:129], PT[:krows, po:po + 128], vaug[:krows, kidx, 0:129],
                                    start=(kt == 'm'), stop=(kt == qt)),
                                    reads=[ptr, 'vaug'], writes=[PB(ab)], signal=(kt == qt or (c == 1 and qt == qts[-1])))
                        step += 1
                    for qt in (j0, j1):
                        a0, a1 = 2 * (qt - j0), 2 * (qt - j0) + 1
                        k = qt % 2
                        sm = small[:, 16 + 8 * k:24 + 8 * k]
                        smr = ('smq', k)
                        o1 = WA[k][:, 1536:1664]
                        o2 = WA[k][:, 1664:1792]
                        onb = WA[k][:, 1792:1856].bitcast(BF16)
                        wr = [WAr[k][3]]
                        sc.op('dve', lambda e, sm=sm, a0=a0: e.reciprocal(out=sm[:, 0:1], in_=bank(a0)[:, 128:129]),
                              reads=[PB(a0)], writes=[smr])
                        sc.op('dve', lambda e, sm=sm, a1=a1: e.reciprocal(out=sm[:, 1:2], in_=bank(a1)[:, 128:129]),
                              reads=[PB(a1), smr], writes=[smr])
                        sc.op('dve', lambda e, sm=sm: e.tensor_tensor(out=sm[:, 2:3], in0=sm[:, 1:2], in1=neglam, op=ALU.mult),
                              reads=[smr, 'neglam'], writes=[smr])
                        sc.op('dve', lambda e, sm=sm, a0=a0, o1=o1: e.tensor_scalar(out=o1, in0=bank(a0)[:, 0:128],
                                                                               scalar1=sm[:, 0:1], scalar2=None, op0=ALU.mult),
                              reads=[PB(a0), smr], writes=wr)
                        sc.op('dve', lambda e, sm=sm, a1=a1, o1=o1: e.scalar_tensor_tensor(
                            out=o1, in0=bank(a1)[:, 0:128], scalar=sm[:, 2:3], in1=o1, op0=ALU.mult, op1=ALU.add),
                            reads=[PB(a1), smr] + wr, writes=wr)
                        sc.op('dve', lambda e, sm=sm, o1=o1, o2=o2: e.scalar_tensor_tensor(out=o2, in0=o1, scalar=1.0, in1=o1, op0=ALU.mult, op1=ALU.mult, accum_out=sm[:, 3:4]),
                            reads=wr, writes=wr + [('smr', k)])
                        rstd_pool(sm[:, 3:4], 1, 1.0 / 128, ('smr', k))
                        sc.op('dve', lambda e, sm=sm, o1=o1, onb=onb: e.scalar_tensor_tensor(
                            out=onb, in0=o1, scalar=sm[:, 3:4], in1=sublng[:, :], op0=ALU.mult, op1=ALU.mult),
                            reads=wr + [('smr', k), 'sublng'], writes=wr)
                        ptb = bank_bf(7)
                        sc.op('pe', lambda e, ptb=ptb, onb=onb: e.transpose(out=ptb[:, 0:128], in_=onb, identity=ident[:, :]),
                              reads=wr + ['ident'], writes=[PB(7)])
                        sc.op('act', lambda e, ptb=ptb, qt=qt, h=h: e.activation(out=oT[:, h, 128 * qt:128 * qt + 128],
                                                                              in_=ptb[:, 0:128], func=AF.Copy),
                              reads=[PB(7)], writes=[('oT', h)])
            if dbg is not None and dbg[0] == 'oT':
                sc.dma('sp', lambda e: e.dma_start(out=dbg_out, in_=AR[:, 0:16384]),
                       reads=[('oT', c) for c in range(8)], is_out=True)

        if stop_after is None or stop_after >= 4:
            mT = T0[:, :].rearrange("p (c t) -> p c t", c=16)
            barrier(['vaug', 'qT', 'kT', ('WS', 2)] + [('PT', i) for i in range(3)] + [('mT', fc) for fc in range(16)])
            it = 0
            for b in range(4):
                ucol = NMETA + 512 * b
                blk = slice(512 * b, 512 * b + 512)
                for fc in range(16):
                    slot = it % 2
                    wg = WS[:, slot * 4096:(slot + 1) * 4096].rearrange("p (c w) -> p c w", c=16)
                    wo = WS[:, 8192:10240].rearrange("p (c w) -> p c w", c=16)
                    wgr = ('WS', slot)
                    wor = ('WS', 2)
                    sc.dma('pool', lambda e, wg=wg, fc=fc: e.dma_start(
                        out=wg[:, :, 0:128], in_=w_in_v[:, :, 5120 + fc * 128:5120 + (fc + 1) * 128]), writes=[wgr])
                    sc.dma('pool', lambda e, wg=wg, fc=fc: e.dma_start(
                        out=wg[:, :, 128:256], in_=w_in_v[:, :, 7168 + fc * 128:7168 + (fc + 1) * 128]), writes=[wgr])
                    sc.dma('pool', lambda e, wo=wo, fc=fc: e.dma_start(
                        out=wo[:, 0:8, :], in_=w_ao_v[:, :, fc * 128:(fc + 1) * 128]), writes=[wor])
                    sc.dma('pool', lambda e, wo=wo, fc=fc: e.dma_start(
                        out=wo[:, 8:16, :], in_=w_co_v[:, :, fc * 128:(fc + 1) * 128]), writes=[wor])
                    bs = 4 * (it % 2)
                    for c in range(16):
                        sc.op('pe', lambda e, c=c, bs=bs, wg=wg, ucol=ucol: e.matmul(
                            bank(bs), wg[:, c, 0:128], uT[:, c, ucol:ucol + 512], start=(c == 0), stop=(c == 15)),
                            reads=[wgr] + ublk(b), writes=[PB(bs)], signal=(c == 15))
                    for c in range(16):
                        sc.op('pe', lambda e, c=c, bs=bs, wg=wg, ucol=ucol: e.matmul(
                            bank(bs + 1), wg[:, c, 128:256], uT[:, c, ucol:ucol + 512], start=(c == 0), stop=(c == 15)),
                            reads=[wgr] + ublk(b), writes=[PB(bs + 1)], signal=(c == 15))
                    for c in range(8):
                        sc.op('pe', lambda e, c=c, bs=bs, wo=wo, blk=blk: e.matmul(
                            bank(bs + 2), wo[:, c, :], oT[:, c, blk], start=(c == 0), stop=(c == 7)),
                            reads=[wor] + [('oT', c)], writes=[PB(bs + 2)], signal=(c == 7))
                    for c in range(8):
                        sc.op('pe', lambda e, c=c, bs=bs, wo=wo, blk=blk: e.matmul(
                            bank(bs + 3), wo[:, 8 + c, :], cT[:, c, blk], start=(c == 0), stop=(c == 7)),
                            reads=[wor] + [('cT', c)], writes=[PB(bs + 3)], signal=(c == 7))
                    k = it % 2
                    ta = WA[1][:, 1024 * k:1024 * k + 512]
                    tcn = WA[1][:, 1024 * k + 512:1024 * k + 1024]
                    tar = [WAr[1][2 * k]]
                    tcr = [WAr[1][2 * k + 1]]
                    sc.op('act', lambda e, ta=ta, bs=bs: e.activation(out=ta, in_=bank(bs), func=AF.Tanh, scale=0.5),
                          reads=[PB(bs)], writes=tar)
                    sc.op('act', lambda e, tcn=tcn, bs=bs: e.activation(out=tcn, in_=bank(bs + 1), func=AF.Tanh, scale=0.5),
                          reads=[PB(bs + 1)], writes=tcr)
                    sc.op('dve', lambda e, ta=ta, bs=bs: e.scalar_tensor_tensor(
                        out=ta, in0=ta, scalar=1.0, in1=bank(bs + 2), op0=ALU.add, op1=ALU.mult),
                        reads=tar + [PB(bs + 2)], writes=tar)
                    sc.op('dve', lambda e, tcn=tcn, bs=bs: e.scalar_tensor_tensor(
                        out=tcn, in0=tcn, scalar=1.0, in1=bank(bs + 3), op0=ALU.add, op1=ALU.mult),
                        reads=tcr + [PB(bs + 3)], writes=tcr)
                    sc.op('dve', lambda e, ta=ta, tcn=tcn, fc=fc: e.tensor_tensor(out=mT[:, fc, :], in0=ta, in1=tcn, op=ALU.add),
                          reads=tar + tcr, writes=[('mT', fc)])
                    it += 1
                wslots = [WS[:, 4096 + 1024 * i:4096 + 1024 * (i + 1)] for i in range(4)]
                wsr = [('WSq', i) for i in range(4)]
                barrier([('WS', 1)] + wsr)
                di = 0
                for cg in range(4):
                    bs = 4 * (cg % 2)
                    for fp in range(8):
                        sl = di % 4
                        wsl = wslots[sl].rearrange("p (k n) -> p k n", k=2)
                        wres = [wsr[sl]]
                        sc.dma('pool', lambda e, wsl=wsl, fp=fp, cg=cg: e.dma_start(
                            out=wsl, in_=w_out_v[:, 2 * fp:2 * fp + 2, cg * 512:(cg + 1) * 512]), writes=wres)
                        for k in range(2):
                            fc = 2 * fp + k
                            for t in range(4):
                                sc.op('pe', lambda e, t=t, bs=bs, fc=fc, wsl=wsl, k=k: e.matmul(
                                    bank(bs + t), mT[:, fc, t * 128:(t + 1) * 128], wsl[:, k, :],
                                    start=(fc == 0), stop=(fc == 15)),
                                    reads=[('mT', fc)] + wres, writes=[PB(bs + t)], signal=(fc == 15 or (k == 1 and t == 3)))
                        di += 1
                    for t in range(4):
                        tile_i = 4 * b + t
                        xp = WA[0][:, 512 * t:512 * t + 512]
                        xpr = [WAr[0][t]]
                        rs, re = tile_i * 128, tile_i * 128 + 128
                        sc.dma('sp', lambda e, xp=xp, rs=rs, re=re, cg=cg: e.dma_start(
                            out=xp, in_=x[rs:re, cg * 512:(cg + 1) * 512]), writes=xpr)
                        sc.op('dve', lambda e, xp=xp, bs=bs, t=t: e.scalar_tensor_tensor(
                            out=xp, in0=bank(bs + t), scalar=0.5, in1=xp, op0=ALU.mult, op1=ALU.add),
                            reads=[PB(bs + t)] + xpr, writes=xpr)
                        sc.op('act', lambda e, xp=xp, t=t, cg=cg: e.activation(
                            out=WB[0][:, 512 * (t % 2):512 * (t % 2) + 512], in_=xp, func=AF.Square,
                            accum_out=ssb[:, 4 * t + cg:4 * t + cg + 1]),
                            reads=xpr, writes=WBr[0] + [('ssq', t)])
                        sc.dma('sp', lambda e, xp=xp, rs=rs, re=re, cg=cg: e.dma_start(
                            out=h1s[rs:re, cg * 512:(cg + 1) * 512], in_=xp), reads=xpr, writes=[('h1s', tile_i)])
                barrier([('WS', 1)] + wsr)
                for t in range(4):
                    tile_i = 4 * b + t
                    rs, re = tile_i * 128, tile_i * 128 + 128
                    slot = t % 2
                    sc.op('dve', lambda e, t=t: e.tensor_reduce(out=ssb[:, 16 + t:17 + t], in_=ssb[:, 4 * t:4 * t + 4],
                                                              axis=mybir.AxisListType.X, op=ALU.add),
                          reads=[('ssq', t)], writes=[('ss', 16 + t)])
                    sc.dma('sp', lambda e, slot=slot, rs=rs, re=re: e.dma_start(out=WA[slot][:, 0:D], in_=h1s[rs:re, :]),
                           reads=[('h1s', tile_i)], writes=WAr[slot])
                    norm_to_featmajor(WA[slot][:, 0:D], WAr[slot], 128, 0, NMETA + 128 * tile_i, ('uT', tile_i), 16 + t,
                                      have_ss=True)
            if dbg is not None and dbg[0] == 'h1':
                sc.dma('sp', lambda e: e.dma_start(out=dbg_out, in_=h1s), reads=[('h1s', i) for i in range(16)], is_out=True)
            if dbg is not None and dbg[0] == 'u2T':
                sc.dma('sp', lambda e: e.dma_start(out=dbg_out, in_=uT[:].rearrange("p a b -> p (a b)")),
                       reads=[('uT', j) for j in range(NT + 1)], is_out=True)

        if stop_after is None or stop_after >= 5:
            ui = 0
            barrier([('mT', fc) for fc in range(16)] + [('T0d', 0), ('T0d', 1)] + WBr[1] + [('WB1q', q) for q in range(4)]
                    + [('WS', 1)] + [('WSq', i) for i in range(4)])
            for b in range(4):
                ucol = NMETA + 512 * b
                for fp in range(32):
                    slot = ui % 2
                    wu = WS[:, slot * 4096:(slot + 1) * 4096].rearrange("p (c w) -> p c w", c=16)
                    wur = ('WS', slot)
                    sc.dma('pool', lambda e, wu=wu, fp=fp: e.dma_start(out=wu, in_=w_up_v[:, :, fp * 256:(fp + 1) * 256]),
                           writes=[wur])
                    for k in range(2):
                        f = 2 * fp + k
                        ub = f % 4
                        for c in range(16):
                            sc.op('pe', lambda e, c=c, ub=ub, wu=wu, k=k, ucol=ucol: e.matmul(
                                bank(ub), wu[:, c, k * 128:(k + 1) * 128], uT[:, c, ucol:ucol + 512],
                                start=(c == 0), stop=(c == 15)),
                                reads=[wur] + ublk(b), writes=[PB(ub)], signal=(c == 15))
                        r = WB[1][:, 512 * (f % 4):512 * (f % 4) + 512]
                        rr = [('WB1q', f % 4)]
                        sc.op('act', lambda e, r=r, ub=ub: e.activation(out=r, in_=bank(ub), func=AF.Relu),
                              reads=[PB(ub)], writes=rr)
                        sc.op('dve', lambda e, r=r, f=f: e.tensor_tensor(out=ZT[:, f, :], in0=r, in1=r, op=ALU.mult),
                              reads=rr, writes=[('ZT', f)])
                    ui += 1
                di = 0
                for cg in range(4):
                    bs = 4 * (cg % 2)
                    for fq in range(8):
                        sl = di % 2
                        wd = T0[:, sl * 4096:(sl + 1) * 4096].rearrange("p (k n) -> p k n", k=8)
                        wdr = ('T0d', sl)
                        sc.dma('pool', lambda e, wd=wd, fq=fq, cg=cg: e.dma_start(
                            out=wd, in_=w_dn_v[:, 8 * fq:8 * fq + 8, cg * 512:(cg + 1) * 512]), writes=[wdr])
                        for k in range(8):
                            f = 8 * fq + k
                            for t in range(4):
                                sc.op('pe', lambda e, t=t, bs=bs, f=f, wd=wd, k=k: e.matmul(
                                    bank(bs + t), ZT[:, f, t * 128:(t + 1) * 128], wd[:, k, :],
                                    start=(f == 0), stop=(f == 63)),
                                    reads=[('ZT', f), wdr], writes=[PB(bs + t)], signal=(f == 63 or (k == 7 and t == 3)))
                        di += 1
                    for t in range(4):
                        tile_i = 4 * b + t
                        rs, re = tile_i * 128, tile_i * 128 + 128
                        hp = WA[cg % 2][:, 512 * t:512 * t + 512]
                        hpr = [WAr[cg % 2][t]]
                        sc.dma('sp', lambda e, hp=hp, rs=rs, re=re, cg=cg: e.dma_start(
                            out=hp, in_=h1s[rs:re, cg * 512:(cg + 1) * 512]), reads=[('h1s', tile_i)], writes=hpr)
                        sc.op('dve', lambda e, hp=hp, bs=bs, t=t: e.tensor_tensor(out=hp, in0=bank(bs + t), in1=hp, op=ALU.add),
                              reads=[PB(bs + t)] + hpr, writes=hpr)
                        sc.dma('sp', lambda e, hp=hp, rs=rs, re=re, cg=cg: e.dma_start(
                            out=y[rs:re, cg * 512:(cg + 1) * 512], in_=hp), reads=hpr, writes=[('y', tile_i, cg)], is_out=True)

        sc.finish('sp')

        with nc.Block() as block:
            @block.sync
            def _(e):
                sc.replay('sp', e)

            @block.gpsimd
            def _(e):
                sc.replay('pool', e)

            @block.scalar
            def _(e):
                sc.replay('act', e)

            @block.vector
            def _(e):
                sc.replay('dve', e)

            @block.tensor
            def _(e):
                sc.replay('pe', e)
    return nc


def _bf16(a):
    return np.asarray(a, dtype=np.float32).astype(ml_dtypes.bfloat16)


def make_consts(inputs):
    f = np.float32
    c = {}
    c["g1bc"] = np.ascontiguousarray(np.broadcast_to(np.asarray(inputs["norm1_g"], f)[0][None, :], (128, D)))
    c["g2bc"] = np.ascontiguousarray(np.broadcast_to(np.asarray(inputs["norm2_g"], f)[0][None, :], (128, D)))
    qg = np.asarray(inputs["q_norm_g"], f)[0]
    kg = np.asarray(inputs["k_norm_g"], f)[0]
    c["qkg"] = np.ascontiguousarray(np.broadcast_to(np.concatenate([qg, qg, kg, kg])[None, :], (128, 256)))
    c["sublng"] = np.ascontiguousarray(np.broadcast_to(np.asarray(inputs["subln_g"], f)[0][None, :], (128, 128)))
    lv = np.concatenate([np.asarray(inputs[k], f)[0] for k in ("lambda_q1", "lambda_k1", "lambda_q2", "lambda_k2")])
    c["lamv"] = np.ascontiguousarray(np.broadcast_to(lv[None, :], (128, 256)))
    inv_freq = (np.float32(500000.0) ** (-np.arange(0, 16, 2, dtype=f) / np.float32(16))).astype(f)
    cs = np.zeros((128, 17, 32), f)
    for j in range(17):
        pos = (np.arange(128) + (NMETA + 128 * j if j < 16 else 0)).astype(f)
        ang = (pos[:, None] * inv_freq[None, :]).astype(f)
        co, si = np.cos(ang).astype(f), np.sin(ang).astype(f)
        cs[:, j, 0:8] = co
        cs[:, j, 8:16] = co
        cs[:, j, 16:24] = -si
        cs[:, j, 24:32] = si
    c["cstab"] = cs.reshape(128, 17 * 32)
    dk = np.asarray(inputs["dw_kernel"], f)[0]
    c["dwk"] = np.ascontiguousarray(dk.T.reshape(8, 128, KTAPS).transpose(1, 0, 2)).reshape(128, 8 * KTAPS)
    cv = np.zeros((128, 24), f)
    cv[:, 0:8] = np.asarray(inputs["dw_bias"], f)[0].reshape(8, 128).T
    cv[:, 8:16] = np.asarray(inputs["conv_ln_g"], f)[0].reshape(8, 128).T
    cv[:, 16:24] = np.asarray(inputs["conv_ln_b"], f)[0].reshape(8, 128).T
    c["cvec"] = cv
    c["ident"] = _bf16(np.eye(128))
    kk = np.arange(128)
    c["maskneg"] = _bf16(np.where(kk[:, None] > kk[None, :], NEG, 0.0))
    return c


_NC_CACHE = {}


def kernel(**inputs):
    f = np.float32
    consts = make_consts(inputs)
    shared = dict(consts)
    shared["meta"] = np.ascontiguousarray(np.asarray(inputs["meta"], f))
    shared["w_in"] = np.ascontiguousarray(np.asarray(inputs["w_in"], f)[0])
    shared["w_attn_o"] = np.ascontiguousarray(np.asarray(inputs["w_attn_o"], f)[0])
    shared["w_conv_o"] = np.ascontiguousarray(np.asarray(inputs["w_conv_o"], f)[0])
    shared["w_out"] = np.ascontiguousarray(np.asarray(inputs["w_out"], f)[0])
    shared["w_up"] = np.ascontiguousarray(np.asarray(inputs["w_up"], f)[0])
    shared["w_down"] = np.ascontiguousarray(np.asarray(inputs["w_down"], f)[0])
    xin = np.asarray(inputs["x"], f)
    if "nc" not in _NC_CACHE:
        _NC_CACHE["nc"] = build_program()
    nc = _NC_CACHE["nc"]
    in_maps = []
    for b in range(8):
        m = dict(shared)
        m["x"] = np.ascontiguousarray(xin[b])
        in_maps.append(m)
    res = run_bass_kernel_spmd(nc, in_maps, core_ids=list(range(8)))
    out = np.stack([np.asarray(r["y"], f) for r in res.results], axis=0)
    return out
```

```python
import math
import numpy as np
import ml_dtypes
import concourse.bass as bass
import concourse.mybir as mybir
from concourse.bass_utils import run_bass_kernel_spmd
from concourse.alu_op_type import AluOpType as ALU

F32 = mybir.dt.float32
BF16 = mybir.dt.bfloat16
AF = mybir.ActivationFunctionType

D = 2048
S = 2048
NMETA = 16
L = S + NMETA
NT = S // 128
H = 8
DFF = 4 * D
CW = 1024
KTAPS = 31
EPS = 1e-6
LAM_INIT = 0.8 - 0.6 * math.exp(0.0)
NEG = -30000.0
DBG_HEADS = None
DBG_GROUPS = None
DBG_CUT = 99
DBG_LOOK = 2
DBG_H0 = 0


class Sched:
    def __init__(self, nc, esems, dsems):
        self.nc = nc
        self.esem = esems
        self.dsem = dsems
        self.cnt = {e: 0 for e in esems}
        self.ops = {e: [] for e in esems}
        self.seen = {e: {} for e in esems}
        self.lastw = {}
        self.readers = {}
        self.dcnt = {q: [0] * len(dsems[q]) for q in dsems}
        self.dnext = {q: 0 for q in dsems}
        self.out_handles = []

    def _sem(self, k):
        return self.esem[k] if isinstance(k, str) else self.dsem[k[1]][k[2]]

    def _collect(self, eng, reads, writes, extra):
        deps = list(extra)
        for r in reads:
            h = self.lastw.get(r)
            if h is not None:
                deps.append(h)
        for w in writes:
            h = self.lastw.get(w)
            if h is not None:
                deps.append(h)
            deps.extend(self.readers.get(w, {}).items())
        waits = {}
        for (k, n) in deps:
            if k == eng and eng == 'pe':
                continue
            if self.seen[eng].get(k, 0) >= n:
                continue
            if waits.get(k, 0) < n:
                waits[k] = n
        for k, n in waits.items():
            self.seen[eng][k] = n
        return list(waits.items())

    def _record(self, h, reads, writes):
        for r in reads:
            d = self.readers.setdefault(r, {})
            if d.get(h[0], 0) < h[1]:
                d[h[0]] = h[1]
        for w in writes:
            self.lastw[w] = h
            self.readers[w] = {}

    def op(self, eng, fn, reads=(), writes=(), signal=True):
        waits = self._collect(eng, reads, writes, ())
        if eng != 'pe':
            signal = True
        h = (eng, self.cnt[eng] + 1)
        if signal:
            self.cnt[eng] += 1
        self.ops[eng].append((waits, fn, self.esem[eng] if signal else None, 1))
        self._record(h, reads, writes)
        return h

    def dma(self, q, fn, reads=(), writes=(), is_out=False):
        i = self.dnext[q]
        self.dnext[q] = (i + 1) % len(self.dsem[q])
        key = ('d', q, i)
        extra = []
        if self.dcnt[q][i] > 0:
            extra.append((key, self.dcnt[q][i]))
        waits = self._collect(q, reads, writes, extra)
        self.dcnt[q][i] += 16
        h = (key, self.dcnt[q][i])
        self.ops[q].append((waits, fn, self.dsem[q][i], 16))
        self._record(h, reads, writes)
        if is_out:
            self.out_handles.append(h)
        return h

    def finish(self, eng='sp'):
        waits = {}
        for (k, n) in self.out_handles:
            if waits.get(k, 0) < n:
                waits[k] = n
        self.ops[eng].append((list(waits.items()), None, None, 0))

    def replay(self, eng, e):
        for (waits, fn, sem, inc) in self.ops[eng]:
            for (k, n) in waits:
                e.wait_ge(self._sem(k), n)
            if fn is None:
                continue
            ins = fn(e)
            if sem is not None:
                ins.then_inc(sem, inc)


def build_program(dbg=None, stop_after=None):
    nc = bass.Bass("TRN2", target_bir_lowering=False)
    dt = nc.dram_tensor
    x = dt("x", [S, D], F32, kind="ExternalInput").ap()
    meta = dt("meta", [NMETA, D], F32, kind="ExternalInput").ap()
    w_in = dt("w_in", [D, 9216], F32, kind="ExternalInput").ap()
    w_attn_o = dt("w_attn_o", [1024, D], F32, kind="ExternalInput").ap()
    w_conv_o = dt("w_conv_o", [1024, D], F32, kind="ExternalInput").ap()
    w_out = dt("w_out", [D, D], F32, kind="ExternalInput").ap()
    w_up = dt("w_up", [D, DFF], F32, kind="ExternalInput").ap()
    w_down = dt("w_down", [DFF, D], F32, kind="ExternalInput").ap()
    g1bc_d = dt("g1bc", [128, D], F32, kind="ExternalInput").ap()
    g2bc_d = dt("g2bc", [128, D], F32, kind="ExternalInput").ap()
    qkg_d = dt("qkg", [128, 256], F32, kind="ExternalInput").ap()
    sublng_d = dt("sublng", [128, 128], F32, kind="ExternalInput").ap()
    lamv_d = dt("lamv", [128, 256], F32, kind="ExternalInput").ap()
    cs_d = dt("cstab", [128, 17 * 32], F32, kind="ExternalInput").ap()
    dwk_d = dt("dwk", [128, 8 * KTAPS], F32, kind="ExternalInput").ap()
    cvec_d = dt("cvec", [128, 24], F32, kind="ExternalInput").ap()
    ident_d = dt("ident", [128, 128], BF16, kind="ExternalInput").ap()
    maskneg_d = dt("maskneg", [128, 128], BF16, kind="ExternalInput").ap()
    y = dt("y", [S, D], F32, kind="ExternalOutput").ap()
    h1s = dt("h1s", [S, D], F32, kind="ExternalOutput").ap()
    dbg_out = None
    if dbg is not None:
        dbg_out = dt("dbg", list(dbg[1]), dbg[2], kind="ExternalOutput").ap()

    import contextlib
    es = contextlib.ExitStack()
    with es:
        def sb(name, shape, dtype):
            return es.enter_context(nc.sbuf_tensor("sb_" + name, shape, dtype))

        def sem(name):
            return es.enter_context(nc.semaphore(name))

        uT = sb("uT", [128, 16, L], BF16)
        AR = sb("AR", [128, 32768], BF16)
        T0 = sb("T0", [128, 8192], BF16)
        WS = sb("WS", [128, 10240], BF16)
        WA = [sb("WA%d" % i, [128, 2064], F32) for i in range(2)]
        WBall = sb("WBall", [128, 4096], BF16)
        WB = [WBall[:, 0:2048], WBall[:, 2048:4096]]
        ident = sb("ident", [128, 128], BF16)
        maskneg = sb("maskneg", [128, 128], BF16)
        ones_bf = sb("ones_bf", [128, 128], BF16)
        gbc = sb("gbc", [128, D], F32)
        qkg = sb("qkg", [128, 256], F32)
        sublng = sb("sublng", [128, 128], F32)
        lamv = sb("lamv", [128, 256], F32)
        cstab = sb("cstab", [128, 17, 32], F32)
        dwk = sb("dwk", [128, 8, KTAPS], F32)
        cvec = sb("cvec", [128, 24], F32)
        small = sb("small", [128, 64], F32)
        mhalf = sb("mhalf", [128, 512], BF16)
        ssb = sb("ssb", [128, 48], F32)

        oT = AR[:, 0:16384].rearrange("p (c t) -> p c t", c=8)
        cT = AR[:, 16384:32768].rearrange("p (c t) -> p c t", c=8)
        ZT = AR[:, :].rearrange("p (f t) -> p f t", f=64)

        PS = es.enter_context(nc.psum_tensor("ps", [128, 4096], F32))
        PSB = PS[:].bitcast(BF16)

        def bank(i):
            return PS[:, 512 * i:512 * i + 512]

        def bank_bf(i):
            return PSB[:, 1024 * i:1024 * i + 1024]

        def PB(i):
            return ('ps', i)

        esems = {e: sem("s_" + e) for e in ['pe', 'act', 'dve', 'pool', 'sp']}
        dsems = {'sp': [sem("d_sp%d" % i) for i in range(12)], 'pool': [sem("d_pl%d" % i) for i in range(12)]}
        sc = Sched(nc, esems, dsems)
        dummy = sb("dummy", [128, 8], F32)

        def barrier(names):
            sc.op('pool', lambda e: e.memset(dummy[:, 0:2], 0.0), writes=list(names))
        w_in_v = w_in.rearrange("(c p) n -> p c n", p=128)
        w_ao_v = w_attn_o.rearrange("(c p) n -> p c n", p=128)
        w_co_v = w_conv_o.rearrange("(c p) n -> p c n", p=128)
        w_out_v = w_out.rearrange("(c p) n -> p c n", p=128)
        w_up_v = w_up.rearrange("(c p) n -> p c n", p=128)
        w_dn_v = w_down.rearrange("(c p) n -> p c n", p=128)

        WAr = [[('WA', i, k) for k in range(4)] for i in range(2)]
        WBr = [[('WB', i)] for i in range(2)]

        def ld(dst, src, res):
            sc.dma('sp', lambda e, d=dst, s=src: e.dma_start(out=d, in_=s), writes=[res])
        ld(ident[:], ident_d, 'ident')
        ld(maskneg[:], maskneg_d, 'maskneg')
        ld(gbc[:], g1bc_d, 'gbc')
        ld(qkg[:], qkg_d, 'qkg')
        ld(sublng[:], sublng_d, 'sublng')
        ld(lamv[:], lamv_d, 'lamv')
        ld(cstab[:].rearrange("p a b -> p (a b)"), cs_d, 'cstab')
        ld(dwk[:].rearrange("p a b -> p (a b)"), dwk_d, 'dwk')
        ld(cvec[:], cvec_d, 'cvec')
        sc.op('pool', lambda e: e.memset(mhalf[:], -0.5), writes=['mhalf'])
        sc.op('pool', lambda e: e.memset(ones_bf[:], 1.0), writes=['ones'])
        sc.op('dve', lambda e: e.tensor_scalar(out=dwk[:], in0=dwk[:], scalar1=0.5, scalar2=None, op0=ALU.mult),
              reads=['dwk'], writes=['dwk'])
        sc.op('dve', lambda e: e.tensor_scalar(out=cvec[:, 8:24], in0=cvec[:, 8:24], scalar1=0.5, scalar2=None,
                                               op0=ALU.mult), reads=['cvec'], writes=['cvec'])
        sc.op('dve', lambda e: e.tensor_scalar(out=sublng[:], in0=sublng[:], scalar1=1.0 - LAM_INIT, scalar2=None,
                                               op0=ALU.mult), reads=['sublng'], writes=['sublng'])
        sc.op('dve', lambda e: e.scalar_tensor_tensor(out=WB[0][:, 0:64], in0=lamv[:, 0:64], scalar=1.0, in1=lamv[:, 64:128],
                                                      op0=ALU.mult, op1=ALU.mult, accum_out=small[:, 0:1]),
              reads=['lamv'], writes=WBr[0] + ['sm0'])
        sc.op('dve', lambda e: e.scalar_tensor_tensor(out=WB[0][:, 0:64], in0=lamv[:, 128:192], scalar=1.0, in1=lamv[:, 192:256],
                                                      op0=ALU.mult, op1=ALU.mult, accum_out=small[:, 1:2]),
              reads=['lamv'], writes=WBr[0] + ['sm1'])
        sc.op('act', lambda e: e.activation(out=small[:, 2:4], in_=small[:, 0:2], func=AF.Exp),
              reads=['sm0', 'sm1'], writes=['sm23'])
        sc.op('dve', lambda e: e.tensor_tensor(out=small[:, 5:6], in0=small[:, 3:4], in1=small[:, 2:3],
                                               op=ALU.subtract), reads=['sm23'], writes=['sm5'])
        sc.op('dve', lambda e: e.tensor_scalar(out=small[:, 4:5], in0=small[:, 5:6], scalar1=-LAM_INIT,
                                               scalar2=None, op0=ALU.add), reads=['sm5'], writes=['neglam'])
        neglam = small[:, 4:5]

        def rstd_pool(io_ap, n, inv_n, res):
            rows = io_ap.shape[0]
            sc.op('pool', lambda e: e.tensor_scalar(out=io_ap, in0=io_ap, scalar1=inv_n, scalar2=EPS,
                                                    op0=ALU.mult, op1=ALU.add), reads=[res], writes=[res])
            sc.op('pool', lambda e: e.tensor_tensor(out=io_ap, in0=io_ap, in1=mhalf[:rows, 0:n], op=ALU.pow),
                  reads=[res, 'mhalf'], writes=[res])

        def norm_to_featmajor(xs, xs_res, rows, slot, colofs, dst_res, ssidx, have_ss=False):
            ssr = ('ss', ssidx)
            ss_ap = ssb[:rows, ssidx:ssidx + 1]
            if not have_ss:
                sc.op('act', lambda e: e.activation(out=WB[slot][:rows, :], in_=xs, func=AF.Square, accum_out=ss_ap),
                      reads=xs_res, writes=WBr[slot] + [ssr])
            rstd_pool(ss_ap, 1, 1.0 / D, ssr)
            xn = WB[slot]
            sc.op('dve', lambda e: e.scalar_tensor_tensor(out=xn[:rows, :], in0=xs, scalar=ss_ap,
                                                          in1=gbc[:rows, :], op0=ALU.mult, op1=ALU.mult),
                  reads=xs_res + [ssr, 'gbc'], writes=WBr[slot])
            pi = 3 - slot
            pt = PSB[:, 2048 * pi:2048 * pi + 2048]
            pres = [PB(2 * pi), PB(2 * pi + 1)]
            for c in range(16):
                sc.op('pe', lambda e, c=c: e.transpose(out=pt[:, c * 128:c * 128 + rows],
                                                       in_=xn[:rows, c * 128:(c + 1) * 128],
                                                       identity=ident[:rows, :rows]),
                      reads=WBr[slot] + ['ident'], writes=pres, signal=(c == 15))
            sc.op('act', lambda e: e.activation(out=uT[:, :, colofs:colofs + rows],
                                                in_=pt.rearrange("p (c t) -> p c t", c=16)[:, :, 0:rows],
                                                func=AF.Copy),
                  reads=pres, writes=[dst_res])

        def ublk(b):
            return [('uT', 4 * b + k) for k in range(4)]

        for j in range(NT + 1):
            slot = j % 2
            rows = 128 if j < NT else NMETA
            src = x[j * 128:(j + 1) * 128, :] if j < NT else meta
            colofs = NMETA + j * 128 if j < NT else 0
            sc.dma('sp', lambda e, s=src, sl=slot, r=rows: e.dma_start(out=WA[sl][:r, 0:D], in_=s),
                   writes=WAr[slot])
            norm_to_featmajor(WA[slot][:rows, 0:D], WAr[slot], rows, slot, colofs, ('uT', j), j)

        if dbg is not None and dbg[0] == 'uT':
            sc.dma('sp', lambda e: e.dma_start(out=dbg_out, in_=uT[:].rearrange("p a b -> p (a b)")),
                   reads=[('uT', j) for j in range(NT + 1)], is_out=True)
        if stop_after is None or stop_after > 1:
            sc.dma('sp', lambda e: e.dma_start(out=gbc[:], in_=g2bc_d), writes=['gbc'])

        TB = [(0, NMETA)] + [(NMETA + 512 * b, 512) for b in range(4)]
        TBres = [[('uT', 16)]] + [ublk(b) for b in range(4)]

        if stop_after is None or stop_after >= 2:
            aTp = [AR[:, 0:2096], AR[:, 2096:4192]]
            aTr = ['aT0', 'aT1']
            dg = WBall[:, 0:31 * 128].rearrange("p (j q) -> p j q", j=KTAPS)
            for k in range(2):
                sc.op('pool', lambda e, k=k: e.memset(aTp[k][:, 0:30], 0.0), writes=[aTr[k]])

            def emit_glu(cc):
                slot = cc % 2
                wg = T0[:, slot * 4096:(slot + 1) * 4096].rearrange("p (c w) -> p c w", c=16)
                wres = ('T0', slot)
                sc.dma('pool', lambda e: e.dma_start(out=wg[:, :, 0:128],
                                                     in_=w_in_v[:, :, 3072 + cc * 128:3072 + (cc + 1) * 128]), writes=[wres])
                sc.dma('pool', lambda e: e.dma_start(out=wg[:, :, 128:256],
                                                     in_=w_in_v[:, :, 4096 + cc * 128:4096 + (cc + 1) * 128]), writes=[wres])
                for bi, (c0, n) in enumerate(TB):
                    ba, bb = 2 * (bi % 2), 2 * (bi % 2) + 1
                    for (bk, wo) in ((ba, 0), (bb, 128)):
                        for c in range(16):
                            sc.op('pe', lambda e, c=c, bk=bk, wo=wo, c0=c0, n=n: e.matmul(
                                bank(bk)[:, 0:n], wg[:, c, wo:wo + 128], uT[:, c, c0:c0 + n],
                                start=(c == 0), stop=(c == 15)),
                                reads=[wres] + TBres[bi], writes=[PB(bk)], signal=(c == 15))
                    th = WA[0][:, 512 * (bi % 2):512 * (bi % 2) + 512]
                    thr = [WAr[0][bi % 2]]
                    sc.op('act', lambda e, th=th, bb=bb, n=n: e.activation(out=th[:, 0:n], in_=bank(bb)[:, 0:n],
                                                                        func=AF.Tanh, scale=0.5),
                          reads=[PB(bb)], writes=thr)
                    sc.op('dve', lambda e, th=th, ba=ba, n=n, c0=c0: e.scalar_tensor_tensor(
                        out=aTp[slot][:, 30 + c0:30 + c0 + n], in0=th[:, 0:n], scalar=1.0, in1=bank(ba)[:, 0:n],
                        op0=ALU.add, op1=ALU.mult),
                        reads=thr + [PB(ba)], writes=[aTr[slot]])

            def emit_conv(cc):
                slot = cc % 2
                sc.op('dve', lambda e: e.tensor_tensor(
                    out=dg, in0=ident[:, :].unsqueeze(1).to_broadcast([128, KTAPS, 128]),
                    in1=dwk[:, cc, :].unsqueeze(2).to_broadcast([128, KTAPS, 128]), op=ALU.mult),
                    reads=['ident', 'dwk'], writes=WBr[0] + WBr[1])
                for b in range(4):
                    bk = 4 + b % 2
                    for j in range(KTAPS):
                        c0 = NMETA + 512 * b + j
                        sc.op('pe', lambda e, j=j, c0=c0, bk=bk: e.matmul(
                            bank(bk), dg[:, j, :], aTp[slot][:, c0:c0 + 512], start=(j == 0), stop=(j == KTAPS - 1)),
                            reads=WBr[0] + WBr[1] + [aTr[slot]], writes=[PB(bk)], signal=(j == KTAPS - 1))
                    sc.op('dve', lambda e, b=b, bk=bk: e.tensor_scalar(
                        out=cT[:, cc, 512 * b:512 * b + 512], in0=bank(bk), scalar1=cvec[:, cc:cc + 1], scalar2=None,
                        op0=ALU.add), reads=[PB(bk), 'cvec'], writes=[('cT', cc)])

            for cc in range(9):
                if cc < 8:
                    emit_glu(cc)
                if cc >= 1:
                    emit_conv(cc - 1)
            sqb = T0[:, 0:4096].rearrange("p (c t) -> p c t", c=8)
            for b in range(4):
                blk = slice(512 * b, 512 * b + 512)
                sc.op('dve', lambda e, blk=blk: e.tensor_tensor(out=sqb, in0=cT[:, :, blk], in1=cT[:, :, blk],
                                                             op=ALU.mult),
                      reads=[('cT', c) for c in range(8)], writes=[('T0', 0)])
                for (bk, srcs) in ((6, 'c'), (7, 's')):
                    for c in range(8):
                        rhs = cT[:, c, blk] if srcs == 'c' else sqb[:, c, :]
                        sc.op('pe', lambda e, bk=bk, rhs=rhs, c=c: e.matmul(bank(bk), ones_bf[:, :], rhs,
                                                                         start=(c == 0), stop=(c == 7)),
                              reads=['ones', ('cT', c), ('T0', 0)], writes=[PB(bk)], signal=(c == 7))
                mean = WA[0][:, 0:512]
                msq = WA[0][:, 512:1024]
                rstd = WA[0][:, 1024:1536]
                sc.op('dve', lambda e: e.tensor_scalar(out=mean, in0=bank(6), scalar1=1.0 / CW, scalar2=None,
                                                       op0=ALU.mult), reads=[PB(6)], writes=[WAr[0][0]])
                sc.op('dve', lambda e: e.tensor_tensor(out=msq, in0=mean, in1=mean, op=ALU.mult),
                      reads=[WAr[0][0]], writes=[WAr[0][1]])
                sc.op('dve', lambda e: e.scalar_tensor_tensor(out=rstd, in0=bank(7), scalar=1.0 / CW, in1=msq,
                                                              op0=ALU.mult, op1=ALU.subtract),
                      reads=[PB(7), WAr[0][1]], writes=[WAr[0][2]])
                sc.op('pool', lambda e: e.tensor_scalar(out=rstd, in0=rstd, scalar1=EPS, scalar2=None, op0=ALU.add),
                      reads=[WAr[0][2]], writes=[WAr[0][2]])
                sc.op('pool', lambda e: e.tensor_tensor(out=rstd, in0=rstd, in1=mhalf[:, 0:512], op=ALU.pow),
                      reads=[WAr[0][2], 'mhalf'], writes=[WAr[0][2]])
                for cc in range(8):
                    k = cc % 2
                    t = WA[1][:, 1024 * k:1024 * k + 512]
                    th = WA[1][:, 1024 * k + 512:1024 * k + 1024]
                    tr = [WAr[1][2 * k]]
                    thr = [WAr[1][2 * k + 1]]
                    sc.op('dve', lambda e, t=t, cc=cc, blk=blk: e.tensor_tensor(out=t, in0=cT[:, cc, blk], in1=mean,
                                                                             op=ALU.subtract),
                          reads=[('cT', cc), WAr[0][0]], writes=tr)
                    sc.op('dve', lambda e, t=t: e.tensor_tensor(out=t, in0=t, in1=rstd, op=ALU.mult),
                          reads=tr + [WAr[0][2]], writes=tr)
                    sc.op('dve', lambda e, t=t, cc=cc: e.tensor_scalar(out=t, in0=t, scalar1=cvec[:, 8 + cc:9 + cc],
                                                                    scalar2=cvec[:, 16 + cc:17 + cc],
                                                                    op0=ALU.mult, op1=ALU.add),
                          reads=tr + ['cvec'], writes=tr)
                    sc.op('act', lambda e, t=t, th=th: e.activation(out=th, in_=t, func=AF.Tanh),
                          reads=tr, writes=thr)
                    sc.op('dve', lambda e, t=t, th=th, cc=cc, blk=blk: e.scalar_tensor_tensor(
                        out=cT[:, cc, blk], in0=th, scalar=1.0, in1=t, op0=ALU.add, op1=ALU.mult),
                        reads=tr + thr, writes=[('cT', cc)])
            if dbg is not None and dbg[0] == 'cT':
                sc.dma('sp', lambda e: e.dma_start(out=dbg_out, in_=AR[:, 16384:32768]),
                       reads=[('cT', c) for c in range(8)], is_out=True)

        if stop_after is None or stop_after >= 3:
            qTs = [T0[:, 0:2048], WS[:, 8192:10240]]
            kT = T0[:, 2048:2048 + L]
            VA0 = 2048 + L
            vaug = T0[:, VA0:VA0 + 17 * 130].rearrange("p (j v) -> p j v", j=17)
            PT0 = VA0 + 17 * 130
            PTs = [T0[:, PT0 + 512 * i:PT0 + 512 * (i + 1)] for i in range(3)]
            assert PT0 + 1536 <= 8192
            sc.op('pool', lambda e: e.memset(T0[:, VA0:VA0 + 17 * 130], 1.0),
                  writes=['vaug', 'qT', 'kT', ('T0', 0), ('T0', 1), 'aT0', 'aT1', ('WS', 2)] + [('PT', i) for i in range(3)]
                  + [('oT', c) for c in range(8)])
            sc.op('pool', lambda e: e.memset(qTs[0][64:128, :], 0.0), writes=['qT'])
            sc.op('pool', lambda e: e.memset(qTs[1][0:64, :], 0.0), writes=['qT'])
            for h in range(H if DBG_HEADS is None else DBG_HEADS):
                slot = h % 2
                wqk = WS[:, slot * 4096:(slot + 1) * 4096].rearrange("p (c w) -> p c w", c=16)
                wv = WB[slot][:, :].rearrange("p (c w) -> p c w", c=16)
                wqk_r = ('WS', slot)
                sc.dma('pool', lambda e, wqk=wqk, h=h: e.dma_start(out=wqk[:, :, 0:128],
                                                                 in_=w_in_v[:, :, h * 128:(h + 1) * 128]), writes=[wqk_r])
                sc.dma('pool', lambda e, wqk=wqk, h=h: e.dma_start(out=wqk[:, :, 128:256],
                                                                 in_=w_in_v[:, :, 1024 + h * 128:1024 + (h + 1) * 128]),
                       writes=[wqk_r])
                sc.dma('pool', lambda e, wv=wv, h=h: e.dma_start(out=wv, in_=w_in_v[:, :, 2048 + h * 128:2048 + (h + 1) * 128]),
                       writes=WBr[slot])
                for j in range(NT + 1):
                    rows = 128 if j < NT else NMETA
                    colofs = NMETA + j * 128 if j < NT else 0
                    pj = j % 2
                    pr = bank(pj)
                    for c in range(16):
                        sc.op('pe', lambda e, c=c, pr=pr, rows=rows, colofs=colofs, wqk=wqk: e.matmul(
                            pr[:rows, 0:256], uT[:, c, colofs:colofs + rows], wqk[:, c, :], start=(c == 0), stop=(c == 15)),
                            reads=[('uT', j), wqk_r], writes=[PB(pj)], signal=(c == 15))
                    for c in range(16):
                        sc.op('pe', lambda e, c=c, pr=pr, rows=rows, colofs=colofs, wv=wv: e.matmul(
                            pr[:rows, 256:384], uT[:, c, colofs:colofs + rows], wv[:, c, :], start=(c == 0), stop=(c == 15)),
                            reads=[('uT', j)] + WBr[slot], writes=[PB(pj)], signal=(c == 15))
                    wa = WA[pj]
                    sq = wa[:rows, 0:256]
                    qn = wa[:rows, 256:512]
                    ra = wa[:rows, 512:576].rearrange("p (g d) -> p g d", g=4)
                    rb = wa[:rows, 576:640].rearrange("p (g d) -> p g d", g=4)
                    s4 = ssb[:rows, 24 + 4 * pj:28 + 4 * pj]
                    s4r = ('s4', pj)
                    war = [WAr[pj][0], WAr[pj][1]]
                    sc.op('act', lambda e, sq=sq, pr=pr, rows=rows: e.activation(out=sq, in_=pr[:rows, 0:256], func=AF.Square),
                          reads=[PB(pj)], writes=war)
                    sc.op('dve', lambda e, sq=sq, s4=s4: e.tensor_reduce(out=s4, in_=sq.rearrange("p (g d) -> p g d", g=4),
                                                                      axis=mybir.AxisListType.X, op=ALU.add),
                          reads=war, writes=[s4r])
                    rstd_pool(s4, 4, 1.0 / 64, s4r)
                    sc.op('dve', lambda e, qn=qn, pr=pr, rows=rows, s4=s4: e.tensor_tensor(
                        out=qn.rearrange("p (g d) -> p g d", g=4),
                        in0=pr[:rows, 0:256].rearrange("p (g d) -> p g d", g=4),
                        in1=s4.unsqueeze(2).to_broadcast([rows, 4, 64]), op=ALU.mult),
                        reads=[PB(pj), s4r], writes=war)
                    sc.op('dve', lambda e, qn=qn, rows=rows: e.tensor_tensor(out=qn, in0=qn, in1=qkg[:rows, :], op=ALU.mult),
                          reads=war + ['qkg'], writes=war)
                    q3 = qn.rearrange("p (g d) -> p g d", g=4)
                    cs = cstab[:rows, j, :]
                    sc.op('dve', lambda e, q3=q3, ra=ra, cs=cs, rows=rows: e.tensor_tensor(
                        out=ra, in0=q3[:, :, 0:16], in1=cs[:, 0:16].unsqueeze(1).to_broadcast([rows, 4, 16]), op=ALU.mult),
                        reads=war + ['cstab'], writes=war)
                    sc.op('dve', lambda e, q3=q3, rb=rb, cs=cs, rows=rows: e.tensor_tensor(
                        out=rb[:, :, 0:8], in0=q3[:, :, 8:16], in1=cs[:, 16:24].unsqueeze(1).to_broadcast([rows, 4, 8]),
                        op=ALU.mult), reads=war + ['cstab'], writes=war)
                    sc.op('dve', lambda e, q3=q3, rb=rb, cs=cs, rows=rows: e.tensor_tensor(
                        out=rb[:, :, 8:16], in0=q3[:, :, 0:8], in1=cs[:, 24:32].unsqueeze(1).to_broadcast([rows, 4, 8]),
                        op=ALU.mult), reads=war + ['cstab'], writes=war)
                    sc.op('dve', lambda e, q3=q3, ra=ra, rb=rb: e.tensor_tensor(out=q3[:, :, 0:16], in0=ra, in1=rb, op=ALU.add),
                          reads=war, writes=war)
                    qkb = WA[pj][:, 1024:1152].bitcast(BF16)
                    qkr = [WAr[pj][2]]
                    sc.op('act', lambda e, qkb=qkb, qn=qn, rows=rows: e.activation(out=qkb[:rows, :], in_=qn, func=AF.Copy),
                          reads=war, writes=qkr)
                    sc.op('act', lambda e, pr=pr, rows=rows, j=j: e.activation(out=vaug[:rows, j, 0:128],
                                                                            in_=pr[:rows, 256:384], func=AF.Copy),
                          reads=[PB(pj)], writes=['vaug'])
                    tb = 2 + pj
                    ptb = bank_bf(tb)
                    sc.op('pe', lambda e, ptb=ptb, qkb=qkb, rows=rows: e.transpose(out=ptb[:, 0:rows], in_=qkb[:rows, 0:128],
                                                                                identity=ident[:rows, :rows]),
                          reads=qkr + ['ident'], writes=[PB(tb)], signal=False)
                    sc.op('pe', lambda e, ptb=ptb, qkb=qkb, rows=rows: e.transpose(out=ptb[:, 128:128 + rows],
                                                                                in_=qkb[:rows, 128:256],
                                                                                identity=ident[:rows, :rows]),
                          reads=qkr + ['ident'], writes=[PB(tb)], signal=True)
                    if j < NT:
                        for c in range(2):
                            sc.op('act', lambda e, ptb=ptb, j=j, c=c: e.activation(
                                out=qTs[c][64 * c:64 * c + 64, 128 * j:128 * j + 128], in_=ptb[64 * c:64 * c + 64, 0:128],
                                func=AF.Copy), reads=[PB(tb)], writes=['qT'])
                    sc.op('dve', lambda e, ptb=ptb, rows=rows, colofs=colofs: e.tensor_copy(
                        out=kT[:, colofs:colofs + rows], in_=ptb[:, 128:128 + rows]),
                        reads=[PB(tb)], writes=['kT'])
                step = 0
                for g2 in range(NT // 2 if DBG_GROUPS is None else DBG_GROUPS):
                    j0, j1 = 2 * g2, 2 * g2 + 1
                    kts = ['m'] + list(range(j1 + 1))
                    for kt in kts:
                        krows = NMETA if kt == 'm' else 128
                        kc0 = 0 if kt == 'm' else NMETA + 128 * kt
                        kidx = 16 if kt == 'm' else kt
                        both = (kt == 'm') or (kt <= j0)
                        qc0 = 128 * j0 if both else 128 * j1
                        N = 256 if both else 128
                        sbk = 4 + step % 3
                        sbank = bank(sbk)
                        diag = (kt != 'm') and (kt == j0 or kt == j1)
                        for c in range(2):
                            sc.op('pe', lambda e, c=c, sbank=sbank, krows=krows, kc0=kc0, qc0=qc0, N=N, diag=diag: e.matmul(
                                sbank[:krows, c * 256:c * 256 + N], kT[:, kc0:kc0 + krows],
                                qTs[c][:, qc0:qc0 + N], start=True, stop=(not diag)),
                                reads=['kT', 'qT'], writes=[PB(sbk)], signal=(c == 1 and not diag))
                            if diag:
                                sc.op('pe', lambda e, c=c, sbank=sbank: e.matmul(
                                    sbank[:, c * 256:c * 256 + 128], ident[:, :], maskneg[:, :], start=False, stop=True),
                                    reads=['ident', 'maskneg'], writes=[PB(sbk)], signal=(c == 1))
                        PT = PTs[step % 3]
                        ptr = ('PT', step % 3)
                        sc.op('act', lambda e, PT=PT, sbank=sbank, krows=krows, N=N: e.activation(
                            out=PT[:krows, :].rearrange("p (c n) -> p c n", c=2)[:, :, 0:N],
                            in_=sbank[:krows, :].rearrange("p (c n) -> p c n", c=2)[:, :, 0:N],
                            func=AF.Exp, scale=0.125),
                            reads=[PB(sbk)], writes=[ptr])
                        for c in range(2):
                            qts = (j0, j1) if both else (j1,)
                            for qt in qts:
                                ab = 2 * (qt - j0) + c
                                po = c * 256 + ((qt - j0) * 128 if both else 0)
                                sc.op('pe', lambda e, ab=ab, PT=PT, po=po, krows=krows, kidx=kidx, kt=kt, qt=qt: e.matmul(
                                    bank(ab)[:, 0:129], PT[:krows, po:po + 128], vaug[:krows, kidx, 0:129],
                                    start=(kt == 'm'), stop=(kt == qt)),
                                    reads=[ptr, 'vaug'], writes=[PB(ab)], signal=(kt == qt or (c == 1 and qt == qts[-1])))
                        step += 1
                    for qt in (j0, j1):
                        a0, a1 = 2 * (qt - j0), 2 * (qt - j0) + 1
                        k = qt % 2
                        sm = small[:, 16 + 8 * k:24 + 8 * k]
                        smr = ('smq', k)
                        o1 = WA[k][:, 1536:1664]
                        o2 = WA[k][:, 1664:1792]
                        onb = WA[k][:, 1792:1856].bitcast(BF16)
                        wr = [WAr[k][3]]
                        sc.op('dve', lambda e, sm=sm, a0=a0: e.reciprocal(out=sm[:, 0:1], in_=bank(a0)[:, 128:129]),
                              reads=[PB(a0)], writes=[smr])
                        sc.op('dve', lambda e, sm=sm, a1=a1: e.reciprocal(out=sm[:, 1:2], in_=bank(a1)[:, 128:129]),
                              reads=[PB(a1), smr], writes=[smr])
                        sc.op('dve', lambda e, sm=sm: e.tensor_tensor(out=sm[:, 2:3], in0=sm[:, 1:2], in1=neglam, op=ALU.mult),
                              reads=[smr, 'neglam'], writes=[smr])
                        sc.op('dve', lambda e, sm=sm, a0=a0, o1=o1: e.tensor_scalar(out=o1, in0=bank(a0)[:, 0:128],
                                                                               scalar1=sm[:, 0:1], scalar2=None, op0=ALU.mult),
                              reads=[PB(a0), smr], writes=wr)
                        sc.op('dve', lambda e, sm=sm, a1=a1, o1=o1: e.scalar_tensor_tensor(
                            out=o1, in0=bank(a1)[:, 0:128], scalar=sm[:, 2:3], in1=o1, op0=ALU.mult, op1=ALU.add),
                            reads=[PB(a1), smr] + wr, writes=wr)
                        sc.op('dve', lambda e, sm=sm, o1=o1, o2=o2: e.scalar_tensor_tensor(out=o2, in0=o1, scalar=1.0, in1=o1, op0=ALU.mult, op1=ALU.mult, accum_out=sm[:, 3:4]),
                            reads=wr, writes=wr + [('smr', k)])
                        rstd_pool(sm[:, 3:4], 1, 1.0 / 128, ('smr', k))
                        sc.op('dve', lambda e, sm=sm, o1=o1, onb=onb: e.scalar_tensor_tensor(
                            out=onb, in0=o1, scalar=sm[:, 3:4], in1=sublng[:, :], op0=ALU.mult, op1=ALU.mult),
                            reads=wr + [('smr', k), 'sublng'], writes=wr)
                        ptb = bank_bf(7)
                        sc.op('pe', lambda e, ptb=ptb, onb=onb: e.transpose(out=ptb[:, 0:128], in_=onb, identity=ident[:, :]),
                              reads=wr + ['ident'], writes=[PB(7)])
                        sc.op('act', lambda e, ptb=ptb, qt=qt, h=h: e.activation(out=oT[:, h, 128 * qt:128 * qt + 128],
                                                                              in_=ptb[:, 0:128], func=AF.Copy),
                              reads=[PB(7)], writes=[('oT', h)])
            if dbg is not None and dbg[0] == 'oT':
                sc.dma('sp', lambda e: e.dma_start(out=dbg_out, in_=AR[:, 0:16384]),
                       reads=[('oT', c) for c in range(8)], is_out=True)

        if stop_after is None or stop_after >= 4:
            mT = T0[:, :].rearrange("p (c t) -> p c t", c=16)
            barrier(['vaug', 'qT', 'kT', ('WS', 2)] + [('PT', i) for i in range(3)] + [('mT', fc) for fc in range(16)])
            it = 0
            for b in range(4):
                ucol = NMETA + 512 * b
                blk = slice(512 * b, 512 * b + 512)
                for fc in range(16):
                    slot = it % 2
                    wg = WS[:, slot * 4096:(slot + 1) * 4096].rearrange("p (c w) -> p c w", c=16)
                    wo = WS[:, 8192:10240].rearrange("p (c w) -> p c w", c=16)
                    wgr = ('WS', slot)
                    wor = ('WS', 2)
                    sc.dma('pool', lambda e, wg=wg, fc=fc: e.dma_start(
                        out=wg[:, :, 0:128], in_=w_in_v[:, :, 5120 + fc * 128:5120 + (fc + 1) * 128]), writes=[wgr])
                    sc.dma('pool', lambda e, wg=wg, fc=fc: e.dma_start(
                        out=wg[:, :, 128:256], in_=w_in_v[:, :, 7168 + fc * 128:7168 + (fc + 1) * 128]), writes=[wgr])
                    sc.dma('pool', lambda e, wo=wo, fc=fc: e.dma_start(
                        out=wo[:, 0:8, :], in_=w_ao_v[:, :, fc * 128:(fc + 1) * 128]), writes=[wor])
                    sc.dma('pool', lambda e, wo=wo, fc=fc: e.dma_start(
                        out=wo[:, 8:16, :], in_=w_co_v[:, :, fc * 128:(fc + 1) * 128]), writes=[wor])
                    bs = 4 * (it % 2)
                    for c in range(16):
                        sc.op('pe', lambda e, c=c, bs=bs, wg=wg, ucol=ucol: e.matmul(
                            bank(bs), wg[:, c, 0:128], uT[:, c, ucol:ucol + 512], start=(c == 0), stop=(c == 15)),
                            reads=[wgr] + ublk(b), writes=[PB(bs)], signal=(c == 15))
                    for c in range(16):
                        sc.op('pe', lambda e, c=c, bs=bs, wg=wg, ucol=ucol: e.matmul(
                            bank(bs + 1), wg[:, c, 128:256], uT[:, c, ucol:ucol + 512], start=(c == 0), stop=(c == 15)),
                            reads=[wgr] + ublk(b), writes=[PB(bs + 1)], signal=(c == 15))
                    for c in range(8):
                        sc.op('pe', lambda e, c=c, bs=bs, wo=wo, blk=blk: e.matmul(
                            bank(bs + 2), wo[:, c, :], oT[:, c, blk], start=(c == 0), stop=(c == 7)),
                            reads=[wor] + [('oT', c)], writes=[PB(bs + 2)], signal=(c == 7))
                    for c in range(8):
                        sc.op('pe', lambda e, c=c, bs=bs, wo=wo, blk=blk: e.matmul(
                            bank(bs + 3), wo[:, 8 + c, :], cT[:, c, blk], start=(c == 0), stop=(c == 7)),
                            reads=[wor] + [('cT', c)], writes=[PB(bs + 3)], signal=(c == 7))
                    k = it % 2
                    ta = WA[1][:, 1024 * k:1024 * k + 512]
                    tcn = WA[1][:, 1024 * k + 512:1024 * k + 1024]
                    tar = [WAr[1][2 * k]]
                    tcr = [WAr[1][2 * k + 1]]
                    sc.op('act', lambda e, ta=ta, bs=bs: e.activation(out=ta, in_=bank(bs), func=AF.Tanh, scale=0.5),
                          reads=[PB(bs)], writes=tar)
                    sc.op('act', lambda e, tcn=tcn, bs=bs: e.activation(out=tcn, in_=bank(bs + 1), func=AF.Tanh, scale=0.5),
                          reads=[PB(bs + 1)], writes=tcr)
                    sc.op('dve', lambda e, ta=ta, bs=bs: e.scalar_tensor_tensor(
                        out=ta, in0=ta, scalar=1.0, in1=bank(bs + 2), op0=ALU.add, op1=ALU.mult),
                        reads=tar + [PB(bs + 2)], writes=tar)
                    sc.op('dve', lambda e, tcn=tcn, bs=bs: e.scalar_tensor_tensor(
                        out=tcn, in0=tcn, scalar=1.0, in1=bank(bs + 3), op0=ALU.add, op1=ALU.mult),
                        reads=tcr + [PB(bs + 3)], writes=tcr)
                    sc.op('dve', lambda e, ta=ta, tcn=tcn, fc=fc: e.tensor_tensor(out=mT[:, fc, :], in0=ta, in1=tcn, op=ALU.add),
                          reads=tar + tcr, writes=[('mT', fc)])
                    it += 1
                wslots = [WS[:, 4096 + 1024 * i:4096 + 1024 * (i + 1)] for i in range(4)]
                wsr = [('WSq', i) for i in range(4)]
                barrier([('WS', 1)] + wsr)
                di = 0
                for cg in range(4):
                    bs = 4 * (cg % 2)
                    for fp in range(8):
                        sl = di % 4
                        wsl = wslots[sl].rearrange("p (k n) -> p k n", k=2)
                        wres = [wsr[sl]]
                        sc.dma('pool', lambda e, wsl=wsl, fp=fp, cg=cg: e.dma_start(
                            out=wsl, in_=w_out_v[:, 2 * fp:2 * fp + 2, cg * 512:(cg + 1) * 512]), writes=wres)
                        for k in range(2):
                            fc = 2 * fp + k
                            for t in range(4):
                                sc.op('pe', lambda e, t=t, bs=bs, fc=fc, wsl=wsl, k=k: e.matmul(
                                    bank(bs + t), mT[:, fc, t * 128:(t + 1) * 128], wsl[:, k, :],
                                    start=(fc == 0), stop=(fc == 15)),
                                    reads=[('mT', fc)] + wres, writes=[PB(bs + t)], signal=(fc == 15 or (k == 1 and t == 3)))
                        di += 1
                    for t in range(4):
                        tile_i = 4 * b + t
                        xp = WA[0][:, 512 * t:512 * t + 512]
                        xpr = [WAr[0][t]]
                        rs, re = tile_i * 128, tile_i * 128 + 128
                        sc.dma('sp', lambda e, xp=xp, rs=rs, re=re, cg=cg: e.dma_start(
                            out=xp, in_=x[rs:re, cg * 512:(cg + 1) * 512]), writes=xpr)
                        sc.op('dve', lambda e, xp=xp, bs=bs, t=t: e.scalar_tensor_tensor(
                            out=xp, in0=bank(bs + t), scalar=0.5, in1=xp, op0=ALU.mult, op1=ALU.add),
                            reads=[PB(bs + t)] + xpr, writes=xpr)
                        sc.op('act', lambda e, xp=xp, t=t, cg=cg: e.activation(
                            out=WB[0][:, 512 * (t % 2):512 * (t % 2) + 512], in_=xp, func=AF.Square,
                            accum_out=ssb[:, 4 * t + cg:4 * t + cg + 1]),
                            reads=xpr, writes=WBr[0] + [('ssq', t)])
                        sc.dma('sp', lambda e, xp=xp, rs=rs, re=re, cg=cg: e.dma_start(
                            out=h1s[rs:re, cg * 512:(cg + 1) * 512], in_=xp), reads=xpr, writes=[('h1s', tile_i)])
                barrier([('WS', 1)] + wsr)
                for t in range(4):
                    tile_i = 4 * b + t
                    rs, re = tile_i * 128, tile_i * 128 + 128
                    slot = t % 2
                    sc.op('dve', lambda e, t=t: e.tensor_reduce(out=ssb[:, 16 + t:17 + t], in_=ssb[:, 4 * t:4 * t + 4],
                                                              axis=mybir.AxisListType.X, op=ALU.add),
                          reads=[('ssq', t)], writes=[('ss', 16 + t)])
                    sc.dma('sp', lambda e, slot=slot, rs=rs, re=re: e.dma_start(out=WA[slot][:, 0:D], in_=h1s[rs:re, :]),
                           reads=[('h1s', tile_i)], writes=WAr[slot])
                    norm_to_featmajor(WA[slot][:, 0:D], WAr[slot], 128, 0, NMETA + 128 * tile_i, ('uT', tile_i), 16 + t,
                                      have_ss=True)
            if dbg is not None and dbg[0] == 'h1':
                sc.dma('sp', lambda e: e.dma_start(out=dbg_out, in_=h1s), reads=[('h1s', i) for i in range(16)], is_out=True)
            if dbg is not None and dbg[0] == 'u2T':
                sc.dma('sp', lambda e: e.dma_start(out=dbg_out, in_=uT[:].rearrange("p a b -> p (a b)")),
                       reads=[('uT', j) for j in range(NT + 1)], is_out=True)

        if stop_after is None or stop_after >= 5:
            ui = 0
            barrier([('mT', fc) for fc in range(16)] + [('T0d', 0), ('T0d', 1)] + WBr[1] + [('WB1q', q) for q in range(4)]
                    + [('WS', 1)] + [('WSq', i) for i in range(4)])
            for b in range(4):
                ucol = NMETA + 512 * b
                for fp in range(32):
                    slot = ui % 2
                    wu = WS[:, slot * 4096:(slot + 1) * 4096].rearrange("p (c w) -> p c w", c=16)
                    wur = ('WS', slot)
                    sc.dma('pool', lambda e, wu=wu, fp=fp: e.dma_start(out=wu, in_=w_up_v[:, :, fp * 256:(fp + 1) * 256]),
                           writes=[wur])
                    for k in range(2):
                        f = 2 * fp + k
                        ub = f % 4
                        for c in range(16):
                            sc.op('pe', lambda e, c=c, ub=ub, wu=wu, k=k, ucol=ucol: e.matmul(
                                bank(ub), wu[:, c, k * 128:(k + 1) * 128], uT[:, c, ucol:ucol + 512],
                                start=(c == 0), stop=(c == 15)),
                                reads=[wur] + ublk(b), writes=[PB(ub)], signal=(c == 15))
                        r = WB[1][:, 512 * (f % 4):512 * (f % 4) + 512]
                        rr = [('WB1q', f % 4)]
                        sc.op('act', lambda e, r=r, ub=ub: e.activation(out=r, in_=bank(ub), func=AF.Relu),
                              reads=[PB(ub)], writes=rr)
                        sc.op('dve', lambda e, r=r, f=f: e.tensor_tensor(out=ZT[:, f, :], in0=r, in1=r, op=ALU.mult),
                              reads=rr, writes=[('ZT', f)])
                    ui += 1
                di = 0
                for cg in range(4):
                    bs = 4 * (cg % 2)
                    for fq in range(8):
                        sl = di % 2
                        wd = T0[:, sl * 4096:(sl + 1) * 4096].rearrange("p (k n) -> p k n", k=8)
                        wdr = ('T0d', sl)
                        sc.dma('pool', lambda e, wd=wd, fq=fq, cg=cg: e.dma_start(
                            out=wd, in_=w_dn_v[:, 8 * fq:8 * fq + 8, cg * 512:(cg + 1) * 512]), writes=[wdr])
                        for k in range(8):
                            f = 8 * fq + k
                            for t in range(4):
                                sc.op('pe', lambda e, t=t, bs=bs, f=f, wd=wd, k=k: e.matmul(
                                    bank(bs + t), ZT[:, f, t * 128:(t + 1) * 128], wd[:, k, :],
                                    start=(f == 0), stop=(f == 63)),
                                    reads=[('ZT', f), wdr], writes=[PB(bs + t)], signal=(f == 63 or (k == 7 and t == 3)))
                        di += 1
                    for t in range(4):
                        tile_i = 4 * b + t
                        rs, re = tile_i * 128, tile_i * 128 + 128
                        hp = WA[cg % 2][:, 512 * t:512 * t + 512]
                        hpr = [WAr[cg % 2][t]]
                        sc.dma('sp', lambda e, hp=hp, rs=rs, re=re, cg=cg: e.dma_start(
                            out=hp, in_=h1s[rs:re, cg * 512:(cg + 1) * 512]), reads=[('h1s', tile_i)], writes=hpr)
                        sc.op('dve', lambda e, hp=hp, bs=bs, t=t: e.tensor_tensor(out=hp, in0=bank(bs + t), in1=hp, op=ALU.add),
                              reads=[PB(bs + t)] + hpr, writes=hpr)
                        sc.dma('sp', lambda e, hp=hp, rs=rs, re=re, cg=cg: e.dma_start(
                            out=y[rs:re, cg * 512:(cg + 1) * 512], in_=hp), reads=hpr, writes=[('y', tile_i, cg)], is_out=True)

        sc.finish('sp')

        with nc.Block() as block:
            @block.sync
            def _(e):
                sc.replay('sp', e)

            @block.gpsimd
            def _(e):
                sc.replay('pool', e)

            @block.scalar
            def _(e):
                sc.replay('act', e)

            @block.vector
            def _(e):
                sc.replay('dve', e)

            @block.tensor
            def _(e):
                sc.replay('pe', e)
    return nc


def _bf16(a):
    return np.asarray(a, dtype=np.float32).astype(ml_dtypes.bfloat16)


def make_consts(inputs):
    f = np.float32
    c = {}
    c["g1bc"] = np.ascontiguousarray(np.broadcast_to(np.asarray(inputs["norm1_g"], f)[0][None, :], (128, D)))
    c["g2bc"] = np.ascontiguousarray(np.broadcast_to(np.asarray(inputs["norm2_g"], f)[0][None, :], (128, D)))
    qg = np.asarray(inputs["q_norm_g"], f)[0]
    kg = np.asarray(inputs["k_norm_g"], f)[0]
    c["qkg"] = np.ascontiguousarray(np.broadcast_to(np.concatenate([qg, qg, kg, kg])[None, :], (128, 256)))
    c["sublng"] = np.ascontiguousarray(np.broadcast_to(np.asarray(inputs["subln_g"], f)[0][None, :], (128, 128)))
    lv = np.concatenate([np.asarray(inputs[k], f)[0] for k in ("lambda_q1", "lambda_k1", "lambda_q2", "lambda_k2")])
    c["lamv"] = np.ascontiguousarray(np.broadcast_to(lv[None, :], (128, 256)))
    inv_freq = (np.float32(500000.0) ** (-np.arange(0, 16, 2, dtype=f) / np.float32(16))).astype(f)
    cs = np.zeros((128, 17, 32), f)
    for j in range(17):
        pos = (np.arange(128) + (NMETA + 128 * j if j < 16 else 0)).astype(f)
        ang = (pos[:, None] * inv_freq[None, :]).astype(f)
        co, si = np.cos(ang).astype(f), np.sin(ang).astype(f)
        cs[:, j, 0:8] = co
        cs[:, j, 8:16] = co
        cs[:, j, 16:24] = -si
        cs[:, j, 24:32] = si
    c["cstab"] = cs.reshape(128, 17 * 32)
    dk = np.asarray(inputs["dw_kernel"], f)[0]
    c["dwk"] = np.ascontiguousarray(dk.T.reshape(8, 128, KTAPS).transpose(1, 0, 2)).reshape(128, 8 * KTAPS)
    cv = np.zeros((128, 24), f)
    cv[:, 0:8] = np.asarray(inputs["dw_bias"], f)[0].reshape(8, 128).T
    cv[:, 8:16] = np.asarray(inputs["conv_ln_g"], f)[0].reshape(8, 128).T
    cv[:, 16:24] = np.asarray(inputs["conv_ln_b"], f)[0].reshape(8, 128).T
    c["cvec"] = cv
    c["ident"] = _bf16(np.eye(128))
    kk = np.arange(128)
    c["maskneg"] = _bf16(np.where(kk[:, None] > kk[None, :], NEG, 0.0))
    return c


_NC_CACHE = {}


def kernel(**inputs):
    f = np.float32
    consts = make_consts(inputs)
    shared = dict(consts)
    shared["meta"] = np.ascontiguousarray(np.asarray(inputs["meta"], f))
    shared["w_in"] = np.ascontiguousarray(np.asarray(inputs["w_in"], f)[0])
    shared["w_attn_o"] = np.ascontiguousarray(np.asarray(inputs["w_attn_o"], f)[0])
    shared["w_conv_o"] = np.ascontiguousarray(np.asarray(inputs["w_conv_o"], f)[0])
    shared["w_out"] = np.ascontiguousarray(np.asarray(inputs["w_out"], f)[0])
    shared["w_up"] = np.ascontiguousarray(np.asarray(inputs["w_up"], f)[0])
    shared["w_down"] = np.ascontiguousarray(np.asarray(inputs["w_down"], f)[0])
    xin = np.asarray(inputs["x"], f)
    if "nc" not in _NC_CACHE:
        _NC_CACHE["nc"] = build_program()
    nc = _NC_CACHE["nc"]
    in_maps = []
    for b in range(8):
        m = dict(shared)
        m["x"] = np.ascontiguousarray(xin[b])
        in_maps.append(m)
    res = run_bass_kernel_spmd(nc, in_maps, core_ids=list(range(8)))
    out = np.stack([np.asarray(r["y"], f) for r in res.results], axis=0)
    return out
```

```python
import math
import numpy as np
import ml_dtypes
import concourse.bass as bass
import concourse.mybir as mybir
from concourse.bass_utils import run_bass_kernel_spmd
from concourse.alu_op_type import AluOpType as ALU

F32 = mybir.dt.float32
BF16 = mybir.dt.bfloat16
AF = mybir.ActivationFunctionType

D = 2048
S = 2048
NMETA = 16
L = S + NMETA
NT = S // 128
H = 8
DFF = 4 * D
CW = 1024
KTAPS = 31
EPS = 1e-6
LAM_INIT = 0.8 - 0.6 * math.exp(0.0)
NEG = -30000.0
DBG_HEADS = None
DBG_GROUPS = None
DBG_CUT = 99
DBG_LOOK = 2
DBG_H0 = 0


class Sched:
    def __init__(self, nc, esems, dsems):
        self.nc = nc
        self.esem = esems
        self.dsem = dsems
        self.cnt = {e: 0 for e in esems}
        self.ops = {e: [] for e in esems}
        self.seen = {e: {} for e in esems}
        self.lastw = {}
        self.readers = {}
        self.dcnt = {q: [0] * len(dsems[q]) for q in dsems}
        self.dnext = {q: 0 for q in dsems}
        self.out_handles = []

    def _sem(self, k):
        return self.esem[k] if isinstance(k, str) else self.dsem[k[1]][k[2]]

    def _collect(self, eng, reads, writes, extra):
        deps = list(extra)
        for r in reads:
            h = self.lastw.get(r)
            if h is not None:
                deps.append(h)
        for w in writes:
            h = self.lastw.get(w)
            if h is not None:
                deps.append(h)
            deps.extend(self.readers.get(w, {}).items())
        waits = {}
        for (k, n) in deps:
            if k == eng and eng == 'pe':
                continue
            if self.seen[eng].get(k, 0) >= n:
                continue
            if waits.get(k, 0) < n:
                waits[k] = n
        for k, n in waits.items():
            self.seen[eng][k] = n
        return list(waits.items())

    def _record(self, h, reads, writes):
        for r in reads:
            d = self.readers.setdefault(r, {})
            if d.get(h[0], 0) < h[1]:
                d[h[0]] = h[1]
        for w in writes:
            self.lastw[w] = h
            self.readers[w] = {}

    def op(self, eng, fn, reads=(), writes=(), signal=True):
        waits = self._collect(eng, reads, writes, ())
        if eng != 'pe':
            signal = True
        h = (eng, self.cnt[eng] + 1)
        if signal:
            self.cnt[eng] += 1
        self.ops[eng].append((waits, fn, self.esem[eng] if signal else None, 1))
        self._record(h, reads, writes)
        return h

    def dma(self, q, fn, reads=(), writes=(), is_out=False):
        i = self.dnext[q]
        self.dnext[q] = (i + 1) % len(self.dsem[q])
        key = ('d', q, i)
        extra = []
        if self.dcnt[q][i] > 0:
            extra.append((key, self.dcnt[q][i]))
        waits = self._collect(q, reads, writes, extra)
        self.dcnt[q][i] += 16
        h = (key, self.dcnt[q][i])
        self.ops[q].append((waits, fn, self.dsem[q][i], 16))
        self._record(h, reads, writes)
        if is_out:
            self.out_handles.append(h)
        return h

    def finish(self, eng='sp'):
        waits = {}
        for (k, n) in self.out_handles:
            if waits.get(k, 0) < n:
                waits[k] = n
        self.ops[eng].append((list(waits.items()), None, None, 0))

    def replay(self, eng, e):
        for (waits, fn, sem, inc) in self.ops[eng]:
            for (k, n) in waits:
                e.wait_ge(self._sem(k), n)
            if fn is None:
                continue
            ins = fn(e)
            if sem is not None:
                ins.then_inc(sem, inc)


def build_program(dbg=None, stop_after=None):
    nc = bass.Bass("TRN2", target_bir_lowering=False)
    dt = nc.dram_tensor
    x = dt("x", [S, D], F32, kind="ExternalInput").ap()
    meta = dt("meta", [NMETA, D], F32, kind="ExternalInput").ap()
    w_in = dt("w_in", [D, 9216], F32, kind="ExternalInput").ap()
    w_attn_o = dt("w_attn_o", [1024, D], F32, kind="ExternalInput").ap()
    w_conv_o = dt("w_conv_o", [1024, D], F32, kind="ExternalInput").ap()
    w_out = dt("w_out", [D, D], F32, kind="ExternalInput").ap()
    w_up = dt("w_up", [D, DFF], F32, kind="ExternalInput").ap()
    w_down = dt("w_down", [DFF, D], F32, kind="ExternalInput").ap()
    g1bc_d = dt("g1bc", [128, D], F32, kind="ExternalInput").ap()
    g2bc_d = dt("g2bc", [128, D], F32, kind="ExternalInput").ap()
    qkg_d = dt("qkg", [128, 256], F32, kind="ExternalInput").ap()
    sublng_d = dt("sublng", [128, 128], F32, kind="ExternalInput").ap()
    lamv_d = dt("lamv", [128, 256], F32, kind="ExternalInput").ap()
    cs_d = dt("cstab", [128, 17 * 32], F32, kind="ExternalInput").ap()
    dwk_d = dt("dwk", [128, 8 * KTAPS], F32, kind="ExternalInput").ap()
    cvec_d = dt("cvec", [128, 24], F32, kind="ExternalInput").ap()
    ident_d = dt("ident", [128, 128], BF16, kind="ExternalInput").ap()
    maskneg_d = dt("maskneg", [128, 128], BF16, kind="ExternalInput").ap()
    y = dt("y", [S, D], F32, kind="ExternalOutput").ap()
    h1s = dt("h1s", [S, D], F32, kind="ExternalOutput").ap()
    dbg_out = None
    if dbg is not None:
        dbg_out = dt("dbg", list(dbg[1]), dbg[2], kind="ExternalOutput").ap()

    import contextlib
    es = contextlib.ExitStack()
    with es:
        def sb(name, shape, dtype):
            return es.enter_context(nc.sbuf_tensor("sb_" + name, shape, dtype))

        def sem(name):
            return es.enter_context(nc.semaphore(name))

        uT = sb("uT", [128, 16, L], BF16)
        AR = sb("AR", [128, 32768], BF16)
        T0 = sb("T0", [128, 8192], BF16)
        WS = sb("WS", [128, 10240], BF16)
        WA = [sb("WA%d" % i, [128, 2064], F32) for i in range(2)]
        WBall = sb("WBall", [128, 4096], BF16)
        WB = [WBall[:, 0:2048], WBall[:, 2048:4096]]
        ident = sb("ident", [128, 128], BF16)
        maskneg = sb("maskneg", [128, 128], BF16)
        ones_bf = sb("ones_bf", [128, 128], BF16)
        gbc = sb("gbc", [128, D], F32)
        qkg = sb("qkg", [128, 256], F32)
        sublng = sb("sublng", [128, 128], F32)
        lamv = sb("lamv", [128, 256], F32)
        cstab = sb("cstab", [128, 17, 32], F32)
        dwk = sb("dwk", [128, 8, KTAPS], F32)
        cvec = sb("cvec", [128, 24], F32)
        small = sb("small", [128, 64], F32)
        mhalf = sb("mhalf", [128, 512], BF16)
        ssb = sb("ssb", [128, 48], F32)

        oT = AR[:, 0:16384].rearrange("p (c t) -> p c t", c=8)
        cT = AR[:, 16384:32768].rearrange("p (c t) -> p c t", c=8)
        ZT = AR[:, :].rearrange("p (f t) -> p f t", f=64)

        PS = es.enter_context(nc.psum_tensor("ps", [128, 4096], F32))
        PSB = PS[:].bitcast(BF16)

        def bank(i):
            return PS[:, 512 * i:512 * i + 512]

        def bank_bf(i):
            return PSB[:, 1024 * i:1024 * i + 1024]

        def PB(i):
            return ('ps', i)

        esems = {e: sem("s_" + e) for e in ['pe', 'act', 'dve', 'pool', 'sp']}
        dsems = {'sp': [sem("d_sp%d" % i) for i in range(12)], 'pool': [sem("d_pl%d" % i) for i in range(12)]}
        sc = Sched(nc, esems, dsems)
        dummy = sb("dummy", [128, 8], F32)

        def barrier(names):
            sc.op('pool', lambda e: e.memset(dummy[:, 0:2], 0.0), writes=list(names))
        w_in_v = w_in.rearrange("(c p) n -> p c n", p=128)
        w_ao_v = w_attn_o.rearrange("(c p) n -> p c n", p=128)
        w_co_v = w_conv_o.rearrange("(c p) n -> p c n", p=128)
        w_out_v = w_out.rearrange("(c p) n -> p c n", p=128)
        w_up_v = w_up.rearrange("(c p) n -> p c n", p=128)
        w_dn_v = w_down.rearrange("(c p) n -> p c n", p=128)

        WAr = [[('WA', i, k) for k in range(4)] for i in range(2)]
        WBr = [[('WB', i)] for i in range(2)]

        def ld(dst, src, res):
            sc.dma('sp', lambda e, d=dst, s=src: e.dma_start(out=d, in_=s), writes=[res])
        ld(ident[:], ident_d, 'ident')
        ld(maskneg[:], maskneg_d, 'maskneg')
        ld(gbc[:], g1bc_d, 'gbc')
        ld(qkg[:], qkg_d, 'qkg')
        ld(sublng[:], sublng_d, 'sublng')
        ld(lamv[:], lamv_d, 'lamv')
        ld(cstab[:].rearrange("p a b -> p (a b)"), cs_d, 'cstab')
        ld(dwk[:].rearrange("p a b -> p (a b)"), dwk_d, 'dwk')
        ld(cvec[:], cvec_d, 'cvec')
        sc.op('pool', lambda e: e.memset(mhalf[:], -0.5), writes=['mhalf'])
        sc.op('pool', lambda e: e.memset(ones_bf[:], 1.0), writes=['ones'])
        sc.op('dve', lambda e: e.tensor_scalar(out=dwk[:], in0=dwk[:], scalar1=0.5, scalar2=None, op0=ALU.mult),
              reads=['dwk'], writes=['dwk'])
        sc.op('dve', lambda e: e.tensor_scalar(out=cvec[:, 8:24], in0=cvec[:, 8:24], scalar1=0.5, scalar2=None,
                                               op0=ALU.mult), reads=['cvec'], writes=['cvec'])
        sc.op('dve', lambda e: e.tensor_scalar(out=sublng[:], in0=sublng[:], scalar1=1.0 - LAM_INIT, scalar2=None,
                                               op0=ALU.mult), reads=['sublng'], writes=['sublng'])
        sc.op('dve', lambda e: e.scalar_tensor_tensor(out=WB[0][:, 0:64], in0=lamv[:, 0:64], scalar=1.0, in1=lamv[:, 64:128],
                                                      op0=ALU.mult, op1=ALU.mult, accum_out=small[:, 0:1]),
              reads=['lamv'], writes=WBr[0] + ['sm0'])
        sc.op('dve', lambda e: e.scalar_tensor_tensor(out=WB[0][:, 0:64], in0=lamv[:, 128:192], scalar=1.0, in1=lamv[:, 192:256],
                                                      op0=ALU.mult, op1=ALU.mult, accum_out=small[:, 1:2]),
              reads=['lamv'], writes=WBr[0] + ['sm1'])
        sc.op('act', lambda e: e.activation(out=small[:, 2:4], in_=small[:, 0:2], func=AF.Exp),
              reads=['sm0', 'sm1'], writes=['sm23'])
        sc.op('dve', lambda e: e.tensor_tensor(out=small[:, 5:6], in0=small[:, 3:4], in1=small[:, 2:3],
                                               op=ALU.subtract), reads=['sm23'], writes=['sm5'])
        sc.op('dve', lambda e: e.tensor_scalar(out=small[:, 4:5], in0=small[:, 5:6], scalar1=-LAM_INIT,
                                               scalar2=None, op0=ALU.add), reads=['sm5'], writes=['neglam'])
        neglam = small[:, 4:5]

        def rstd_pool(io_ap, n, inv_n, res):
            rows = io_ap.shape[0]
            sc.op('pool', lambda e: e.tensor_scalar(out=io_ap, in0=io_ap, scalar1=inv_n, scalar2=EPS,
                                                    op0=ALU.mult, op1=ALU.add), reads=[res], writes=[res])
            sc.op('pool', lambda e: e.tensor_tensor(out=io_ap, in0=io_ap, in1=mhalf[:rows, 0:n], op=ALU.pow),
                  reads=[res, 'mhalf'], writes=[res])

        def norm_to_featmajor(xs, xs_res, rows, slot, colofs, dst_res, ssidx, have_ss=False):
            ssr = ('ss', ssidx)
            ss_ap = ssb[:rows, ssidx:ssidx + 1]
            if not have_ss:
                sc.op('act', lambda e: e.activation(out=WB[slot][:rows, :], in_=xs, func=AF.Square, accum_out=ss_ap),
                      reads=xs_res, writes=WBr[slot] + [ssr])
            rstd_pool(ss_ap, 1, 1.0 / D, ssr)
            xn = WB[slot]
            sc.op('dve', lambda e: e.scalar_tensor_tensor(out=xn[:rows, :], in0=xs, scalar=ss_ap,
                                                          in1=gbc[:rows, :], op0=ALU.mult, op1=ALU.mult),
                  reads=xs_res + [ssr, 'gbc'], writes=WBr[slot])
            pi = 3 - slot
            pt = PSB[:, 2048 * pi:2048 * pi + 2048]
            pres = [PB(2 * pi), PB(2 * pi + 1)]
            for c in range(16):
                sc.op('pe', lambda e, c=c: e.transpose(out=pt[:, c * 128:c * 128 + rows],
                                                       in_=xn[:rows, c * 128:(c + 1) * 128],
                                                       identity=ident[:rows, :rows]),
                      reads=WBr[slot] + ['ident'], writes=pres, signal=(c == 15))
            sc.op('act', lambda e: e.activation(out=uT[:, :, colofs:colofs + rows],
                                                in_=pt.rearrange("p (c t) -> p c t", c=16)[:, :, 0:rows],
                                                func=AF.Copy),
                  reads=pres, writes=[dst_res])

        def ublk(b):
            return [('uT', 4 * b + k) for k in range(4)]

        for j in range(NT + 1):
            slot = j % 2
            rows = 128 if j < NT else NMETA
            src = x[j * 128:(j + 1) * 128, :] if j < NT else meta
            colofs = NMETA + j * 128 if j < NT else 0
            sc.dma('sp', lambda e, s=src, sl=slot, r=rows: e.dma_start(out=WA[sl][:r, 0:D], in_=s),
                   writes=WAr[slot])
            norm_to_featmajor(WA[slot][:rows, 0:D], WAr[slot], rows, slot, colofs, ('uT', j), j)

        if dbg is not None and dbg[0] == 'uT':
            sc.dma('sp', lambda e: e.dma_start(out=dbg_out, in_=uT[:].rearrange("p a b -> p (a b)")),
                   reads=[('uT', j) for j in range(NT + 1)], is_out=True)
        if stop_after is None or stop_after > 1:
            sc.dma('sp', lambda e: e.dma_start(out=gbc[:], in_=g2bc_d), writes=['gbc'])

        TB = [(0, NMETA)] + [(NMETA + 512 * b, 512) for b in range(4)]
        TBres = [[('uT', 16)]] + [ublk(b) for b in range(4)]

        if stop_after is None or stop_after >= 2:
            aTp = [AR[:, 0:2096], AR[:, 2096:4192]]
            aTr = ['aT0', 'aT1']
            dg = WBall[:, 0:31 * 128].rearrange("p (j q) -> p j q", j=KTAPS)
            for k in range(2):
                sc.op('pool', lambda e, k=k: e.memset(aTp[k][:, 0:30], 0.0), writes=[aTr[k]])

            def emit_glu(cc):
                slot = cc % 2
                wg = T0[:, slot * 4096:(slot + 1) * 4096].rearrange("p (c w) -> p c w", c=16)
                wres = ('T0', slot)
                sc.dma('pool', lambda e: e.dma_start(out=wg[:, :, 0:128],
                                                     in_=w_in_v[:, :, 3072 + cc * 128:3072 + (cc + 1) * 128]), writes=[wres])
                sc.dma('pool', lambda e: e.dma_start(out=wg[:, :, 128:256],
                                                     in_=w_in_v[:, :, 4096 + cc * 128:4096 + (cc + 1) * 128]), writes=[wres])
                for bi, (c0, n) in enumerate(TB):
                    ba, bb = 2 * (bi % 2), 2 * (bi % 2) + 1
                    for (bk, wo) in ((ba, 0), (bb, 128)):
                        for c in range(16):
                            sc.op('pe', lambda e, c=c, bk=bk, wo=wo, c0=c0, n=n: e.matmul(
                                bank(bk)[:, 0:n], wg[:, c, wo:wo + 128], uT[:, c, c0:c0 + n],
                                start=(c == 0), stop=(c == 15)),
                                reads=[wres] + TBres[bi], writes=[PB(bk)], signal=(c == 15))
                    th = WA[0][:, 512 * (bi % 2):512 * (bi % 2) + 512]
                    thr = [WAr[0][bi % 2]]
                    sc.op('act', lambda e, th=th, bb=bb, n=n: e.activation(out=th[:, 0:n], in_=bank(bb)[:, 0:n],
                                                                        func=AF.Tanh, scale=0.5),
                          reads=[PB(bb)], writes=thr)
                    sc.op('dve', lambda e, th=th, ba=ba, n=n, c0=c0: e.scalar_tensor_tensor(
                        out=aTp[slot][:, 30 + c0:30 + c0 + n], in0=th[:, 0:n], scalar=1.0, in1=bank(ba)[:, 0:n],
                        op0=ALU.add, op1=ALU.mult),
                        reads=thr + [PB(ba)], writes=[aTr[slot]])

            def emit_conv(cc):
                slot = cc % 2
                sc.op('dve', lambda e: e.tensor_tensor(
                    out=dg, in0=ident[:, :].unsqueeze(1).to_broadcast([128, KTAPS, 128]),
                    in1=dwk[:, cc, :].unsqueeze(2).to_broadcast([128, KTAPS, 128]), op=ALU.mult),
                    reads=['ident', 'dwk'], writes=WBr[0] + WBr[1])
                for b in range(4):
                    bk = 4 + b % 2
                    for j in range(KTAPS):
                        c0 = NMETA + 512 * b + j
                        sc.op('pe', lambda e, j=j, c0=c0, bk=bk: e.matmul(
                            bank(bk), dg[:, j, :], aTp[slot][:, c0:c0 + 512], start=(j == 0), stop=(j == KTAPS - 1)),
                            reads=WBr[0] + WBr[1] + [aTr[slot]], writes=[PB(bk)], signal=(j == KTAPS - 1))
                    sc.op('dve', lambda e, b=b, bk=bk: e.tensor_scalar(
                        out=cT[:, cc, 512 * b:512 * b + 512], in0=bank(bk), scalar1=cvec[:, cc:cc + 1], scalar2=None,
                        op0=ALU.add), reads=[PB(bk), 'cvec'], writes=[('cT', cc)])

            for cc in range(9):
                if cc < 8:
                    emit_glu(cc)
                if cc >= 1:
                    emit_conv(cc - 1)
            sqb = T0[:, 0:4096].rearrange("p (c t) -> p c t", c=8)
            for b in range(4):
                blk = slice(512 * b, 512 * b + 512)
                sc.op('dve', lambda e, blk=blk: e.tensor_tensor(out=sqb, in0=cT[:, :, blk], in1=cT[:, :, blk],
                                                             op=ALU.mult),
                      reads=[('cT', c) for c in range(8)], writes=[('T0', 0)])
                for (bk, srcs) in ((6, 'c'), (7, 's')):
                    for c in range(8):
                        rhs = cT[:, c, blk] if srcs == 'c' else sqb[:, c, :]
                        sc.op('pe', lambda e, bk=bk, rhs=rhs, c=c: e.matmul(bank(bk), ones_bf[:, :], rhs,
                                                                         start=(c == 0), stop=(c == 7)),
                              reads=['ones', ('cT', c), ('T0', 0)], writes=[PB(bk)], signal=(c == 7))
                mean = WA[0][:, 0:512]
                msq = WA[0][:, 512:1024]
                rstd = WA[0][:, 1024:1536]
                sc.op('dve', lambda e: e.tensor_scalar(out=mean, in0=bank(6), scalar1=1.0 / CW, scalar2=None,
                                                       op0=ALU.mult), reads=[PB(6)], writes=[WAr[0][0]])
                sc.op('dve', lambda e: e.tensor_tensor(out=msq, in0=mean, in1=mean, op=ALU.mult),
                      reads=[WAr[0][0]], writes=[WAr[0][1]])
                sc.op('dve', lambda e: e.scalar_tensor_tensor(out=rstd, in0=bank(7), scalar=1.0 / CW, in1=msq,
                                                              op0=ALU.mult, op1=ALU.subtract),
                      reads=[PB(7), WAr[0][1]], writes=[WAr[0][2]])
                sc.op('pool', lambda e: e.tensor_scalar(out=rstd, in0=rstd, scalar1=EPS, scalar2=None, op0=ALU.add),
                      reads=[WAr[0][2]], writes=[WAr[0][2]])
                sc.op('pool', lambda e: e.tensor_tensor(out=rstd, in0=rstd, in1=mhalf[:, 0:512], op=ALU.pow),
                      reads=[WAr[0][2], 'mhalf'], writes=[WAr[0][2]])
                for cc in range(8):
                    k = cc % 2
                    t = WA[1][:, 1024 * k:1024 * k + 512]
                    th = WA[1][:, 1024 * k + 512:1024 * k + 1024]
                    tr = [WAr[1][2 * k]]
                    thr = [WAr[1][2 * k + 1]]
                    sc.op('dve', lambda e, t=t, cc=cc, blk=blk: e.tensor_tensor(out=t, in0=cT[:, cc, blk], in1=mean,
                                                                             op=ALU.subtract),
                          reads=[('cT', cc), WAr[0][0]], writes=tr)
                    sc.op('dve', lambda e, t=t: e.tensor_tensor(out=t, in0=t, in1=rstd, op=ALU.mult),
                          reads=tr + [WAr[0][2]], writes=tr)
                    sc.op('dve', lambda e, t=t, cc=cc: e.tensor_scalar(out=t, in0=t, scalar1=cvec[:, 8 + cc:9 + cc],
                                                                    scalar2=cvec[:, 16 + cc:17 + cc],
                                                                    op0=ALU.mult, op1=ALU.add),
                          reads=tr + ['cvec'], writes=tr)
                    sc.op('act', lambda e, t=t, th=th: e.activation(out=th, in_=t, func=AF.Tanh),
                          reads=tr, writes=thr)
                    sc.op('dve', lambda e, t=t, th=th, cc=cc, blk=blk: e.scalar_tensor_tensor(
                        out=cT[:, cc, blk], in0=th, scalar=1.0, in1=t, op0=ALU.add, op1=ALU.mult),
                        reads=tr + thr, writes=[('cT', cc)])
            if dbg is not None and dbg[0] == 'cT':
                sc.dma('sp', lambda e: e.dma_start(out=dbg_out, in_=AR[:, 16384:32768]),
                       reads=[('cT', c) for c in range(8)], is_out=True)

        if stop_after is None or stop_after >= 3:
            qTs = [T0[:, 0:2048], WS[:, 8192:10240]]
            kT = T0[:, 2048:2048 + L]
            VA0 = 2048 + L
            vaug = T0[:, VA0:VA0 + 17 * 130].rearrange("p (j v) -> p j v", j=17)
            PT0 = VA0 + 17 * 130
            PTs = [T0[:, PT0 + 512 * i:PT0 + 512 * (i + 1)] for i in range(3)]
            assert PT0 + 1536 <= 8192
            sc.op('pool', lambda e: e.memset(T0[:, VA0:VA0 + 17 * 130], 1.0),
                  writes=['vaug', 'qT', 'kT', ('T0', 0), ('T0', 1), 'aT0', 'aT1', ('WS', 2)] + [('PT', i) for i in range(3)]
                  + [('oT', c) for c in range(8)])
            sc.op('pool', lambda e: e.memset(qTs[0][64:128, :], 0.0), writes=['qT'])
            sc.op('pool', lambda e: e.memset(qTs[1][0:64, :], 0.0), writes=['qT'])
            def attn_sweep(h):
                steps = []
                for g2 in range(NT // 2 if DBG_GROUPS is None else DBG_GROUPS):
                    j1 = 2 * g2 + 1
                    kts = ['m'] + list(range(j1 + 1))
                    for kt in kts:
                        steps.append((g2, kt, kt == kts[-1]))

                def geom(i):
                    g2, kt, last = steps[i]
                    j0, j1 = 2 * g2, 2 * g2 + 1
                    krows = NMETA if kt == 'm' else 128
                    kc0 = 0 if kt == 'm' else NMETA + 128 * kt
                    kidx = 16 if kt == 'm' else kt
                    both = (kt == 'm') or (kt <= j0)
                    qc0 = 128 * j0 if both else 128 * j1
                    N = 256 if both else 128
                    diag = (kt != 'm') and (kt == j0 or kt == j1)
                    return g2, kt, last, j0, j1, krows, kc0, kidx, both, qc0, N, diag

                def emit_S(i):
                    g2, kt, last, j0, j1, krows, kc0, kidx, both, qc0, N, diag = geom(i)
                    sbk = 4 + i % 3
                    sbank = bank(sbk)
                    for c in range(2):
                        sc.op('pe', lambda e, c=c: e.matmul(
                            sbank[:krows, c * 256:c * 256 + N], kT[:, kc0:kc0 + krows],
                            qTs[c][:, qc0:qc0 + N], start=True, stop=(not diag)),
                            reads=['kT', 'qT'], writes=[PB(sbk)], signal=(c == 1 and not diag))
                        if diag:
                            sc.op('pe', lambda e, c=c: e.matmul(
                                sbank[:, c * 256:c * 256 + 128], ident[:, :], maskneg[:, :], start=False, stop=True),
                                reads=['ident', 'maskneg'], writes=[PB(sbk)], signal=(c == 1))

                def emit_EP(i):
                    g2, kt, last, j0, j1, krows, kc0, kidx, both, qc0, N, diag = geom(i)
                    sbk = 4 + i % 3
                    sbank = bank(sbk)
                    PT = PTs[i % 3]
                    ptr = ('PT', i % 3)
                    sc.op('act', lambda e: e.activation(
                        out=PT[:krows, :].rearrange("p (c n) -> p c n", c=2)[:, :, 0:N],
                        in_=sbank[:krows, :].rearrange("p (c n) -> p c n", c=2)[:, :, 0:N],
                        func=AF.Exp, scale=0.125),
                        reads=[PB(sbk)], writes=[ptr])
                    for c in range(2):
                        qts = (j0, j1) if both else (j1,)
                        for qt in qts:
                            ab = 2 * (qt - j0) + c
                            po = c * 256 + ((qt - j0) * 128 if both else 0)
                            sc.op('pe', lambda e, ab=ab, po=po, qt=qt: e.matmul(
                                bank(ab)[:, 0:129], PT[:krows, po:po + 128], vaug[:krows, kidx, 0:129],
                                start=(kt == 'm'), stop=(kt == qt)),
                                reads=[ptr, 'vaug'], writes=[PB(ab)], signal=(kt == qt or (c == 1 and qt == qts[-1])))
                    if last:
                        for qt in (j0, j1):
                            post_one(qt, j0)

                def post_one(qt, j0):
                    a0, a1 = 2 * (qt - j0), 2 * (qt - j0) + 1
                    k = qt % 2
                    sm = small[:, 16 + 8 * k:24 + 8 * k]
                    smr = ('smq', k)
                    o1 = WA[k][:, 1536:1664]
                    o2 = WA[k][:, 1664:1792]
                    onb = WA[k][:, 1792:1856].bitcast(BF16)
                    wr = [WAr[k][3]]
                    sc.op('dve', lambda e: e.reciprocal(out=sm[:, 0:1], in_=bank(a0)[:, 128:129]),
                          reads=[PB(a0)], writes=[smr])
                    sc.op('dve', lambda e: e.reciprocal(out=sm[:, 1:2], in_=bank(a1)[:, 128:129]),
                          reads=[PB(a1), smr], writes=[smr])
                    sc.op('dve', lambda e: e.tensor_tensor(out=sm[:, 2:3], in0=sm[:, 1:2], in1=neglam, op=ALU.mult),
                          reads=[smr, 'neglam'], writes=[smr])
                    sc.op('dve', lambda e: e.tensor_scalar(out=o1, in0=bank(a0)[:, 0:128],
                                                           scalar1=sm[:, 0:1], scalar2=None, op0=ALU.mult),
                          reads=[PB(a0), smr], writes=wr)
                    sc.op('dve', lambda e: e.scalar_tensor_tensor(
                        out=o1, in0=bank(a1)[:, 0:128], scalar=sm[:, 2:3], in1=o1, op0=ALU.mult, op1=ALU.add),
                        reads=[PB(a1), smr] + wr, writes=wr)
                    sc.op('dve', lambda e: e.scalar_tensor_tensor(
                        out=o2, in0=o1, scalar=1.0, in1=o1, op0=ALU.mult, op1=ALU.mult, accum_out=sm[:, 3:4]),
                        reads=wr, writes=wr + [('smr', k)])
                    rstd_pool(sm[:, 3:4], 1, 1.0 / 128, ('smr', k))
                    sc.op('dve', lambda e: e.scalar_tensor_tensor(
                        out=onb, in0=o1, scalar=sm[:, 3:4], in1=sublng[:, :], op0=ALU.mult, op1=ALU.mult),
                        reads=wr + [('smr', k), 'sublng'], writes=wr)
                    ptb = bank_bf(7)
                    sc.op('pe', lambda e: e.transpose(out=ptb[:, 0:128], in_=onb, identity=ident[:, :]),
                          reads=wr + ['ident'], writes=[PB(7)])
                    sc.op('act', lambda e: e.activation(out=oT[:, h, 128 * qt:128 * qt + 128],
                                                        in_=ptb[:, 0:128], func=AF.Copy),
                          reads=[PB(7)], writes=[('oT', h)])

                LOOK = DBG_LOOK
                nst = len(steps)
                for i in range(min(LOOK, nst)):
                    emit_S(i)
                for i in range(nst):
                    if i + LOOK < nst:
                        emit_S(i + LOOK)
                    emit_EP(i)

            for h in range(H if DBG_HEADS is None else DBG_HEADS):
                slot = h % 2
                wqk = WS[:, slot * 4096:(slot + 1) * 4096].rearrange("p (c w) -> p c w", c=16)
                wv = WB[slot][:, :].rearrange("p (c w) -> p c w", c=16)
                wqk_r = ('WS', slot)
                sc.dma('pool', lambda e, wqk=wqk, h=h: e.dma_start(out=wqk[:, :, 0:128],
                                                                 in_=w_in_v[:, :, h * 128:(h + 1) * 128]), writes=[wqk_r])
                sc.dma('pool', lambda e, wqk=wqk, h=h: e.dma_start(out=wqk[:, :, 128:256],
                                                                 in_=w_in_v[:, :, 1024 + h * 128:1024 + (h + 1) * 128]),
                       writes=[wqk_r])
                sc.dma('pool', lambda e, wv=wv, h=h: e.dma_start(out=wv, in_=w_in_v[:, :, 2048 + h * 128:2048 + (h + 1) * 128]),
                       writes=WBr[slot])
                for j in range(NT + 1):
                    rows = 128 if j < NT else NMETA
                    colofs = NMETA + j * 128 if j < NT else 0
                    pj = j % 2
                    pr = bank(pj)
                    for c in range(16):
                        sc.op('pe', lambda e, c=c, pr=pr, rows=rows, colofs=colofs, wqk=wqk: e.matmul(
                            pr[:rows, 0:256], uT[:, c, colofs:colofs + rows], wqk[:, c, :], start=(c == 0), stop=(c == 15)),
                            reads=[('uT', j), wqk_r], writes=[PB(pj)], signal=(c == 15))
                    for c in range(16):
                        sc.op('pe', lambda e, c=c, pr=pr, rows=rows, colofs=colofs, wv=wv: e.matmul(
                            pr[:rows, 256:384], uT[:, c, colofs:colofs + rows], wv[:, c, :], start=(c == 0), stop=(c == 15)),
                            reads=[('uT', j)] + WBr[slot], writes=[PB(pj)], signal=(c == 15))
                    wa = WA[pj]
                    sq = wa[:rows, 0:256]
                    qn = wa[:rows, 256:512]
                    ra = wa[:rows, 512:576].rearrange("p (g d) -> p g d", g=4)
                    rb = wa[:rows, 576:640].rearrange("p (g d) -> p g d", g=4)
                    s4 = ssb[:rows, 24 + 4 * pj:28 + 4 * pj]
                    s4r = ('s4', pj)
                    war = [WAr[pj][0], WAr[pj][1]]
                    sc.op('act', lambda e, sq=sq, pr=pr, rows=rows: e.activation(out=sq, in_=pr[:rows, 0:256], func=AF.Square),
                          reads=[PB(pj)], writes=war)
                    sc.op('dve', lambda e, sq=sq, s4=s4: e.tensor_reduce(out=s4, in_=sq.rearrange("p (g d) -> p g d", g=4),
                                                                      axis=mybir.AxisListType.X, op=ALU.add),
                          reads=war, writes=[s4r])
                    rstd_pool(s4, 4, 1.0 / 64, s4r)
                    sc.op('dve', lambda e, qn=qn, pr=pr, rows=rows, s4=s4: e.tensor_tensor(
                        out=qn.rearrange("p (g d) -> p g d", g=4),
                        in0=pr[:rows, 0:256].rearrange("p (g d) -> p g d", g=4),
                        in1=s4.unsqueeze(2).to_broadcast([rows, 4, 64]), op=ALU.mult),
                        reads=[PB(pj), s4r], writes=war)
                    sc.op('dve', lambda e, qn=qn, rows=rows: e.tensor_tensor(out=qn, in0=qn, in1=qkg[:rows, :], op=ALU.mult),
                          reads=war + ['qkg'], writes=war)
                    q3 = qn.rearrange("p (g d) -> p g d", g=4)
                    cs = cstab[:rows, j, :]
                    sc.op('dve', lambda e, q3=q3, ra=ra, cs=cs, rows=rows: e.tensor_tensor(
                        out=ra, in0=q3[:, :, 0:16], in1=cs[:, 0:16].unsqueeze(1).to_broadcast([rows, 4, 16]), op=ALU.mult),
                        reads=war + ['cstab'], writes=war)
                    sc.op('dve', lambda e, q3=q3, rb=rb, cs=cs, rows=rows: e.tensor_tensor(
                        out=rb[:, :, 0:8], in0=q3[:, :, 8:16], in1=cs[:, 16:24].unsqueeze(1).to_broadcast([rows, 4, 8]),
                        op=ALU.mult), reads=war + ['cstab'], writes=war)
                    sc.op('dve', lambda e, q3=q3, rb=rb, cs=cs, rows=rows: e.tensor_tensor(
                        out=rb[:, :, 8:16], in0=q3[:, :, 0:8], in1=cs[:, 24:32].unsqueeze(1).to_broadcast([rows, 4, 8]),
                        op=ALU.mult), reads=war + ['cstab'], writes=war)
                    sc.op('dve', lambda e, q3=q3, ra=ra, rb=rb: e.tensor_tensor(out=q3[:, :, 0:16], in0=ra, in1=rb, op=ALU.add),
                          reads=war, writes=war)
                    qkb = WA[pj][:, 1024:1152].bitcast(BF16)
                    qkr = [WAr[pj][2]]
                    sc.op('act', lambda e, qkb=qkb, qn=qn, rows=rows: e.activation(out=qkb[:rows, :], in_=qn, func=AF.Copy),
                          reads=war, writes=qkr)
                    sc.op('act', lambda e, pr=pr, rows=rows, j=j: e.activation(out=vaug[:rows, j, 0:128],
                                                                            in_=pr[:rows, 256:384], func=AF.Copy),
                          reads=[PB(pj)], writes=['vaug'])
                    tb = 2 + pj
                    ptb = bank_bf(tb)
                    sc.op('pe', lambda e, ptb=ptb, qkb=qkb, rows=rows: e.transpose(out=ptb[:, 0:rows], in_=qkb[:rows, 0:128],
                                                                                identity=ident[:rows, :rows]),
                          reads=qkr + ['ident'], writes=[PB(tb)], signal=False)
                    sc.op('pe', lambda e, ptb=ptb, qkb=qkb, rows=rows: e.transpose(out=ptb[:, 128:128 + rows],
                                                                                in_=qkb[:rows, 128:256],
                                                                                identity=ident[:rows, :rows]),
                          reads=qkr + ['ident'], writes=[PB(tb)], signal=True)
                    if j < NT:
                        for c in range(2):
                            sc.op('act', lambda e, ptb=ptb, j=j, c=c: e.activation(
                                out=qTs[c][64 * c:64 * c + 64, 128 * j:128 * j + 128], in_=ptb[64 * c:64 * c + 64, 0:128],
                                func=AF.Copy), reads=[PB(tb)], writes=['qT'])
                    sc.op('dve', lambda e, ptb=ptb, rows=rows, colofs=colofs: e.tensor_copy(
                        out=kT[:, colofs:colofs + rows], in_=ptb[:, 128:128 + rows]),
                        reads=[PB(tb)], writes=['kT'])
                attn_sweep(h)
            if dbg is not None and dbg[0] == 'oT':
                sc.dma('sp', lambda e: e.dma_start(out=dbg_out, in_=AR[:, 0:16384]),
                       reads=[('oT', c) for c in range(8)], is_out=True)

        if stop_after is None or stop_after >= 4:
            mT = T0[:, :].rearrange("p (c t) -> p c t", c=16)
            barrier(['vaug', 'qT', 'kT', ('WS', 2)] + [('PT', i) for i in range(3)] + [('mT', fc) for fc in range(16)])
            it = 0
            for b in range(4):
                ucol = NMETA + 512 * b
                blk = slice(512 * b, 512 * b + 512)
                for fc in range(16):
                    slot = it % 2
                    wg = WS[:, slot * 4096:(slot + 1) * 4096].rearrange("p (c w) -> p c w", c=16)
                    wo = WS[:, 8192:10240].rearrange("p (c w) -> p c w", c=16)
                    wgr = ('WS', slot)
                    wor = ('WS', 2)
                    sc.dma('pool', lambda e, wg=wg, fc=fc: e.dma_start(
                        out=wg[:, :, 0:128], in_=w_in_v[:, :, 5120 + fc * 128:5120 + (fc + 1) * 128]), writes=[wgr])
                    sc.dma('pool', lambda e, wg=wg, fc=fc: e.dma_start(
                        out=wg[:, :, 128:256], in_=w_in_v[:, :, 7168 + fc * 128:7168 + (fc + 1) * 128]), writes=[wgr])
                    sc.dma('pool', lambda e, wo=wo, fc=fc: e.dma_start(
                        out=wo[:, 0:8, :], in_=w_ao_v[:, :, fc * 128:(fc + 1) * 128]), writes=[wor])
                    sc.dma('pool', lambda e, wo=wo, fc=fc: e.dma_start(
                        out=wo[:, 8:16, :], in_=w_co_v[:, :, fc * 128:(fc + 1) * 128]), writes=[wor])
                    bs = 4 * (it % 2)
                    for c in range(16):
                        sc.op('pe', lambda e, c=c, bs=bs, wg=wg, ucol=ucol: e.matmul(
                            bank(bs), wg[:, c, 0:128], uT[:, c, ucol:ucol + 512], start=(c == 0), stop=(c == 15)),
                            reads=[wgr] + ublk(b), writes=[PB(bs)], signal=(c == 15))
                    for c in range(16):
                        sc.op('pe', lambda e, c=c, bs=bs, wg=wg, ucol=ucol: e.matmul(
                            bank(bs + 1), wg[:, c, 128:256], uT[:, c, ucol:ucol + 512], start=(c == 0), stop=(c == 15)),
                            reads=[wgr] + ublk(b), writes=[PB(bs + 1)], signal=(c == 15))
                    for c in range(8):
                        sc.op('pe', lambda e, c=c, bs=bs, wo=wo, blk=blk: e.matmul(
                            bank(bs + 2), wo[:, c, :], oT[:, c, blk], start=(c == 0), stop=(c == 7)),
                            reads=[wor] + [('oT', c)], writes=[PB(bs + 2)], signal=(c == 7))
                    for c in range(8):
                        sc.op('pe', lambda e, c=c, bs=bs, wo=wo, blk=blk: e.matmul(
                            bank(bs + 3), wo[:, 8 + c, :], cT[:, c, blk], start=(c == 0), stop=(c == 7)),
                            reads=[wor] + [('cT', c)], writes=[PB(bs + 3)], signal=(c == 7))
                    k = it % 2
                    ta = WA[1][:, 1024 * k:1024 * k + 512]
                    tcn = WA[1][:, 1024 * k + 512:1024 * k + 1024]
                    tar = [WAr[1][2 * k]]
                    tcr = [WAr[1][2 * k + 1]]
                    sc.op('act', lambda e, ta=ta, bs=bs: e.activation(out=ta, in_=bank(bs), func=AF.Tanh, scale=0.5),
                          reads=[PB(bs)], writes=tar)
                    sc.op('act', lambda e, tcn=tcn, bs=bs: e.activation(out=tcn, in_=bank(bs + 1), func=AF.Tanh, scale=0.5),
                          reads=[PB(bs + 1)], writes=tcr)
                    sc.op('dve', lambda e, ta=ta, bs=bs: e.scalar_tensor_tensor(
                        out=ta, in0=ta, scalar=1.0, in1=bank(bs + 2), op0=ALU.add, op1=ALU.mult),
                        reads=tar + [PB(bs + 2)], writes=tar)
                    sc.op('dve', lambda e, tcn=tcn, bs=bs: e.scalar_tensor_tensor(
                        out=tcn, in0=tcn, scalar=1.0, in1=bank(bs + 3), op0=ALU.add, op1=ALU.mult),
                        reads=tcr + [PB(bs + 3)], writes=tcr)
                    sc.op('dve', lambda e, ta=ta, tcn=tcn, fc=fc: e.tensor_tensor(out=mT[:, fc, :], in0=ta, in1=tcn, op=ALU.add),
                          reads=tar + tcr, writes=[('mT', fc)])
                    it += 1
                wslots = [WS[:, 4096 + 1024 * i:4096 + 1024 * (i + 1)] for i in range(4)]
                wsr = [('WSq', i) for i in range(4)]
                barrier([('WS', 1)] + wsr)
                di = 0
                for cg in range(4):
                    bs = 4 * (cg % 2)
                    xps = []
                    for t in range(4):
                        tile_i = 4 * b + t
                        xp = WA[cg % 2][:, 512 * t:512 * t + 512]
                        xpr = [WAr[cg % 2][t]]
                        rs, re = tile_i * 128, tile_i * 128 + 128
                        sc.dma('sp', lambda e, xp=xp, rs=rs, re=re, cg=cg: e.dma_start(
                            out=xp, in_=x[rs:re, cg * 512:(cg + 1) * 512]), writes=xpr)
                        xps.append((xp, xpr, rs, re, tile_i))
                    for fp in range(8):
                        sl = di % 4
                        wsl = wslots[sl].rearrange("p (k n) -> p k n", k=2)
                        wres = [wsr[sl]]
                        sc.dma('pool', lambda e, wsl=wsl, fp=fp, cg=cg: e.dma_start(
                            out=wsl, in_=w_out_v[:, 2 * fp:2 * fp + 2, cg * 512:(cg + 1) * 512]), writes=wres)
                        for k in range(2):
                            fc = 2 * fp + k
                            for t in range(4):
                                sc.op('pe', lambda e, t=t, bs=bs, fc=fc, wsl=wsl, k=k: e.matmul(
                                    bank(bs + t), mT[:, fc, t * 128:(t + 1) * 128], wsl[:, k, :],
                                    start=(fc == 0), stop=(fc == 15)),
                                    reads=[('mT', fc)] + wres, writes=[PB(bs + t)], signal=(fc == 15 or (k == 1 and t == 3)))
                        di += 1
                    for t in range(4):
                        xp, xpr, rs, re, tile_i = xps[t]
                        sc.op('dve', lambda e, xp=xp, bs=bs, t=t: e.scalar_tensor_tensor(
                            out=xp, in0=bank(bs + t), scalar=0.5, in1=xp, op0=ALU.mult, op1=ALU.add),
                            reads=[PB(bs + t)] + xpr, writes=xpr)
                        sc.op('act', lambda e, xp=xp, t=t, cg=cg: e.activation(
                            out=WB[0][:, 512 * (t % 2):512 * (t % 2) + 512], in_=xp, func=AF.Square,
                            accum_out=ssb[:, 4 * t + cg:4 * t + cg + 1]),
                            reads=xpr, writes=WBr[0] + [('ssq', t)])
                        sc.dma('sp', lambda e, xp=xp, rs=rs, re=re, cg=cg: e.dma_start(
                            out=h1s[rs:re, cg * 512:(cg + 1) * 512], in_=xp), reads=xpr, writes=[('h1s', tile_i)])
                barrier([('WS', 1)] + wsr)
                for t in range(4):
                    tile_i = 4 * b + t
                    rs, re = tile_i * 128, tile_i * 128 + 128
                    slot = t % 2
                    sc.op('dve', lambda e, t=t: e.tensor_reduce(out=ssb[:, 16 + t:17 + t], in_=ssb[:, 4 * t:4 * t + 4],
                                                              axis=mybir.AxisListType.X, op=ALU.add),
                          reads=[('ssq', t)], writes=[('ss', 16 + t)])
                    sc.dma('sp', lambda e, slot=slot, rs=rs, re=re: e.dma_start(out=WA[slot][:, 0:D], in_=h1s[rs:re, :]),
                           reads=[('h1s', tile_i)], writes=WAr[slot])
                    norm_to_featmajor(WA[slot][:, 0:D], WAr[slot], 128, 0, NMETA + 128 * tile_i, ('uT', tile_i), 16 + t,
                                      have_ss=True)
            if dbg is not None and dbg[0] == 'h1':
                sc.dma('sp', lambda e: e.dma_start(out=dbg_out, in_=h1s), reads=[('h1s', i) for i in range(16)], is_out=True)
            if dbg is not None and dbg[0] == 'u2T':
                sc.dma('sp', lambda e: e.dma_start(out=dbg_out, in_=uT[:].rearrange("p a b -> p (a b)")),
                       reads=[('uT', j) for j in range(NT + 1)], is_out=True)

        if stop_after is None or stop_after >= 5:
            ui = 0
            barrier([('mT', fc) for fc in range(16)] + [('T0d', i) for i in range(4)] + WBr[1] + [('WB1q', q) for q in range(4)]
                    + [('WS', 1)] + [('WSq', i) for i in range(4)])
            for b in range(4):
                ucol = NMETA + 512 * b
                for fp in range(32):
                    slot = ui % 3
                    wu = (WS[:, slot * 4096:(slot + 1) * 4096] if slot < 2 else gbc[:].bitcast(BF16)).rearrange(
                        "p (c w) -> p c w", c=16)
                    wur = ('WS', slot) if slot < 2 else 'gbc'
                    sc.dma('pool', lambda e, wu=wu, fp=fp: e.dma_start(out=wu, in_=w_up_v[:, :, fp * 256:(fp + 1) * 256]),
                           writes=[wur])
                    for k in range(2):
                        f = 2 * fp + k
                        ub = f % 4
                        for c in range(16):
                            sc.op('pe', lambda e, c=c, ub=ub, wu=wu, k=k, ucol=ucol: e.matmul(
                                bank(ub), wu[:, c, k * 128:(k + 1) * 128], uT[:, c, ucol:ucol + 512],
                                start=(c == 0), stop=(c == 15)),
                                reads=[wur] + ublk(b), writes=[PB(ub)], signal=(c == 15))
                        r = WB[1][:, 512 * (f % 4):512 * (f % 4) + 512]
                        rr = [('WB1q', f % 4)]
                        sc.op('act', lambda e, r=r, ub=ub: e.activation(out=r, in_=bank(ub), func=AF.Relu),
                              reads=[PB(ub)], writes=rr)
                        sc.op('dve', lambda e, r=r, f=f: e.tensor_tensor(out=ZT[:, f, :], in0=r, in1=r, op=ALU.mult),
                              reads=rr, writes=[('ZT', f)])
                    ui += 1
                di = 0
                for cg in range(4):
                    bs = 4 * (cg % 2)
                    hps = []
                    for t in range(4):
                        tile_i = 4 * b + t
                        rs, re = tile_i * 128, tile_i * 128 + 128
                        hp = WA[cg % 2][:, 512 * t:512 * t + 512]
                        hpr = [WAr[cg % 2][t]]
                        sc.dma('sp', lambda e, hp=hp, rs=rs, re=re, cg=cg: e.dma_start(
                            out=hp, in_=h1s[rs:re, cg * 512:(cg + 1) * 512]), reads=[('h1s', tile_i)], writes=hpr)
                        hps.append((hp, hpr, rs, re, tile_i))
                    for fq in range(16):
                        sl = di % 4
                        wd = T0[:, sl * 2048:(sl + 1) * 2048].rearrange("p (k n) -> p k n", k=4)
                        wdr = ('T0d', sl)
                        sc.dma('pool', lambda e, wd=wd, fq=fq, cg=cg: e.dma_start(
                            out=wd, in_=w_dn_v[:, 4 * fq:4 * fq + 4, cg * 512:(cg + 1) * 512]), writes=[wdr])
                        for k in range(4):
                            f = 4 * fq + k
                            for t in range(4):
                                sc.op('pe', lambda e, t=t, bs=bs, f=f, wd=wd, k=k: e.matmul(
                                    bank(bs + t), ZT[:, f, t * 128:(t + 1) * 128], wd[:, k, :],
                                    start=(f == 0), stop=(f == 63)),
                                    reads=[('ZT', f), wdr], writes=[PB(bs + t)], signal=(f == 63 or (k == 3 and t == 3)))
                        di += 1
                    for t in range(4):
                        hp, hpr, rs, re, tile_i = hps[t]
                        sc.op('dve', lambda e, hp=hp, bs=bs, t=t: e.tensor_tensor(out=hp, in0=bank(bs + t), in1=hp, op=ALU.add),
                              reads=[PB(bs + t)] + hpr, writes=hpr)
                        sc.dma('sp', lambda e, hp=hp, rs=rs, re=re, cg=cg: e.dma_start(
                            out=y[rs:re, cg * 512:(cg + 1) * 512], in_=hp), reads=hpr, writes=[('y', tile_i, cg)], is_out=True)

        sc.finish('sp')

        with nc.Block() as block:
            @block.sync
            def _(e):
                sc.replay('sp', e)

            @block.gpsimd
            def _(e):
                sc.replay('pool', e)

            @block.scalar
            def _(e):
                sc.replay('act', e)

            @block.vector
            def _(e):
                sc.replay('dve', e)

            @block.tensor
            def _(e):
                sc.replay('pe', e)
    return nc


def _bf16(a):
    return np.asarray(a, dtype=np.float32).astype(ml_dtypes.bfloat16)


def make_consts(inputs):
    f = np.float32
    c = {}
    c["g1bc"] = np.ascontiguousarray(np.broadcast_to(np.asarray(inputs["norm1_g"], f)[0][None, :], (128, D)))
    c["g2bc"] = np.ascontiguousarray(np.broadcast_to(np.asarray(inputs["norm2_g"], f)[0][None, :], (128, D)))
    qg = np.asarray(inputs["q_norm_g"], f)[0]
    kg = np.asarray(inputs["k_norm_g"], f)[0]
    c["qkg"] = np.ascontiguousarray(np.broadcast_to(np.concatenate([qg, qg, kg, kg])[None, :], (128, 256)))
    c["sublng"] = np.ascontiguousarray(np.broadcast_to(np.asarray(inputs["subln_g"], f)[0][None, :], (128, 128)))
    lv = np.concatenate([np.asarray(inputs[k], f)[0] for k in ("lambda_q1", "lambda_k1", "lambda_q2", "lambda_k2")])
    c["lamv"] = np.ascontiguousarray(np.broadcast_to(lv[None, :], (128, 256)))
    inv_freq = (np.float32(500000.0) ** (-np.arange(0, 16, 2, dtype=f) / np.float32(16))).astype(f)
    cs = np.zeros((128, 17, 32), f)
    for j in range(17):
        pos = (np.arange(128) + (NMETA + 128 * j if j < 16 else 0)).astype(f)
        ang = (pos[:, None] * inv_freq[None, :]).astype(f)
        co, si = np.cos(ang).astype(f), np.sin(ang).astype(f)
        cs[:, j, 0:8] = co
        cs[:, j, 8:16] = co
        cs[:, j, 16:24] = -si
        cs[:, j, 24:32] = si
    c["cstab"] = cs.reshape(128, 17 * 32)
    dk = np.asarray(inputs["dw_kernel"], f)[0]
    c["dwk"] = np.ascontiguousarray(dk.T.reshape(8, 128, KTAPS).transpose(1, 0, 2)).reshape(128, 8 * KTAPS)
    cv = np.zeros((128, 24), f)
    cv[:, 0:8] = np.asarray(inputs["dw_bias"], f)[0].reshape(8, 128).T
    cv[:, 8:16] = np.asarray(inputs["conv_ln_g"], f)[0].reshape(8, 128).T
    cv[:, 16:24] = np.asarray(inputs["conv_ln_b"], f)[0].reshape(8, 128).T
    c["cvec"] = cv
    c["ident"] = _bf16(np.eye(128))
    kk = np.arange(128)
    c["maskneg"] = _bf16(np.where(kk[:, None] > kk[None, :], NEG, 0.0))
    return c


_NC_CACHE = {}


def kernel(**inputs):
    f = np.float32
    consts = make_consts(inputs)
    shared = dict(consts)
    shared["meta"] = np.ascontiguousarray(np.asarray(inputs["meta"], f))
    shared["w_in"] = np.ascontiguousarray(np.asarray(inputs["w_in"], f)[0])
    shared["w_attn_o"] = np.ascontiguousarray(np.asarray(inputs["w_attn_o"], f)[0])
    shared["w_conv_o"] = np.ascontiguousarray(np.asarray(inputs["w_conv_o"], f)[0])
    shared["w_out"] = np.ascontiguousarray(np.asarray(inputs["w_out"], f)[0])
    shared["w_up"] = np.ascontiguousarray(np.asarray(inputs["w_up"], f)[0])
    shared["w_down"] = np.ascontiguousarray(np.asarray(inputs["w_down"], f)[0])
    xin = np.asarray(inputs["x"], f)
    if "nc" not in _NC_CACHE:
        _NC_CACHE["nc"] = build_program()
    nc = _NC_CACHE["nc"]
    in_maps = []
    for b in range(8):
        m = dict(shared)
        m["x"] = np.ascontiguousarray(xin[b])
        in_maps.append(m)
    res = run_bass_kernel_spmd(nc, in_maps, core_ids=list(range(8)))
    out = np.stack([np.asarray(r["y"], f) for r in res.results], axis=0)
    return out
```

```python
import math
import numpy as np
import ml_dtypes
import concourse.bass as bass
import concourse.mybir as mybir
from concourse.bass_utils import run_bass_kernel_spmd
from concourse.alu_op_type import AluOpType as ALU

F32 = mybir.dt.float32
BF16 = mybir.dt.bfloat16
AF = mybir.ActivationFunctionType

D = 2048
S = 2048
NMETA = 16
L = S + NMETA
NT = S // 128
H = 8
DFF = 4 * D
CW = 1024
KTAPS = 31
EPS = 1e-6
LAM_INIT = 0.8 - 0.6 * math.exp(0.0)
NEG = -30000.0
DBG_HEADS = None
DBG_GROUPS = None
DBG_CUT = 99
DBG_LOOK = 2
DBG_H0 = 0


class Sched:
    def __init__(self, nc, esems, dsems):
        self.nc = nc
        self.esem = esems
        self.dsem = dsems
        self.cnt = {e: 0 for e in esems}
        self.ops = {e: [] for e in esems}
        self.seen = {e: {} for e in esems}
        self.lastw = {}
        self.readers = {}
        self.dcnt = {q: [0] * len(dsems[q]) for q in dsems}
        self.dnext = {q: 0 for q in dsems}
        self.out_handles = []

    def _sem(self, k):
        return self.esem[k] if isinstance(k, str) else self.dsem[k[1]][k[2]]

    def _collect(self, eng, reads, writes, extra):
        deps = list(extra)
        for r in reads:
            h = self.lastw.get(r)
            if h is not None:
                deps.append(h)
        for w in writes:
            h = self.lastw.get(w)
            if h is not None:
                deps.append(h)
            deps.extend(self.readers.get(w, {}).items())
        waits = {}
        for (k, n) in deps:
            if k == eng and eng == 'pe':
                continue
            if self.seen[eng].get(k, 0) >= n:
                continue
            if waits.get(k, 0) < n:
                waits[k] = n
        for k, n in waits.items():
            self.seen[eng][k] = n
        return list(waits.items())

    def _record(self, h, reads, writes):
        for r in reads:
            d = self.readers.setdefault(r, {})
            if d.get(h[0], 0) < h[1]:
                d[h[0]] = h[1]
        for w in writes:
            self.lastw[w] = h
            self.readers[w] = {}

    def op(self, eng, fn, reads=(), writes=(), signal=True):
        waits = self._collect(eng, reads, writes, ())
        if eng != 'pe':
            signal = True
        h = (eng, self.cnt[eng] + 1)
        if signal:
            self.cnt[eng] += 1
        self.ops[eng].append((waits, fn, self.esem[eng] if signal else None, 1))
        self._record(h, reads, writes)
        return h

    def dma(self, q, fn, reads=(), writes=(), is_out=False):
        i = self.dnext[q]
        self.dnext[q] = (i + 1) % len(self.dsem[q])
        key = ('d', q, i)
        extra = []
        if self.dcnt[q][i] > 0:
            extra.append((key, self.dcnt[q][i]))
        waits = self._collect(q, reads, writes, extra)
        self.dcnt[q][i] += 16
        h = (key, self.dcnt[q][i])
        self.ops[q].append((waits, fn, self.dsem[q][i], 16))
        self._record(h, reads, writes)
        if is_out:
            self.out_handles.append(h)
        return h

    def finish(self, eng='sp'):
        waits = {}
        for (k, n) in self.out_handles:
            if waits.get(k, 0) < n:
                waits[k] = n
        self.ops[eng].append((list(waits.items()), None, None, 0))

    def replay(self, eng, e):
        for (waits, fn, sem, inc) in self.ops[eng]:
            for (k, n) in waits:
                e.wait_ge(self._sem(k), n)
            if fn is None:
                continue
            ins = fn(e)
            if sem is not None:
                ins.then_inc(sem, inc)


def build_program(dbg=None, stop_after=None):
    nc = bass.Bass("TRN2", target_bir_lowering=False)
    dt = nc.dram_tensor
    x = dt("x", [S, D], F32, kind="ExternalInput").ap()
    meta = dt("meta", [NMETA, D], F32, kind="ExternalInput").ap()
    w_in = dt("w_in", [D, 9216], F32, kind="ExternalInput").ap()
    w_attn_o = dt("w_attn_o", [1024, D], F32, kind="ExternalInput").ap()
    w_conv_o = dt("w_conv_o", [1024, D], F32, kind="ExternalInput").ap()
    w_out = dt("w_out", [D, D], F32, kind="ExternalInput").ap()
    w_up = dt("w_up", [D, DFF], F32, kind="ExternalInput").ap()
    w_down = dt("w_down", [DFF, D], F32, kind="ExternalInput").ap()
    g1bc_d = dt("g1bc", [128, D], F32, kind="ExternalInput").ap()
    g2bc_d = dt("g2bc", [128, D], F32, kind="ExternalInput").ap()
    qkg_d = dt("qkg", [128, 256], F32, kind="ExternalInput").ap()
    sublng_d = dt("sublng", [128, 128], F32, kind="ExternalInput").ap()
    lamv_d = dt("lamv", [128, 256], F32, kind="ExternalInput").ap()
    cs_d = dt("cstab", [128, 17 * 32], F32, kind="ExternalInput").ap()
    dwk_d = dt("dwk", [128, 8 * KTAPS], F32, kind="ExternalInput").ap()
    cvec_d = dt("cvec", [128, 24], F32, kind="ExternalInput").ap()
    ident_d = dt("ident", [128, 128], BF16, kind="ExternalInput").ap()
    maskneg_d = dt("maskneg", [128, 128], BF16, kind="ExternalInput").ap()
    y = dt("y", [S, D], F32, kind="ExternalOutput").ap()
    h1s = dt("h1s", [S, D], F32, kind="ExternalOutput").ap()
    dbg_out = None
    if dbg is not None:
        dbg_out = dt("dbg", list(dbg[1]), dbg[2], kind="ExternalOutput").ap()

    import contextlib
    es = contextlib.ExitStack()
    with es:
        def sb(name, shape, dtype):
            return es.enter_context(nc.sbuf_tensor("sb_" + name, shape, dtype))

        def sem(name):
            return es.enter_context(nc.semaphore(name))

        uT = sb("uT", [128, 16, L], BF16)
        AR = sb("AR", [128, 32768], BF16)
        T0 = sb("T0", [128, 8192], BF16)
        WS = sb("WS", [128, 10240], BF16)
        WA = [sb("WA%d" % i, [128, 2064], F32) for i in range(2)]
        WBall = sb("WBall", [128, 4096], BF16)
        WB = [WBall[:, 0:2048], WBall[:, 2048:4096]]
        ident = sb("ident", [128, 128], BF16)
        maskneg = sb("maskneg", [128, 128], BF16)
        ones_bf = sb("ones_bf", [128, 128], BF16)
        gbc = sb("gbc", [128, D], F32)
        qkg = sb("qkg", [128, 256], F32)
        sublng = sb("sublng", [128, 128], F32)
        lamv = sb("lamv", [128, 256], F32)
        cstab = sb("cstab", [128, 17, 32], F32)
        dwk = sb("dwk", [128, 8, KTAPS], F32)
        cvec = sb("cvec", [128, 24], F32)
        small = sb("small", [128, 64], F32)
        mhalf = sb("mhalf", [128, 512], BF16)
        ssb = sb("ssb", [128, 48], F32)

        oT = AR[:, 0:16384].rearrange("p (c t) -> p c t", c=8)
        cT = AR[:, 16384:32768].rearrange("p (c t) -> p c t", c=8)
        ZT = AR[:, :].rearrange("p (f t) -> p f t", f=64)

        PS = es.enter_context(nc.psum_tensor("ps", [128, 4096], F32))
        PSB = PS[:].bitcast(BF16)

        def bank(i):
            return PS[:, 512 * i:512 * i + 512]

        def bank_bf(i):
            return PSB[:, 1024 * i:1024 * i + 1024]

        def PB(i):
            return ('ps', i)

        esems = {e: sem("s_" + e) for e in ['pe', 'act', 'dve', 'pool', 'sp']}
        dsems = {'sp': [sem("d_sp%d" % i) for i in range(12)], 'pool': [sem("d_pl%d" % i) for i in range(12)]}
        sc = Sched(nc, esems, dsems)
        dummy = sb("dummy", [128, 8], F32)

        def barrier(names):
            sc.op('pool', lambda e: e.memset(dummy[:, 0:2], 0.0), writes=list(names))
        w_in_v = w_in.rearrange("(c p) n -> p c n", p=128)
        w_ao_v = w_attn_o.rearrange("(c p) n -> p c n", p=128)
        w_co_v = w_conv_o.rearrange("(c p) n -> p c n", p=128)
        w_out_v = w_out.rearrange("(c p) n -> p c n", p=128)
        w_up_v = w_up.rearrange("(c p) n -> p c n", p=128)
        w_dn_v = w_down.rearrange("(c p) n -> p c n", p=128)

        WAr = [[('WA', i, k) for k in range(4)] for i in range(2)]
        WBr = [[('WB', i)] for i in range(2)]

        def ld(dst, src, res):
            sc.dma('sp', lambda e, d=dst, s=src: e.dma_start(out=d, in_=s), writes=[res])
        ld(ident[:], ident_d, 'ident')
        ld(maskneg[:], maskneg_d, 'maskneg')
        ld(gbc[:], g1bc_d, 'gbc')
        ld(qkg[:], qkg_d, 'qkg')
        ld(sublng[:], sublng_d, 'sublng')
        ld(lamv[:], lamv_d, 'lamv')
        ld(cstab[:].rearrange("p a b -> p (a b)"), cs_d, 'cstab')
        ld(dwk[:].rearrange("p a b -> p (a b)"), dwk_d, 'dwk')
        ld(cvec[:], cvec_d, 'cvec')
        sc.op('pool', lambda e: e.memset(mhalf[:], -0.5), writes=['mhalf'])
        sc.op('pool', lambda e: e.memset(ones_bf[:], 1.0), writes=['ones'])
        sc.op('dve', lambda e: e.tensor_scalar(out=dwk[:], in0=dwk[:], scalar1=0.5, scalar2=None, op0=ALU.mult),
              reads=['dwk'], writes=['dwk'])
        sc.op('dve', lambda e: e.tensor_scalar(out=cvec[:, 8:24], in0=cvec[:, 8:24], scalar1=0.5, scalar2=None,
                                               op0=ALU.mult), reads=['cvec'], writes=['cvec'])
        sc.op('dve', lambda e: e.tensor_scalar(out=sublng[:], in0=sublng[:], scalar1=1.0 - LAM_INIT, scalar2=None,
                                               op0=ALU.mult), reads=['sublng'], writes=['sublng'])
        sc.op('dve', lambda e: e.scalar_tensor_tensor(out=WB[0][:, 0:64], in0=lamv[:, 0:64], scalar=1.0, in1=lamv[:, 64:128],
                                                      op0=ALU.mult, op1=ALU.mult, accum_out=small[:, 0:1]),
              reads=['lamv'], writes=WBr[0] + ['sm0'])
        sc.op('dve', lambda e: e.scalar_tensor_tensor(out=WB[0][:, 0:64], in0=lamv[:, 128:192], scalar=1.0, in1=lamv[:, 192:256],
                                                      op0=ALU.mult, op1=ALU.mult, accum_out=small[:, 1:2]),
              reads=['lamv'], writes=WBr[0] + ['sm1'])
        sc.op('act', lambda e: e.activation(out=small[:, 2:4], in_=small[:, 0:2], func=AF.Exp),
              reads=['sm0', 'sm1'], writes=['sm23'])
        sc.op('dve', lambda e: e.tensor_tensor(out=small[:, 5:6], in0=small[:, 3:4], in1=small[:, 2:3],
                                               op=ALU.subtract), reads=['sm23'], writes=['sm5'])
        sc.op('dve', lambda e: e.tensor_scalar(out=small[:, 4:5], in0=small[:, 5:6], scalar1=-LAM_INIT,
                                               scalar2=None, op0=ALU.add), reads=['sm5'], writes=['neglam'])
        neglam = small[:, 4:5]

        def rstd_pool(io_ap, n, inv_n, res):
            rows = io_ap.shape[0]
            sc.op('pool', lambda e: e.tensor_scalar(out=io_ap, in0=io_ap, scalar1=inv_n, scalar2=EPS,
                                                    op0=ALU.mult, op1=ALU.add), reads=[res], writes=[res])
            sc.op('pool', lambda e: e.tensor_tensor(out=io_ap, in0=io_ap, in1=mhalf[:rows, 0:n], op=ALU.pow),
                  reads=[res, 'mhalf'], writes=[res])

        def norm_to_featmajor(xs, xs_res, rows, slot, colofs, dst_res, ssidx, have_ss=False):
            ssr = ('ss', ssidx)
            ss_ap = ssb[:rows, ssidx:ssidx + 1]
            if not have_ss:
                sc.op('act', lambda e: e.activation(out=WB[slot][:rows, :], in_=xs, func=AF.Square, accum_out=ss_ap),
                      reads=xs_res, writes=WBr[slot] + [ssr])
            rstd_pool(ss_ap, 1, 1.0 / D, ssr)
            xn = WB[slot]
            sc.op('dve', lambda e: e.scalar_tensor_tensor(out=xn[:rows, :], in0=xs, scalar=ss_ap,
                                                          in1=gbc[:rows, :], op0=ALU.mult, op1=ALU.mult),
                  reads=xs_res + [ssr, 'gbc'], writes=WBr[slot])
            pi = 3 - slot
            pt = PSB[:, 2048 * pi:2048 * pi + 2048]
            pres = [PB(2 * pi), PB(2 * pi + 1)]
            for c in range(16):
                sc.op('pe', lambda e, c=c: e.transpose(out=pt[:, c * 128:c * 128 + rows],
                                                       in_=xn[:rows, c * 128:(c + 1) * 128],
                                                       identity=ident[:rows, :rows]),
                      reads=WBr[slot] + ['ident'], writes=pres, signal=(c == 15))
            sc.op('act', lambda e: e.activation(out=uT[:, :, colofs:colofs + rows],
                                                in_=pt.rearrange("p (c t) -> p c t", c=16)[:, :, 0:rows],
                                                func=AF.Copy),
                  reads=pres, writes=[dst_res])

        def ublk(b):
            return [('uT', 4 * b + k) for k in range(4)]

        for j in range(NT + 1):
            slot = j % 2
            rows = 128 if j < NT else NMETA
            src = x[j * 128:(j + 1) * 128, :] if j < NT else meta
            colofs = NMETA + j * 128 if j < NT else 0
            sc.dma('sp', lambda e, s=src, sl=slot, r=rows: e.dma_start(out=WA[sl][:r, 0:D], in_=s),
                   writes=WAr[slot])
            norm_to_featmajor(WA[slot][:rows, 0:D], WAr[slot], rows, slot, colofs, ('uT', j), j)

        if dbg is not None and dbg[0] == 'uT':
            sc.dma('sp', lambda e: e.dma_start(out=dbg_out, in_=uT[:].rearrange("p a b -> p (a b)")),
                   reads=[('uT', j) for j in range(NT + 1)], is_out=True)
        if stop_after is None or stop_after > 1:
            sc.dma('sp', lambda e: e.dma_start(out=gbc[:], in_=g2bc_d), writes=['gbc'])

        TB = [(0, NMETA)] + [(NMETA + 512 * b, 512) for b in range(4)]
        TBres = [[('uT', 16)]] + [ublk(b) for b in range(4)]

        if stop_after is None or stop_after >= 2:
            aTp = [AR[:, 0:2096], AR[:, 2096:4192]]
            aTr = ['aT0', 'aT1']
            dg = WBall[:, 0:31 * 128].rearrange("p (j q) -> p j q", j=KTAPS)
            for k in range(2):
                sc.op('pool', lambda e, k=k: e.memset(aTp[k][:, 0:30], 0.0), writes=[aTr[k]])

            def emit_glu(cc):
                slot = cc % 2
                wg = T0[:, slot * 4096:(slot + 1) * 4096].rearrange("p (c w) -> p c w", c=16)
                wres = ('T0', slot)
                sc.dma('pool', lambda e: e.dma_start(out=wg[:, :, 0:128],
                                                     in_=w_in_v[:, :, 3072 + cc * 128:3072 + (cc + 1) * 128]), writes=[wres])
                sc.dma('pool', lambda e: e.dma_start(out=wg[:, :, 128:256],
                                                     in_=w_in_v[:, :, 4096 + cc * 128:4096 + (cc + 1) * 128]), writes=[wres])
                for bi, (c0, n) in enumerate(TB):
                    ba, bb = 2 * (bi % 2), 2 * (bi % 2) + 1
                    for (bk, wo) in ((ba, 0), (bb, 128)):
                        for c in range(16):
                            sc.op('pe', lambda e, c=c, bk=bk, wo=wo, c0=c0, n=n: e.matmul(
                                bank(bk)[:, 0:n], wg[:, c, wo:wo + 128], uT[:, c, c0:c0 + n],
                                start=(c == 0), stop=(c == 15)),
                                reads=[wres] + TBres[bi], writes=[PB(bk)], signal=(c == 15))
                    th = WA[0][:, 512 * (bi % 2):512 * (bi % 2) + 512]
                    thr = [WAr[0][bi % 2]]
                    sc.op('act', lambda e, th=th, bb=bb, n=n: e.activation(out=th[:, 0:n], in_=bank(bb)[:, 0:n],
                                                                        func=AF.Tanh, scale=0.5),
                          reads=[PB(bb)], writes=thr)
                    sc.op('dve', lambda e, th=th, ba=ba, n=n, c0=c0: e.scalar_tensor_tensor(
                        out=aTp[slot][:, 30 + c0:30 + c0 + n], in0=th[:, 0:n], scalar=1.0, in1=bank(ba)[:, 0:n],
                        op0=ALU.add, op1=ALU.mult),
                        reads=thr + [PB(ba)], writes=[aTr[slot]])

            def emit_conv(cc):
                slot = cc % 2
                sc.op('dve', lambda e: e.tensor_tensor(
                    out=dg, in0=ident[:, :].unsqueeze(1).to_broadcast([128, KTAPS, 128]),
                    in1=dwk[:, cc, :].unsqueeze(2).to_broadcast([128, KTAPS, 128]), op=ALU.mult),
                    reads=['ident', 'dwk'], writes=WBr[0] + WBr[1])
                for b in range(4):
                    bk = 4 + b % 2
                    for j in range(KTAPS):
                        c0 = NMETA + 512 * b + j
                        sc.op('pe', lambda e, j=j, c0=c0, bk=bk: e.matmul(
                            bank(bk), dg[:, j, :], aTp[slot][:, c0:c0 + 512], start=(j == 0), stop=(j == KTAPS - 1)),
                            reads=WBr[0] + WBr[1] + [aTr[slot]], writes=[PB(bk)], signal=(j == KTAPS - 1))
                    sc.op('dve', lambda e, b=b, bk=bk: e.tensor_scalar(
                        out=cT[:, cc, 512 * b:512 * b + 512], in0=bank(bk), scalar1=cvec[:, cc:cc + 1], scalar2=None,
                        op0=ALU.add), reads=[PB(bk), 'cvec'], writes=[('cT', cc)])

            for cc in range(9):
                if cc < 8:
                    emit_glu(cc)
                if cc >= 1:
                    emit_conv(cc - 1)
            sqb = T0[:, 0:4096].rearrange("p (c t) -> p c t", c=8)
            for b in range(4):
                blk = slice(512 * b, 512 * b + 512)
                sc.op('dve', lambda e, blk=blk: e.tensor_tensor(out=sqb, in0=cT[:, :, blk], in1=cT[:, :, blk],
                                                             op=ALU.mult),
                      reads=[('cT', c) for c in range(8)], writes=[('T0', 0)])
                for (bk, srcs) in ((6, 'c'), (7, 's')):
                    for c in range(8):
                        rhs = cT[:, c, blk] if srcs == 'c' else sqb[:, c, :]
                        sc.op('pe', lambda e, bk=bk, rhs=rhs, c=c: e.matmul(bank(bk), ones_bf[:, :], rhs,
                                                                         start=(c == 0), stop=(c == 7)),
                              reads=['ones', ('cT', c), ('T0', 0)], writes=[PB(bk)], signal=(c == 7))
                mean = WA[0][:, 0:512]
                msq = WA[0][:, 512:1024]
                rstd = WA[0][:, 1024:1536]
                sc.op('dve', lambda e: e.tensor_scalar(out=mean, in0=bank(6), scalar1=1.0 / CW, scalar2=None,
                                                       op0=ALU.mult), reads=[PB(6)], writes=[WAr[0][0]])
                sc.op('dve', lambda e: e.tensor_tensor(out=msq, in0=mean, in1=mean, op=ALU.mult),
                      reads=[WAr[0][0]], writes=[WAr[0][1]])
                sc.op('dve', lambda e: e.scalar_tensor_tensor(out=rstd, in0=bank(7), scalar=1.0 / CW, in1=msq,
                                                              op0=ALU.mult, op1=ALU.subtract),
                      reads=[PB(7), WAr[0][1]], writes=[WAr[0][2]])
                sc.op('dve', lambda e: e.tensor_scalar(out=rstd, in0=rstd, scalar1=EPS, scalar2=None, op0=ALU.add),
                      reads=[WAr[0][2]], writes=[WAr[0][2]])
                sc.op('act', lambda e: e.activation(out=rstd, in_=rstd, func=AF.Sqrt),
                      reads=[WAr[0][2]], writes=[WAr[0][2]])
                sc.op('dve', lambda e: e.reciprocal(out=rstd, in_=rstd),
                      reads=[WAr[0][2]], writes=[WAr[0][2]])
                for cc in range(8):
                    k = cc % 2
                    t = WA[1][:, 1024 * k:1024 * k + 512]
                    th = WA[1][:, 1024 * k + 512:1024 * k + 1024]
                    tr = [WAr[1][2 * k]]
                    thr = [WAr[1][2 * k + 1]]
                    sc.op('dve', lambda e, t=t, cc=cc, blk=blk: e.tensor_tensor(out=t, in0=cT[:, cc, blk], in1=mean,
                                                                             op=ALU.subtract),
                          reads=[('cT', cc), WAr[0][0]], writes=tr)
                    sc.op('dve', lambda e, t=t: e.tensor_tensor(out=t, in0=t, in1=rstd, op=ALU.mult),
                          reads=tr + [WAr[0][2]], writes=tr)
                    sc.op('dve', lambda e, t=t, cc=cc: e.tensor_scalar(out=t, in0=t, scalar1=cvec[:, 8 + cc:9 + cc],
                                                                    scalar2=cvec[:, 16 + cc:17 + cc],
                                                                    op0=ALU.mult, op1=ALU.add),
                          reads=tr + ['cvec'], writes=tr)
                    sc.op('act', lambda e, t=t, th=th: e.activation(out=th, in_=t, func=AF.Tanh),
                          reads=tr, writes=thr)
                    sc.op('dve', lambda e, t=t, th=th, cc=cc, blk=blk: e.scalar_tensor_tensor(
                        out=cT[:, cc, blk], in0=th, scalar=1.0, in1=t, op0=ALU.add, op1=ALU.mult),
                        reads=tr + thr, writes=[('cT', cc)])
            if dbg is not None and dbg[0] == 'cT':
                sc.dma('sp', lambda e: e.dma_start(out=dbg_out, in_=AR[:, 16384:32768]),
                       reads=[('cT', c) for c in range(8)], is_out=True)

        if stop_after is None or stop_after >= 3:
            qTs = [T0[:, 0:2048], WS[:, 8192:10240]]
            kT = T0[:, 2048:2048 + L]
            VA0 = 2048 + L
            vaug = T0[:, VA0:VA0 + 17 * 130].rearrange("p (j v) -> p j v", j=17)
            PT0 = VA0 + 17 * 130
            PTs = [T0[:, PT0 + 512 * i:PT0 + 512 * (i + 1)] for i in range(3)]
            assert PT0 + 1536 <= 8192
            sc.op('pool', lambda e: e.memset(T0[:, VA0:VA0 + 17 * 130], 1.0),
                  writes=['vaug', 'qT', 'kT', ('T0', 0), ('T0', 1), 'aT0', 'aT1', ('WS', 2)] + [('PT', i) for i in range(3)]
                  + [('oT', c) for c in range(8)])
            sc.op('pool', lambda e: e.memset(qTs[0][64:128, :], 0.0), writes=['qT'])
            sc.op('pool', lambda e: e.memset(qTs[1][0:64, :], 0.0), writes=['qT'])
            def head_proj(h, slot, wqk, wv, wqk_r):
                def geo(j):
                    rows = 128 if j < NT else NMETA
                    colofs = NMETA + j * 128 if j < NT else 0
                    pj = j % 2
                    pbk = j % 3
                    return rows, colofs, pj, pbk

                def stA(j):
                    rows, colofs, pj, pbk = geo(j)
                    pr = bank(pbk)
                    for c in range(16):
                        sc.op('pe', lambda e, c=c: e.matmul(
                            pr[:rows, 0:256], uT[:, c, colofs:colofs + rows], wqk[:, c, :], start=(c == 0), stop=(c == 15)),
                            reads=[('uT', j), wqk_r], writes=[PB(pbk)], signal=(c == 15))
                    for c in range(16):
                        sc.op('pe', lambda e, c=c: e.matmul(
                            pr[:rows, 256:384], uT[:, c, colofs:colofs + rows], wv[:, c, :], start=(c == 0), stop=(c == 15)),
                            reads=[('uT', j)] + WBr[slot], writes=[PB(pbk)], signal=(c == 15))
                    sq = WA[pj][:rows, 1024:1280]
                    s4 = ssb[:rows, 24 + 4 * pj:28 + 4 * pj]
                    s4r = ('s4', pj)
                    sqr = [WAr[pj][2]]
                    sc.op('act', lambda e: e.activation(out=sq, in_=pr[:rows, 0:256], func=AF.Square),
                          reads=[PB(pbk)], writes=sqr)
                    sc.op('dve', lambda e: e.tensor_reduce(out=s4, in_=sq.rearrange("p (g d) -> p g d", g=4),
                                                          axis=mybir.AxisListType.X, op=ALU.add),
                          reads=sqr, writes=[s4r])
                    rstd_pool(s4, 4, 1.0 / 64, s4r)

                def stB(j):
                    rows, colofs, pj, pbk = geo(j)
                    pr = bank(pbk)
                    wa = WA[pj]
                    qn = wa[:rows, 256:512]
                    ra = wa[:rows, 512:576].rearrange("p (g d) -> p g d", g=4)
                    rb = wa[:rows, 576:640].rearrange("p (g d) -> p g d", g=4)
                    qkb = wa[:, 640:768].bitcast(BF16)
                    s4 = ssb[:rows, 24 + 4 * pj:28 + 4 * pj]
                    s4r = ('s4', pj)
                    qnr = [WAr[pj][0]]
                    rr_ = [WAr[pj][1]]
                    sc.op('dve', lambda e: e.tensor_tensor(
                        out=qn.rearrange("p (g d) -> p g d", g=4),
                        in0=pr[:rows, 0:256].rearrange("p (g d) -> p g d", g=4),
                        in1=s4.unsqueeze(2).to_broadcast([rows, 4, 64]), op=ALU.mult),
                        reads=[PB(pbk), s4r], writes=qnr)
                    sc.op('dve', lambda e: e.tensor_tensor(out=qn, in0=qn, in1=qkg[:rows, :], op=ALU.mult),
                          reads=qnr + ['qkg'], writes=qnr)
                    q3 = qn.rearrange("p (g d) -> p g d", g=4)
                    cs = cstab[:rows, j, :]
                    sc.op('dve', lambda e: e.tensor_tensor(
                        out=ra, in0=q3[:, :, 0:16], in1=cs[:, 0:16].unsqueeze(1).to_broadcast([rows, 4, 16]), op=ALU.mult),
                        reads=qnr + ['cstab'], writes=rr_)
                    sc.op('dve', lambda e: e.tensor_tensor(
                        out=rb[:, :, 0:8], in0=q3[:, :, 8:16], in1=cs[:, 16:24].unsqueeze(1).to_broadcast([rows, 4, 8]),
                        op=ALU.mult), reads=qnr + ['cstab'], writes=rr_)
                    sc.op('dve', lambda e: e.tensor_tensor(
                        out=rb[:, :, 8:16], in0=q3[:, :, 0:8], in1=cs[:, 24:32].unsqueeze(1).to_broadcast([rows, 4, 8]),
                        op=ALU.mult), reads=qnr + ['cstab'], writes=rr_)
                    sc.op('dve', lambda e: e.tensor_tensor(out=q3[:, :, 0:16], in0=ra, in1=rb, op=ALU.add),
                          reads=qnr + rr_, writes=qnr)
                    sc.op('act', lambda e: e.activation(out=qkb[:rows, :], in_=qn, func=AF.Copy),
                          reads=qnr, writes=rr_)
                    sc.op('act', lambda e: e.activation(out=vaug[:rows, j, 0:128], in_=pr[:rows, 256:384], func=AF.Copy),
                          reads=[PB(pbk)], writes=['vaug'])

                def stC(j):
                    rows, colofs, pj, pbk = geo(j)
                    qkb = WA[pj][:, 640:768].bitcast(BF16)
                    rr_ = [WAr[pj][1]]
                    tb = 3 if pj == 0 else 7
                    ptb = bank_bf(tb)
                    sc.op('pe', lambda e: e.transpose(out=ptb[:, 0:rows], in_=qkb[:rows, 0:128], identity=ident[:rows, :rows]),
                          reads=rr_ + ['ident'], writes=[PB(tb)], signal=False)
                    sc.op('pe', lambda e: e.transpose(out=ptb[:, 128:128 + rows], in_=qkb[:rows, 128:256],
                                                      identity=ident[:rows, :rows]),
                          reads=rr_ + ['ident'], writes=[PB(tb)], signal=True)
                    if j < NT:
                        for c in range(2):
                            sc.op('act', lambda e, c=c: e.activation(
                                out=qTs[c][64 * c:64 * c + 64, 128 * j:128 * j + 128], in_=ptb[64 * c:64 * c + 64, 0:128],
                                func=AF.Copy), reads=[PB(tb)], writes=['qT'])
                    sc.op('dve', lambda e: e.tensor_copy(out=kT[:, colofs:colofs + rows], in_=ptb[:, 128:128 + rows]),
                          reads=[PB(tb)], writes=['kT'])

                for s in range(NT + 3):
                    if s <= NT:
                        stA(s)
                    if 0 <= s - 1 <= NT:
                        stB(s - 1)
                    if 0 <= s - 2 <= NT:
                        stC(s - 2)

            def attn_sweep(h):
                steps = []
                for g2 in range(NT // 2 if DBG_GROUPS is None else DBG_GROUPS):
                    j1 = 2 * g2 + 1
                    kts = ['m'] + list(range(j1 + 1))
                    for kt in kts:
                        steps.append((g2, kt, kt == kts[-1]))

                def geom(i):
                    g2, kt, last = steps[i]
                    j0, j1 = 2 * g2, 2 * g2 + 1
                    krows = NMETA if kt == 'm' else 128
                    kc0 = 0 if kt == 'm' else NMETA + 128 * kt
                    kidx = 16 if kt == 'm' else kt
                    both = (kt == 'm') or (kt <= j0)
                    qc0 = 128 * j0 if both else 128 * j1
                    N = 256 if both else 128
                    diag = (kt != 'm') and (kt == j0 or kt == j1)
                    return g2, kt, last, j0, j1, krows, kc0, kidx, both, qc0, N, diag

                def emit_S(i):
                    g2, kt, last, j0, j1, krows, kc0, kidx, both, qc0, N, diag = geom(i)
                    sbk = 4 + i % 3
                    sbank = bank(sbk)
                    for c in range(2):
                        sc.op('pe', lambda e, c=c: e.matmul(
                            sbank[:krows, c * 256:c * 256 + N], kT[:, kc0:kc0 + krows],
                            qTs[c][:, qc0:qc0 + N], start=True, stop=(not diag)),
                            reads=['kT', 'qT'], writes=[PB(sbk)], signal=(c == 1 and not diag))
                        if diag:
                            sc.op('pe', lambda e, c=c: e.matmul(
                                sbank[:, c * 256:c * 256 + 128], ident[:, :], maskneg[:, :], start=False, stop=True),
                                reads=['ident', 'maskneg'], writes=[PB(sbk)], signal=(c == 1))

                def emit_EP(i):
                    g2, kt, last, j0, j1, krows, kc0, kidx, both, qc0, N, diag = geom(i)
                    sbk = 4 + i % 3
                    sbank = bank(sbk)
                    PT = PTs[i % 3]
                    ptr = ('PT', i % 3)
                    sc.op('act', lambda e: e.activation(
                        out=PT[:krows, :].rearrange("p (c n) -> p c n", c=2)[:, :, 0:N],
                        in_=sbank[:krows, :].rearrange("p (c n) -> p c n", c=2)[:, :, 0:N],
                        func=AF.Exp, scale=0.125),
                        reads=[PB(sbk)], writes=[ptr])
                    for c in range(2):
                        qts = (j0, j1) if both else (j1,)
                        for qt in qts:
                            ab = 2 * (qt - j0) + c
                            po = c * 256 + ((qt - j0) * 128 if both else 0)
                            sc.op('pe', lambda e, ab=ab, po=po, qt=qt: e.matmul(
                                bank(ab)[:, 0:129], PT[:krows, po:po + 128], vaug[:krows, kidx, 0:129],
                                start=(kt == 'm'), stop=(kt == qt)),
                                reads=[ptr, 'vaug'], writes=[PB(ab)], signal=(kt == qt or (c == 1 and qt == qts[-1])))
                    if last:
                        for qt in (j0, j1):
                            post_one(qt, j0)

                def post_one(qt, j0):
                    a0, a1 = 2 * (qt - j0), 2 * (qt - j0) + 1
                    k = qt % 2
                    sm = small[:, 16 + 8 * k:24 + 8 * k]
                    smr = ('smq', k)
                    o1 = WA[k][:, 1536:1664]
                    o2 = WA[k][:, 1664:1792]
                    onb = WA[k][:, 1792:1856].bitcast(BF16)
                    wr = [WAr[k][3]]
                    sc.op('dve', lambda e: e.reciprocal(out=sm[:, 0:1], in_=bank(a0)[:, 128:129]),
                          reads=[PB(a0)], writes=[smr])
                    sc.op('dve', lambda e: e.reciprocal(out=sm[:, 1:2], in_=bank(a1)[:, 128:129]),
                          reads=[PB(a1), smr], writes=[smr])
                    sc.op('dve', lambda e: e.tensor_tensor(out=sm[:, 2:3], in0=sm[:, 1:2], in1=neglam, op=ALU.mult),
                          reads=[smr, 'neglam'], writes=[smr])
                    sc.op('dve', lambda e: e.tensor_scalar(out=o1, in0=bank(a0)[:, 0:128],
                                                           scalar1=sm[:, 0:1], scalar2=None, op0=ALU.mult),
                          reads=[PB(a0), smr], writes=wr)
                    sc.op('dve', lambda e: e.scalar_tensor_tensor(
                        out=o1, in0=bank(a1)[:, 0:128], scalar=sm[:, 2:3], in1=o1, op0=ALU.mult, op1=ALU.add),
                        reads=[PB(a1), smr] + wr, writes=wr)
                    sc.op('dve', lambda e: e.scalar_tensor_tensor(
                        out=o2, in0=o1, scalar=1.0, in1=o1, op0=ALU.mult, op1=ALU.mult, accum_out=sm[:, 3:4]),
                        reads=wr, writes=wr + [('smr', k)])
                    rstd_pool(sm[:, 3:4], 1, 1.0 / 128, ('smr', k))
                    sc.op('dve', lambda e: e.scalar_tensor_tensor(
                        out=onb, in0=o1, scalar=sm[:, 3:4], in1=sublng[:, :], op0=ALU.mult, op1=ALU.mult),
                        reads=wr + [('smr', k), 'sublng'], writes=wr)
                    ptb = bank_bf(7)
                    sc.op('pe', lambda e: e.transpose(out=ptb[:, 0:128], in_=onb, identity=ident[:, :]),
                          reads=wr + ['ident'], writes=[PB(7)])
                    sc.op('act', lambda e: e.activation(out=oT[:, h, 128 * qt:128 * qt + 128],
                                                        in_=ptb[:, 0:128], func=AF.Copy),
                          reads=[PB(7)], writes=[('oT', h)])

                LOOK = DBG_LOOK
                nst = len(steps)
                for i in range(min(LOOK, nst)):
                    emit_S(i)
                for i in range(nst):
                    if i + LOOK < nst:
                        emit_S(i + LOOK)
                    emit_EP(i)

            for h in range(H if DBG_HEADS is None else DBG_HEADS):
                slot = h % 2
                wqk = WS[:, slot * 4096:(slot + 1) * 4096].rearrange("p (c w) -> p c w", c=16)
                wv = WB[slot][:, :].rearrange("p (c w) -> p c w", c=16)
                wqk_r = ('WS', slot)
                sc.dma('pool', lambda e, wqk=wqk, h=h: e.dma_start(out=wqk[:, :, 0:128],
                                                                 in_=w_in_v[:, :, h * 128:(h + 1) * 128]), writes=[wqk_r])
                sc.dma('pool', lambda e, wqk=wqk, h=h: e.dma_start(out=wqk[:, :, 128:256],
                                                                 in_=w_in_v[:, :, 1024 + h * 128:1024 + (h + 1) * 128]),
                       writes=[wqk_r])
                sc.dma('pool', lambda e, wv=wv, h=h: e.dma_start(out=wv, in_=w_in_v[:, :, 2048 + h * 128:2048 + (h + 1) * 128]),
                       writes=WBr[slot])
                head_proj(h, slot, wqk, wv, wqk_r)
                attn_sweep(h)
            if dbg is not None and dbg[0] == 'oT':
                sc.dma('sp', lambda e: e.dma_start(out=dbg_out, in_=AR[:, 0:16384]),
                       reads=[('oT', c) for c in range(8)], is_out=True)

        if stop_after is None or stop_after >= 4:
            mT = T0[:, :].rearrange("p (c t) -> p c t", c=16)
            barrier(['vaug', 'qT', 'kT', ('WS', 2)] + [('PT', i) for i in range(3)] + [('mT', fc) for fc in range(16)])
            it = 0
            for b in range(4):
                ucol = NMETA + 512 * b
                blk = slice(512 * b, 512 * b + 512)
                for fc in range(16):
                    slot = it % 2
                    wg = WS[:, slot * 4096:(slot + 1) * 4096].rearrange("p (c w) -> p c w", c=16)
                    wo = WS[:, 8192:10240].rearrange("p (c w) -> p c w", c=16)
                    wgr = ('WS', slot)
                    wor = ('WS', 2)
                    sc.dma('pool', lambda e, wg=wg, fc=fc: e.dma_start(
                        out=wg[:, :, 0:128], in_=w_in_v[:, :, 5120 + fc * 128:5120 + (fc + 1) * 128]), writes=[wgr])
                    sc.dma('pool', lambda e, wg=wg, fc=fc: e.dma_start(
                        out=wg[:, :, 128:256], in_=w_in_v[:, :, 7168 + fc * 128:7168 + (fc + 1) * 128]), writes=[wgr])
                    sc.dma('pool', lambda e, wo=wo, fc=fc: e.dma_start(
                        out=wo[:, 0:8, :], in_=w_ao_v[:, :, fc * 128:(fc + 1) * 128]), writes=[wor])
                    sc.dma('pool', lambda e, wo=wo, fc=fc: e.dma_start(
                        out=wo[:, 8:16, :], in_=w_co_v[:, :, fc * 128:(fc + 1) * 128]), writes=[wor])
                    bs = 4 * (it % 2)
                    for c in range(16):
                        sc.op('pe', lambda e, c=c, bs=bs, wg=wg, ucol=ucol: e.matmul(
                            bank(bs), wg[:, c, 0:128], uT[:, c, ucol:ucol + 512], start=(c == 0), stop=(c == 15)),
                            reads=[wgr] + ublk(b), writes=[PB(bs)], signal=(c == 15))
                    for c in range(16):
                        sc.op('pe', lambda e, c=c, bs=bs, wg=wg, ucol=ucol: e.matmul(
                            bank(bs + 1), wg[:, c, 128:256], uT[:, c, ucol:ucol + 512], start=(c == 0), stop=(c == 15)),
                            reads=[wgr] + ublk(b), writes=[PB(bs + 1)], signal=(c == 15))
                    for c in range(8):
                        sc.op('pe', lambda e, c=c, bs=bs, wo=wo, blk=blk: e.matmul(
                            bank(bs + 2), wo[:, c, :], oT[:, c, blk], start=(c == 0), stop=(c == 7)),
                            reads=[wor] + [('oT', c)], writes=[PB(bs + 2)], signal=(c == 7))
                    for c in range(8):
                        sc.op('pe', lambda e, c=c, bs=bs, wo=wo, blk=blk: e.matmul(
                            bank(bs + 3), wo[:, 8 + c, :], cT[:, c, blk], start=(c == 0), stop=(c == 7)),
                            reads=[wor] + [('cT', c)], writes=[PB(bs + 3)], signal=(c == 7))
                    k = it % 2
                    ta = WA[1][:, 1024 * k:1024 * k + 512]
                    tcn = WA[1][:, 1024 * k + 512:1024 * k + 1024]
                    tar = [WAr[1][2 * k]]
                    tcr = [WAr[1][2 * k + 1]]
                    sc.op('act', lambda e, ta=ta, bs=bs: e.activation(out=ta, in_=bank(bs), func=AF.Tanh, scale=0.5),
                          reads=[PB(bs)], writes=tar)
                    sc.op('act', lambda e, tcn=tcn, bs=bs: e.activation(out=tcn, in_=bank(bs + 1), func=AF.Tanh, scale=0.5),
                          reads=[PB(bs + 1)], writes=tcr)
                    sc.op('dve', lambda e, ta=ta, bs=bs: e.scalar_tensor_tensor(
                        out=ta, in0=ta, scalar=1.0, in1=bank(bs + 2), op0=ALU.add, op1=ALU.mult),
                        reads=tar + [PB(bs + 2)], writes=tar)
                    sc.op('dve', lambda e, tcn=tcn, bs=bs: e.scalar_tensor_tensor(
                        out=tcn, in0=tcn, scalar=1.0, in1=bank(bs + 3), op0=ALU.add, op1=ALU.mult),
                        reads=tcr + [PB(bs + 3)], writes=tcr)
                    sc.op('dve', lambda e, ta=ta, tcn=tcn, fc=fc: e.tensor_tensor(out=mT[:, fc, :], in0=ta, in1=tcn, op=ALU.add),
                          reads=tar + tcr, writes=[('mT', fc)])
                    it += 1
                wslots = [WS[:, 4096 + 1024 * i:4096 + 1024 * (i + 1)] for i in range(4)]
                wsr = [('WSq', i) for i in range(4)]
                barrier([('WS', 1)] + wsr)
                di = 0
                for cg in range(4):
                    bs = 4 * (cg % 2)
                    xps = []
                    for t in range(4):
                        tile_i = 4 * b + t
                        xp = WA[cg % 2][:, 512 * t:512 * t + 512]
                        xpr = [WAr[cg % 2][t]]
                        rs, re = tile_i * 128, tile_i * 128 + 128
                        sc.dma('sp', lambda e, xp=xp, rs=rs, re=re, cg=cg: e.dma_start(
                            out=xp, in_=x[rs:re, cg * 512:(cg + 1) * 512]), writes=xpr)
                        xps.append((xp, xpr, rs, re, tile_i))
                    for fp in range(8):
                        sl = di % 4
                        wsl = wslots[sl].rearrange("p (k n) -> p k n", k=2)
                        wres = [wsr[sl]]
                        sc.dma('pool', lambda e, wsl=wsl, fp=fp, cg=cg: e.dma_start(
                            out=wsl, in_=w_out_v[:, 2 * fp:2 * fp + 2, cg * 512:(cg + 1) * 512]), writes=wres)
                        for k in range(2):
                            fc = 2 * fp + k
                            for t in range(4):
                                sc.op('pe', lambda e, t=t, bs=bs, fc=fc, wsl=wsl, k=k: e.matmul(
                                    bank(bs + t), mT[:, fc, t * 128:(t + 1) * 128], wsl[:, k, :],
                                    start=(fc == 0), stop=(fc == 15)),
                                    reads=[('mT', fc)] + wres, writes=[PB(bs + t)], signal=(fc == 15 or (k == 1 and t == 3)))
                        di += 1
                    for t in range(4):
                        xp, xpr, rs, re, tile_i = xps[t]
                        sc.op('dve', lambda e, xp=xp, bs=bs, t=t: e.scalar_tensor_tensor(
                            out=xp, in0=bank(bs + t), scalar=0.5, in1=xp, op0=ALU.mult, op1=ALU.add),
                            reads=[PB(bs + t)] + xpr, writes=xpr)
                        sc.op('act', lambda e, xp=xp, t=t, cg=cg: e.activation(
                            out=WB[0][:, 512 * (t % 2):512 * (t % 2) + 512], in_=xp, func=AF.Square,
                            accum_out=ssb[:, 4 * t + cg:4 * t + cg + 1]),
                            reads=xpr, writes=WBr[0] + [('ssq', t)])
                        sc.dma('sp', lambda e, xp=xp, rs=rs, re=re, cg=cg: e.dma_start(
                            out=h1s[rs:re, cg * 512:(cg + 1) * 512], in_=xp), reads=xpr, writes=[('h1s', tile_i)])
                barrier([('WS', 1)] + wsr)
                for t in range(4):
                    tile_i = 4 * b + t
                    rs, re = tile_i * 128, tile_i * 128 + 128
                    slot = t % 2
                    sc.op('dve', lambda e, t=t: e.tensor_reduce(out=ssb[:, 16 + t:17 + t], in_=ssb[:, 4 * t:4 * t + 4],
                                                              axis=mybir.AxisListType.X, op=ALU.add),
                          reads=[('ssq', t)], writes=[('ss', 16 + t)])
                    sc.dma('sp', lambda e, slot=slot, rs=rs, re=re: e.dma_start(out=WA[slot][:, 0:D], in_=h1s[rs:re, :]),
                           reads=[('h1s', tile_i)], writes=WAr[slot])
                    norm_to_featmajor(WA[slot][:, 0:D], WAr[slot], 128, 0, NMETA + 128 * tile_i, ('uT', tile_i), 16 + t,
                                      have_ss=True)
            if dbg is not None and dbg[0] == 'h1':
                sc.dma('sp', lambda e: e.dma_start(out=dbg_out, in_=h1s), reads=[('h1s', i) for i in range(16)], is_out=True)
            if dbg is not None and dbg[0] == 'u2T':
                sc.dma('sp', lambda e: e.dma_start(out=dbg_out, in_=uT[:].rearrange("p a b -> p (a b)")),
                       reads=[('uT', j) for j in range(NT + 1)], is_out=True)

        if stop_after is None or stop_after >= 5:
            ui = 0
            barrier([('mT', fc) for fc in range(16)] + [('T0d', i) for i in range(4)] + WBr[1] + [('WB1q', q) for q in range(4)]
                    + [('WS', 1)] + [('WSq', i) for i in range(4)])
            for b in range(4):
                ucol = NMETA + 512 * b
                for fp in range(32):
                    slot = ui % 3
                    wu = (WS[:, slot * 4096:(slot + 1) * 4096] if slot < 2 else gbc[:].bitcast(BF16)).rearrange(
                        "p (c w) -> p c w", c=16)
                    wur = ('WS', slot) if slot < 2 else 'gbc'
                    sc.dma('pool', lambda e, wu=wu, fp=fp: e.dma_start(out=wu, in_=w_up_v[:, :, fp * 256:(fp + 1) * 256]),
                           writes=[wur])
                    for k in range(2):
                        f = 2 * fp + k
                        ub = f % 4
                        for c in range(16):
                            sc.op('pe', lambda e, c=c, ub=ub, wu=wu, k=k, ucol=ucol: e.matmul(
                                bank(ub), wu[:, c, k * 128:(k + 1) * 128], uT[:, c, ucol:ucol + 512],
                                start=(c == 0), stop=(c == 15)),
                                reads=[wur] + ublk(b), writes=[PB(ub)], signal=(c == 15))
                        r = WB[1][:, 512 * (f % 4):512 * (f % 4) + 512]
                        rr = [('WB1q', f % 4)]
                        sc.op('act', lambda e, r=r, ub=ub: e.activation(out=r, in_=bank(ub), func=AF.Relu),
                              reads=[PB(ub)], writes=rr)
                        sc.op('dve', lambda e, r=r, f=f: e.tensor_tensor(out=ZT[:, f, :], in0=r, in1=r, op=ALU.mult),
                              reads=rr, writes=[('ZT', f)])
                    ui += 1
                di = 0
                for cg in range(4):
                    bs = 4 * (cg % 2)
                    hps = []
                    for t in range(4):
                        tile_i = 4 * b + t
                        rs, re = tile_i * 128, tile_i * 128 + 128
                        hp = WA[cg % 2][:, 512 * t:512 * t + 512]
                        hpr = [WAr[cg % 2][t]]
                        sc.dma('sp', lambda e, hp=hp, rs=rs, re=re, cg=cg: e.dma_start(
                            out=hp, in_=h1s[rs:re, cg * 512:(cg + 1) * 512]), reads=[('h1s', tile_i)], writes=hpr)
                        hps.append((hp, hpr, rs, re, tile_i))
                    for fq in range(16):
                        sl = di % 4
                        wd = T0[:, sl * 2048:(sl + 1) * 2048].rearrange("p (k n) -> p k n", k=4)
                        wdr = ('T0d', sl)
                        sc.dma('pool', lambda e, wd=wd, fq=fq, cg=cg: e.dma_start(
                            out=wd, in_=w_dn_v[:, 4 * fq:4 * fq + 4, cg * 512:(cg + 1) * 512]), writes=[wdr])
                        for k in range(4):
                            f = 4 * fq + k
                            for t in range(4):
                                sc.op('pe', lambda e, t=t, bs=bs, f=f, wd=wd, k=k: e.matmul(
                                    bank(bs + t), ZT[:, f, t * 128:(t + 1) * 128], wd[:, k, :],
                                    start=(f == 0), stop=(f == 63)),
                                    reads=[('ZT', f), wdr], writes=[PB(bs + t)], signal=(f == 63 or (k == 3 and t == 3)))
                        di += 1
                    for t in range(4):
                        hp, hpr, rs, re, tile_i = hps[t]
                        sc.op('dve', lambda e, hp=hp, bs=bs, t=t: e.tensor_tensor(out=hp, in0=bank(bs + t), in1=hp, op=ALU.add),
                              reads=[PB(bs + t)] + hpr, writes=hpr)
                        sc.dma('sp', lambda e, hp=hp, rs=rs, re=re, cg=cg: e.dma_start(
                            out=y[rs:re, cg * 512:(cg + 1) * 512], in_=hp), reads=hpr, writes=[('y', tile_i, cg)], is_out=True)

        sc.finish('sp')

        with nc.Block() as block:
            @block.sync
            def _(e):
                sc.replay('sp', e)

            @block.gpsimd
            def _(e):
                sc.replay('pool', e)

            @block.scalar
            def _(e):
                sc.replay('act', e)

            @block.vector
            def _(e):
                sc.replay('dve', e)

            @block.tensor
            def _(e):
                sc.replay('pe', e)
    return nc


def _bf16(a):
    return np.asarray(a, dtype=np.float32).astype(ml_dtypes.bfloat16)


def make_consts(inputs):
    f = np.float32
    c = {}
    c["g1bc"] = np.ascontiguousarray(np.broadcast_to(np.asarray(inputs["norm1_g"], f)[0][None, :], (128, D)))
    c["g2bc"] = np.ascontiguousarray(np.broadcast_to(np.asarray(inputs["norm2_g"], f)[0][None, :], (128, D)))
    qg = np.asarray(inputs["q_norm_g"], f)[0]
    kg = np.asarray(inputs["k_norm_g"], f)[0]
    c["qkg"] = np.ascontiguousarray(np.broadcast_to(np.concatenate([qg, qg, kg, kg])[None, :], (128, 256)))
    c["sublng"] = np.ascontiguousarray(np.broadcast_to(np.asarray(inputs["subln_g"], f)[0][None, :], (128, 128)))
    lv = np.concatenate([np.asarray(inputs[k], f)[0] for k in ("lambda_q1", "lambda_k1", "lambda_q2", "lambda_k2")])
    c["lamv"] = np.ascontiguousarray(np.broadcast_to(lv[None, :], (128, 256)))
    inv_freq = (np.float32(500000.0) ** (-np.arange(0, 16, 2, dtype=f) / np.float32(16))).astype(f)
    cs = np.zeros((128, 17, 32), f)
    for j in range(17):
        pos = (np.arange(128) + (NMETA + 128 * j if j < 16 else 0)).astype(f)
        ang = (pos[:, None] * inv_freq[None, :]).astype(f)
        co, si = np.cos(ang).astype(f), np.sin(ang).astype(f)
        cs[:, j, 0:8] = co
        cs[:, j, 8:16] = co
        cs[:, j, 16:24] = -si
        cs[:, j, 24:32] = si
    c["cstab"] = cs.reshape(128, 17 * 32)
    dk = np.asarray(inputs["dw_kernel"], f)[0]
    c["dwk"] = np.ascontiguousarray(dk.T.reshape(8, 128, KTAPS).transpose(1, 0, 2)).reshape(128, 8 * KTAPS)
    cv = np.zeros((128, 24), f)
    cv[:, 0:8] = np.asarray(inputs["dw_bias"], f)[0].reshape(8, 128).T
    cv[:, 8:16] = np.asarray(inputs["conv_ln_g"], f)[0].reshape(8, 128).T
    cv[:, 16:24] = np.asarray(inputs["conv_ln_b"], f)[0].reshape(8, 128).T
    c["cvec"] = cv
    c["ident"] = _bf16(np.eye(128))
    kk = np.arange(128)
    c["maskneg"] = _bf16(np.where(kk[:, None] > kk[None, :], NEG, 0.0))
    return c


_NC_CACHE = {}


def kernel(**inputs):
    f = np.float32
    consts = make_consts(inputs)
    shared = dict(consts)
    shared["meta"] = np.ascontiguousarray(np.asarray(inputs["meta"], f))
    shared["w_in"] = np.ascontiguousarray(np.asarray(inputs["w_in"], f)[0])
    shared["w_attn_o"] = np.ascontiguousarray(np.asarray(inputs["w_attn_o"], f)[0])
    shared["w_conv_o"] = np.ascontiguousarray(np.asarray(inputs["w_conv_o"], f)[0])
    shared["w_out"] = np.ascontiguousarray(np.asarray(inputs["w_out"], f)[0])
    shared["w_up"] = np.ascontiguousarray(np.asarray(inputs["w_up"], f)[0])
    shared["w_down"] = np.ascontiguousarray(np.asarray(inputs["w_down"], f)[0])
    xin = np.asarray(inputs["x"], f)
    if "nc" not in _NC_CACHE:
        _NC_CACHE["nc"] = build_program()
    nc = _NC_CACHE["nc"]
    in_maps = []
    for b in range(8):
        m = dict(shared)
        m["x"] = np.ascontiguousarray(xin[b])
        in_maps.append(m)
    res = run_bass_kernel_spmd(nc, in_maps, core_ids=list(range(8)))
    out = np.stack([np.asarray(r["y"], f) for r in res.results], axis=0)
    return out
```
